# Optimizing a Trainium2 kernel written in Bass

```python
import math
import jax, jax.numpy as jnp
from jax import lax
import numpy as np

D_MODEL = 2048
BATCH = 1
SEQ = 16384
DEPTH = 4
DEC_BATCH = 2
DEC_SEQ = 4096
PAST_LEN = 128

N_MIXERS = 2
N_ATTN_LAYERS = (DEPTH + 1) // 2
N_DN_LAYERS = DEPTH // 2
RMS_EPS = 1e-6
L2_EPS = 1e-6

ATTN_HEADS = 8
ATTN_HEAD_DIM = D_MODEL // ATTN_HEADS // 2
ATTN_V_DIM = 2 * ATTN_HEAD_DIM
ATTN_QK_W = ATTN_HEADS * 2 * ATTN_HEAD_DIM
ATTN_INNER = ATTN_HEADS * ATTN_V_DIM
ATTN_IN = 2 * ATTN_QK_W + 2 * ATTN_INNER
Q_BLOCK = 128

DN_K_HEADS = 16
DN_V_HEADS = 32
DN_DK = 128
DN_DV = 128
DN_KW = DN_K_HEADS * DN_DK
DN_VW = DN_V_HEADS * DN_DV
DN_CONV_DIM = 2 * DN_KW + DN_VW
DN_IN = DN_CONV_DIM + DN_VW + 4 * DN_V_HEADS
CONV_W = 5
CHUNK = 64

kernel_name = 'hybrid_diffattn_gdn_bidir_encoder'


def rms_norm(x, g):
    xf = x.astype(jnp.float32)
    y = xf * lax.rsqrt(jnp.mean(xf * xf, axis=-1, keepdims=True) + RMS_EPS)
    return (y * g.astype(jnp.float32)).astype(x.dtype)


def l2_norm(x):
    xf = x.astype(jnp.float32)
    return xf * lax.rsqrt(jnp.sum(xf * xf, axis=-1, keepdims=True) + L2_EPS)


def alibi_slopes(n_heads):
    return jnp.exp2(-8.0 * jnp.arange(1, n_heads + 1, dtype=jnp.float32) / n_heads)


def lambda_init_fn(layer_idx):
    return 0.8 - 0.6 * math.exp(-0.3 * layer_idx)


def diff_softmax_attention(q1, q2, k1, k2, v, lam):
    B, T, H, dh = q1.shape
    dv = v.shape[-1]
    nb = T // Q_BLOCK
    scale = dh ** -0.5
    slopes = alibi_slopes(H)
    pos_k = jnp.arange(T, dtype=jnp.float32)

    def to_blocks(a):
        return jnp.moveaxis(a.reshape(B, nb, Q_BLOCK, H, dh), 1, 0)

    starts = jnp.arange(nb, dtype=jnp.float32) * Q_BLOCK

    def one_block(args):
        qa, qb, start = args
        pos_q = start + jnp.arange(Q_BLOCK, dtype=jnp.float32)
        bias = -slopes[:, None, None] * jnp.abs(pos_q[:, None] - pos_k[None, :])[None]
        s1 = jnp.einsum('bqhd,bkhd->bhqk', qa, k1).astype(jnp.float32) * scale + bias
        s2 = jnp.einsum('bqhd,bkhd->bhqk', qb, k2).astype(jnp.float32) * scale + bias
        a = jax.nn.softmax(s1, axis=-1) - lam * jax.nn.softmax(s2, axis=-1)
        return jnp.einsum('bhqk,bkhe->bqhe', a.astype(v.dtype), v)

    o = lax.map(one_block, (to_blocks(q1), to_blocks(q2), starts))
    return jnp.moveaxis(o, 0, 1).reshape(B, T, H, dv)


def diff_attention_mixer(h, w_in, lam_vec, subln_g, w_out, lambda_init):
    B, T, _ = h.shape
    proj = h @ w_in
    q = proj[..., :ATTN_QK_W].reshape(B, T, ATTN_HEADS, 2, ATTN_HEAD_DIM)
    k = proj[..., ATTN_QK_W:2 * ATTN_QK_W].reshape(B, T, ATTN_HEADS, 2, ATTN_HEAD_DIM)
    v = proj[..., 2 * ATTN_QK_W:2 * ATTN_QK_W + ATTN_INNER].reshape(B, T, ATTN_HEADS, ATTN_V_DIM)
    gate = proj[..., 2 * ATTN_QK_W + ATTN_INNER:]
    lv = lam_vec.astype(jnp.float32)
    lam = jnp.exp(jnp.sum(lv[0] * lv[1])) - jnp.exp(jnp.sum(lv[2] * lv[3])) + lambda_init
    o = diff_softmax_attention(q[..., 0, :], q[..., 1, :], k[..., 0, :], k[..., 1, :], v, lam)
    o = rms_norm(o, subln_g) * (1.0 - lambda_init)
    o = o.reshape(B, T, ATTN_INNER) * jax.nn.silu(gate)
    return o @ w_out


def short_conv(x, w):
    C = x.shape[-1]
    return lax.conv_general_dilated(
        x, w.reshape(CONV_W, 1, C).astype(x.dtype), window_strides=(1,),
        padding=[(CONV_W // 2, CONV_W // 2)], dimension_numbers=('NWC', 'WIO', 'NWC'),
        feature_group_count=C)


def gated_delta_rule(q, k, v, g, beta):
    B, H, T, dk = q.shape
    dv = v.shape[-1]
    N = T // CHUNK
    q = q * dk ** -0.5

    def rs(a):
        return a.reshape(B, H, N, CHUNK, *a.shape[3:])

    q, k, v, g, beta = rs(q), rs(k), rs(v), rs(g), rs(beta)
    g = jnp.cumsum(g, axis=-1)
    idx = jnp.arange(CHUNK)
    tril = idx[:, None] >= idx[None, :]
    strict = idx[:, None] > idx[None, :]
    decay = jnp.exp(jnp.where(tril, g[..., :, None] - g[..., None, :], -jnp.inf))
    k_beta = k * beta[..., None]
    v_beta = v * beta[..., None]
    L = jnp.where(strict, jnp.einsum('bhncd,bhnsd->bhncs', k_beta, k) * decay, 0.0)
    Tm = jnp.eye(CHUNK, dtype=jnp.float32) + L
    u = lax.linalg.triangular_solve(Tm, v_beta, left_side=True, lower=True)
    w = lax.linalg.triangular_solve(Tm, k_beta * jnp.exp(g)[..., None], left_side=True, lower=True)
    a_intra = jnp.where(tril, jnp.einsum('bhncd,bhnsd->bhncs', q, k) * decay, 0.0)
    g_last = g[..., -1]
    q_dec = q * jnp.exp(g)[..., None]
    k_dec = k * jnp.exp(g_last[..., None] - g)[..., None]

    def step(S, inp):
        qd, kd, u_c, w_c, a_c, gl = inp
        v_new = u_c - jnp.einsum('bhcd,bhde->bhce', w_c, S)
        o = jnp.einsum('bhcd,bhde->bhce', qd, S) + jnp.einsum('bhcs,bhse->bhce', a_c, v_new)
        S = S * jnp.exp(gl)[..., None, None] + jnp.einsum('bhcd,bhce->bhde', kd, v_new)
        return S, o

    xs = tuple(jnp.moveaxis(a, 2, 0) for a in (q_dec, k_dec, u, w, a_intra, g_last))
    S0 = jnp.zeros((B, H, dk, dv), jnp.float32)
    _, o = lax.scan(step, S0, xs)
    return jnp.moveaxis(o, 0, 2).reshape(B, H, T, dv)


def gated_deltanet_mixer(h, w_in, conv_w, a_log_f, dt_bias_f, a_log_b, dt_bias_b, norm_g, w_out):
    B, T, _ = h.shape
    proj = h @ w_in
    qkv = jax.nn.silu(short_conv(proj[..., :DN_CONV_DIM], conv_w))
    z = proj[..., DN_CONV_DIM:DN_CONV_DIM + DN_VW].reshape(B, T, DN_V_HEADS, DN_DV)
    ab = proj[..., DN_CONV_DIM + DN_VW:].astype(jnp.float32).reshape(B, T, 4, DN_V_HEADS)
    rep = DN_V_HEADS // DN_K_HEADS
    q = jnp.repeat(l2_norm(qkv[..., :DN_KW].reshape(B, T, DN_K_HEADS, DN_DK)), rep, axis=2)
    k = jnp.repeat(l2_norm(qkv[..., DN_KW:2 * DN_KW].reshape(B, T, DN_K_HEADS, DN_DK)), rep, axis=2)
    v = qkv[..., 2 * DN_KW:].reshape(B, T, DN_V_HEADS, DN_DV).astype(jnp.float32)
    q, k, v = (jnp.transpose(a, (0, 2, 1, 3)) for a in (q, k, v))

    def gates(a, b, a_log, dt_bias):
        g = -jnp.exp(a_log.astype(jnp.float32)) * jax.nn.softplus(a + dt_bias.astype(jnp.float32))
        return jnp.transpose(g, (0, 2, 1)), jnp.transpose(jax.nn.sigmoid(b), (0, 2, 1))

    g_f, beta_f = gates(ab[:, :, 0], ab[:, :, 1], a_log_f, dt_bias_f)
    g_b, beta_b = gates(ab[:, :, 2], ab[:, :, 3], a_log_b, dt_bias_b)

    def flip(a):
        return jnp.flip(a, axis=2)

    o_f = gated_delta_rule(q, k, v, g_f, beta_f)
    o_b = flip(gated_delta_rule(flip(q), flip(k), flip(v), flip(g_b), flip(beta_b)))
    o = jnp.transpose(o_f + o_b, (0, 2, 1, 3))
    o = rms_norm(o, norm_g) * jax.nn.silu(z.astype(jnp.float32))
    return o.reshape(B, T, DN_VW).astype(h.dtype) @ w_out


def trunk(x, norm_g, attn_w_in, attn_lambda, attn_subln_g, attn_w_out,
          dn_w_in, dn_conv_w, dn_a_log_fwd, dn_dt_bias_fwd, dn_a_log_bwd, dn_dt_bias_bwd,
          dn_norm_g, dn_w_out, final_norm_g):
    for i in range(DEPTH):
        h = rms_norm(x, norm_g[i])
        j = i // N_MIXERS
        if i % N_MIXERS == 0:
            y = diff_attention_mixer(h, attn_w_in[j], attn_lambda[j], attn_subln_g[j],
                                     attn_w_out[j], lambda_init_fn(i))
        else:
            y = gated_deltanet_mixer(h, dn_w_in[j], dn_conv_w[j], dn_a_log_fwd[j], dn_dt_bias_fwd[j],
                                     dn_a_log_bwd[j], dn_dt_bias_bwd[j], dn_norm_g[j], dn_w_out[j])
        x = x + y.astype(x.dtype)
    return rms_norm(x, final_norm_g)


def setup_inputs(seed: int = 0) -> dict:
    key = jax.random.key(seed)
    ks = jax.random.split(key, 20)
    f32 = jnp.float32
    nrm = lambda k, s: jax.random.normal(k, s, f32)

    def dt_bias(k):
        dt = jnp.exp(jax.random.uniform(k, (N_DN_LAYERS, DN_V_HEADS), f32, math.log(1e-3), math.log(1e-1)))
        return dt + jnp.log(-jnp.expm1(-dt))

    return {
        'x_prompt': nrm(ks[0], (BATCH, SEQ, D_MODEL)),
        'x_sample': nrm(ks[1], (DEC_BATCH, DEC_SEQ, D_MODEL)),
        'norm_g': 1.0 + 0.02 * nrm(ks[2], (DEPTH, D_MODEL)),
        'attn_w_in': nrm(ks[3], (N_ATTN_LAYERS, D_MODEL, ATTN_IN)) * D_MODEL ** -0.5,
        'attn_lambda': 0.1 * nrm(ks[4], (N_ATTN_LAYERS, 4, ATTN_HEAD_DIM)),
        'attn_subln_g': 1.0 + 0.02 * nrm(ks[5], (N_ATTN_LAYERS, ATTN_V_DIM)),
        'attn_w_out': nrm(ks[6], (N_ATTN_LAYERS, ATTN_INNER, D_MODEL)) * ATTN_INNER ** -0.5,
        'dn_w_in': nrm(ks[7], (N_DN_LAYERS, D_MODEL, DN_IN)) * D_MODEL ** -0.5,
        'dn_conv_w': nrm(ks[8], (N_DN_LAYERS, CONV_W, DN_CONV_DIM)) * CONV_W ** -0.5,
        'dn_a_log_fwd': jnp.log(jax.random.uniform(ks[9], (N_DN_LAYERS, DN_V_HEADS), f32, 1.0, 16.0)),
        'dn_dt_bias_fwd': dt_bias(ks[10]),
        'dn_a_log_bwd': jnp.log(jax.random.uniform(ks[11], (N_DN_LAYERS, DN_V_HEADS), f32, 1.0, 16.0)),
        'dn_dt_bias_bwd': dt_bias(ks[12]),
        'dn_norm_g': 1.0 + 0.02 * nrm(ks[13], (N_DN_LAYERS, DN_DV)),
        'dn_w_out': nrm(ks[14], (N_DN_LAYERS, DN_VW, D_MODEL)) * DN_VW ** -0.5,
        'final_norm_g': 1.0 + 0.02 * nrm(ks[15], (D_MODEL,)),
    }


def reference(x_prompt, x_sample, norm_g, attn_w_in, attn_lambda, attn_subln_g, attn_w_out,
              dn_w_in, dn_conv_w, dn_a_log_fwd, dn_dt_bias_fwd, dn_a_log_bwd, dn_dt_bias_bwd,
              dn_norm_g, dn_w_out, final_norm_g):
    y_prompt = trunk(x_prompt, norm_g, attn_w_in, attn_lambda, attn_subln_g, attn_w_out,
                     dn_w_in, dn_conv_w, dn_a_log_fwd, dn_dt_bias_fwd, dn_a_log_bwd, dn_dt_bias_bwd,
                     dn_norm_g, dn_w_out, final_norm_g)
    y_sample = trunk(x_sample, norm_g, attn_w_in, attn_lambda, attn_subln_g, attn_w_out,
                     dn_w_in, dn_conv_w, dn_a_log_fwd, dn_dt_bias_fwd, dn_a_log_bwd, dn_dt_bias_bwd,
                     dn_norm_g, dn_w_out, final_norm_g)
    return (y_prompt, y_sample)
```

```python
import contextlib
import math
import numpy as np
import concourse.bass as bass
import concourse.mybir as mybir
from concourse.bass_utils import run_bass_kernel_spmd

F32 = mybir.dt.float32
BF16 = mybir.dt.bfloat16
AF = mybir.ActivationFunctionType
ALU = mybir.AluOpType
AX = mybir.AxisListType

D = 2048
SEQS = (16384, 4096, 4096)
NCORE = 4
HL = 2
KHL = 4
VHL = 8
ATT_W = (1280, 1 << 30)
NSLOT = 8
SAME_ENGINE_SYNC = True
INV_F32 = True
ATT_THRESH = 80.0
VA = 264
RMS_EPS = 1e-6
L2_EPS = 1e-6
NEG = -30000.0

C_IDENT, C_ONES, C_TRIF, C_TRIB, C_NTRIF, C_NTRIB, C_MBF, C_MBB, C_MBSF, C_MBSB = [i * 128 for i in range(10)]
C_J = 1280
C_JABS = C_J + 256
C_CB = C_JABS + 512
NCB = 132
C_SLP = C_CB + HL * NCB
C_END = C_SLP + 4


def core_heads(core):
    return (core, 7 - core)


def make_consts(core):
    c = np.zeros((128, C_END), np.float32)
    p = np.arange(128)[:, None].astype(np.float64)
    i = np.arange(128)[None, :].astype(np.float64)
    c[:, C_IDENT:C_IDENT + 128] = (p == i)
    c[:, C_ONES:C_ONES + 128] = 1.0
    c[:, C_TRIF:C_TRIF + 128] = (p <= i)
    c[:, C_TRIB:C_TRIB + 128] = (p >= i)
    c[:, C_NTRIF:C_NTRIF + 128] = -1.0 * (p <= i)
    c[:, C_NTRIB:C_NTRIB + 128] = -1.0 * (p >= i)
    c[:, C_MBF:C_MBF + 128] = np.where(i >= p, 0.0, NEG)
    c[:, C_MBB:C_MBB + 128] = np.where(i <= p, 0.0, NEG)
    c[:, C_MBSF:C_MBSF + 128] = np.where(p > i, 0.0, NEG)
    c[:, C_MBSB:C_MBSB + 128] = np.where(p < i, 0.0, NEG)
    j = np.arange(256)[None, :].astype(np.float64)
    c[:, C_J:C_J + 256] = j - p
    c[:, C_JABS:C_JABS + 256] = np.abs(j - p)
    c[:, C_JABS + 256:C_JABS + 512] = np.abs(j - p - 128)
    for sl, h in enumerate(core_heads(core)):
        m = 2.0 ** (-(h + 1))
        c[:, C_CB + sl * NCB:C_CB + (sl + 1) * NCB] = -m * 128.0 * np.arange(NCB)[None, :]
        c[:, C_SLP + 2 * sl] = -m
        c[:, C_SLP + 2 * sl + 1] = m
    return c


class Sem:
    __slots__ = ("h", "v")

    def __init__(self, nc, es, name):
        self.h = es.enter_context(nc.semaphore(name))
        self.v = 0


class Buf:
    __slots__ = ("w", "r", "t")

    def __init__(self, t):
        self.w = None
        self.r = {}
        self.t = t

    def __getitem__(self, idx):
        return self.t[idx]


class TR:
    def __init__(self, nc, es):
        self.nc = nc
        self.es = es
        self.pes = None
        self.engs = ("pe", "act", "dve", "pool", "sp")
        self.sem = {e: Sem(nc, es, "s_" + e) for e in ("pe", "act", "dve", "pool")}
        self.slots = {q: [Sem(nc, es, "d_%s%d" % (q, i)) for i in range(NSLOT)] for q in ("sp", "act", "pool")}
        self.slot_i = {q: 0 for q in self.slots}
        self.known = {e: {} for e in self.engs}
        self.q = {e: [] for e in self.engs}
        self.nins = 0
        self.pend = []
        self.ccsem = None
        self.P = [Buf(es.enter_context(nc.psum_tensor("P%d" % i, [128, 512], F32))) for i in range(8)]
        self.pi = 0

    def bank(self):
        b = self.P[self.pi]
        self.pi = (self.pi + 1) % 8
        return b

    @contextlib.contextmanager
    def phase(self):
        with contextlib.ExitStack() as pes:
            self.pes = pes
            yield
            self.barrier()
            self.emit()
            self.pes = None

    def emit(self):
        with self.nc.Block() as block:
            for e, sect in (("sp", block.sync), ("pe", block.tensor), ("act", block.scalar),
                            ("dve", block.vector), ("pool", block.gpsimd)):
                lst = self.q[e]
                if not lst:
                    continue

                def body(eng, lst=lst):
                    for f in lst:
                        f(eng)
                sect(body)
                self.nins += len(lst)
        self.q = {e: [] for e in self.engs}

    def sb(self, name, shape, dt):
        es = self.pes if self.pes is not None else self.es
        self.uid = getattr(self, "uid", 0) + 1
        return Buf(es.enter_context(self.nc.sbuf_tensor("%s_%d" % (name, self.uid), shape, dt)))

    def wait(self, e, so, val):
        k = self.known[e]
        if k.get(so, 0) >= val:
            return
        self.pend.append((so.h, val))
        k[so] = val

    def flush(self, e, keep_last):
        p = self.pend
        self.pend = []
        last = None
        if keep_last and p:
            last = p.pop()
        for h, val in p:
            self.q[e].append(lambda eng, h=h, val=val: eng.wait_ge(h, val))
        return last

    def _deps(self, e, reads, writes, own):
        same = SAME_ENGINE_SYNC and e != "pe"
        for b in reads:
            if b.w is not None:
                so, v = b.w
                if so is own and not same:
                    continue
                self.wait(e, so, v)
        for b in writes:
            if b.w is not None:
                so, v = b.w
                if not (so is own and not same):
                    self.wait(e, so, v)
            for so, v in b.r.items():
                if so is own:
                    continue
                self.wait(e, so, v)

    def op(self, e, fn, reads=(), writes=()):
        own = self.sem[e]
        self._deps(e, reads, writes, own)
        lw = self.flush(e, True)
        own.v += 1
        h = own.h
        if lw is None:
            self.q[e].append(lambda eng, fn=fn, h=h: fn(eng).then_inc(h, 1))
        else:
            self.q[e].append(lambda eng, fn=fn, h=h, wh=lw[0], wv=lw[1]: _winc(fn(eng), wh, wv, h, 1))
        for b in reads:
            b.r[own] = own.v
        for b in writes:
            b.w = (own, own.v)
            b.r = {}

    def dma(self, q, out, in_, reads=(), writes=()):
        sl = self.slots[q]
        i = self.slot_i[q]
        self.slot_i[q] = (i + 1) % NSLOT
        so = sl[i]
        self._deps(q, reads, writes, None)
        self.wait(q, so, so.v)
        lw = self.flush(q, True)
        so.v += 16
        h = so.h
        if lw is None:
            self.q[q].append(lambda eng, o=out, i_=in_, h=h: eng.dma_start(out=o, in_=i_).then_inc(h, 16))
        else:
            self.q[q].append(lambda eng, o=out, i_=in_, h=h, wh=lw[0], wv=lw[1]:
                             _winc(eng.dma_start(out=o, in_=i_), wh, wv, h, 16))
        for b in reads:
            b.r[so] = so.v
        for b in writes:
            b.w = (so, so.v)
            b.r = {}

    def barrier(self):
        allsems = list(self.sem.values()) + [s for sl in self.slots.values() for s in sl]
        for e in self.engs:
            for so in allsems:
                if so.v > 0:
                    self.wait(e, so, so.v)
            self.flush(e, False)

    def allgather(self, pairs):
        if self.ccsem is None:
            self.ccsem = Sem(self.nc, self.es, "ccsem")
        so = self.ccsem
        h = so.h
        for src, dst in pairs:
            so.v += 1
            self.q["pool"].append(lambda eng, h=h, s_=src, d_=dst: eng.collective_compute(
                "AllGather", ALU.bypass, replica_groups=[list(range(NCORE))],
                ins=[s_.ap().opt()], outs=[d_.ap().opt()]).then_inc(h, 1))
        for e in self.engs:
            self.wait(e, so, so.v)
            self.flush(e, False)
        self.emit()

    def mm(self, out, lhsT, rhs, start, stop, reads, writes):
        self.op("pe", lambda e, o=out, l=lhsT, r=rhs, s=start, t=stop: e.matmul(o, lhsT=l, rhs=r, start=s, stop=t),
                reads, writes)

    def tp(self, out, in_, ident, reads, writes):
        self.op("pe", lambda e, o=out, i=in_, d=ident: e.transpose(o, i, d), reads, writes)

    def act(self, out, in_, func, reads, writes, bias=0.0, scale=1.0):
        self.op("act", lambda e, o=out, i=in_, f=func, b=bias, s=scale: e.activation(out=o, in_=i, func=f, bias=b, scale=s),
                reads, writes)

    def tt(self, eng, out, in0, in1, op, reads, writes):
        self.op(eng, lambda e, o=out, a=in0, b=in1, p=op: e.tensor_tensor(out=o, in0=a, in1=b, op=p), reads, writes)

    def ts(self, eng, out, in0, s1, s2, op0, op1, reads, writes):
        if s2 is None:
            self.op(eng, lambda e, o=out, a=in0, s=s1, p=op0: e.tensor_scalar(out=o, in0=a, scalar1=s, scalar2=None, op0=p),
                    reads, writes)
        else:
            self.op(eng, lambda e, o=out, a=in0, s=s1, u=s2, p=op0, q=op1:
                    e.tensor_scalar(out=o, in0=a, scalar1=s, scalar2=u, op0=p, op1=q), reads, writes)

    def stt(self, eng, out, in0, scalar, in1, op0, op1, reads, writes):
        self.op(eng, lambda e, o=out, a=in0, s=scalar, b=in1, p=op0, q=op1:
                e.scalar_tensor_tensor(out=o, in0=a, scalar=s, in1=b, op0=p, op1=q), reads, writes)

    def cp(self, eng, out, in_, reads, writes):
        if eng == "act":
            self.op("act", lambda e, o=out, i=in_: e.copy(out=o, in_=i), reads, writes)
        else:
            self.op(eng, lambda e, o=out, i=in_: e.tensor_copy(out=o, in_=i), reads, writes)

    def red(self, eng, out, in_, reads, writes):
        self.op(eng, lambda e, o=out, i=in_: e.reduce_sum(out=o, in_=i, axis=AX.X), reads, writes)

    def rcp(self, out, in_, reads, writes):
        self.op("dve", lambda e, o=out, i=in_: e.reciprocal(out=o, in_=i), reads, writes)

    def ms(self, eng, ap, val, writes):
        self.op(eng, lambda e, a=ap, v=val: e.memset(a, v), (), writes)


def _winc(ins, wh, wv, h, inc):
    ins.wait_op(wh, wv, "sem-ge")
    return ins.then_inc(h, inc)


def bc_rows(ap2d):
    return ap2d.partition_broadcast(128).rearrange("p a b -> p (a b)")


class Prog:
    def __init__(self, seqs, dbg=()):
        self.seqs = seqs
        self.dbg = set(dbg)
        nc = self.nc = bass.Bass("TRN2", target_bir_lowering=False)
        Tm = self.Tm = max(seqs)
        nb, nt = Tm // 512, Tm // 128
        ei = lambda n, s: nc.dram_tensor(n, s, F32, kind="ExternalInput")
        self.xin = [ei("x%d" % i, [T, D]) for i, T in enumerate(seqs)]
        self.yout = [nc.dram_tensor("y%d" % i, [T, D], F32, kind="ExternalOutput") for i, T in enumerate(seqs)]
        self.norm_g = ei("norm_g", [4, D])
        self.attn_w_in = ei("attn_w_in", [2, D, 8 * HL * 128])
        self.attn_lambda = ei("attn_lambda", [2, 512])
        self.attn_subln_g = ei("attn_subln_g", [2, 256])
        self.attn_w_out = ei("attn_w_out", [2, 2048, 2048])
        self.dn_w_in = ei("dn_w_in", [2, D, 2 * KHL * 128 + 2 * VHL * 128 + 4 * VHL])
        self.dn_conv_w = ei("dn_conv_w", [2, 5, 2 * KHL * 128 + VHL * 128])
        self.dn_a_log_fwd = ei("dn_a_log_fwd", [2, VHL])
        self.dn_dt_bias_fwd = ei("dn_dt_bias_fwd", [2, VHL])
        self.dn_a_log_bwd = ei("dn_a_log_bwd", [2, VHL])
        self.dn_dt_bias_bwd = ei("dn_dt_bias_bwd", [2, VHL])
        self.dn_norm_g = ei("dn_norm_g", [2, 128])
        self.dn_w_out = ei("dn_w_out", [2, 4096, 2048])
        self.final_norm_g = ei("final_norm_g", [1, D])
        self.cst = ei("cst", [128, C_END])

        def scr(n, s, dt):
            return nc.dram_tensor(n, s, dt, kind=("ExternalOutput" if n in self.dbg else "Internal"))
        self.xs = scr("xs", [Tm, D], F32)
        self.hT = scr("hT", [nb, 128, 16, 512], BF16)
        self.y = scr("ysc", [nt, 128, 2048], F32)
        self.qT = scr("qT", [2 * HL, nb, 128, 512], BF16)
        self.kT = scr("kT", [2 * HL, nb, 128, 512], BF16)
        self.va = scr("va", [nt, 128, HL, VA], BF16)
        self.sg = scr("sg", [nt, 128, HL * 256], F32)
        self.pcs = scr("pcs", [2 * KHL + VHL, 128, Tm], F32)
        self.qn = scr("qn", [nt, 128, KHL, 128], BF16)
        self.kn = scr("kn", [nt, 128, KHL, 128], BF16)
        self.ktok = scr("ktok", [nt, 128, KHL * 128], BF16)
        self.vtok = scr("vtok", [nt, 128, VHL * 128], BF16)
        self.sz = scr("sz", [nt, 128, VHL * 128], F32)
        self.gb = scr("gb", [nt, 128, 4 * VHL], F32)
        self.of = scr("of", [nt, 128, VHL * 128], F32)
        self.og_loc = {}
        self.og_all = {}
        self.og_bpc = {"a": max(1, 1024 // (128 * 2 * HL)), "g": max(1, 1024 // (128 * VHL))}
        for si, T in enumerate(seqs):
            nb_ = T // 512
            for kind, kl in (("a", 2 * HL), ("g", VHL)):
                bpc = self.og_bpc[kind]
                nch = -(-nb_ // bpc)
                szs = [min(bpc, nb_ - c * bpc) * 128 * kl for c in range(nch)]
                self.og_loc[si, kind] = [scr("ogl_%d%s%d" % (si, kind, c), [szs[c], 512], BF16) for c in range(nch)]
                self.og_all[si, kind] = [scr("oga_%d%s%d" % (si, kind, c), [NCORE * szs[c], 512], BF16) for c in range(nch)]

    def og_loc_blk(self, si, kind, kl):
        bpc = self.og_bpc[kind]

        def f(b):
            c, bl = divmod(b, bpc)
            return self.og_loc[si, kind][c].ap().rearrange("(b p k) t -> b p k t", p=128, k=kl)[bl]
        return f

    def og_all_blk(self, si, kind, kl):
        bpc = self.og_bpc[kind]

        def f(r, b):
            c, bl = divmod(b, bpc)
            return self.og_all[si, kind][c].ap().rearrange("(r b p k) t -> r b p k t", r=NCORE, p=128, k=kl)[r, bl]
        return f

    def load_const_bf16(self, tr, name, col, n=128):
        st = tr.sb(name + "_f", [128, n], F32)
        tr.dma("sp", st[:], self.cst[:, col:col + n], writes=[st])
        b = tr.sb(name, [128, n], BF16)
        tr.cp("dve", b[:], st[:], [st], [b])
        return b

    def load_const_f32(self, tr, name, col, n=128):
        st = tr.sb(name, [128, n], F32)
        tr.dma("sp", st[:], self.cst[:, col:col + n], writes=[st])
        return st

    def resnorm(self, tr, T, x_src, y_src, x_dst, g_row, final, out):
        nt = T // 128
        with tr.phase():
            gt = tr.sb("gt", [128, D], F32)
            tr.dma("sp", gt[:], bc_rows(g_row), writes=[gt])
            ident = self.load_const_bf16(tr, "ident", C_IDENT)
            xts = [tr.sb("xt%d" % i, [128, D], F32) for i in range(2)]
            yts = [tr.sb("yt%d" % i, [128, D], F32) for i in range(2)]
            sq = tr.sb("sq", [128, D], F32)
            ssq = tr.sb("ssq", [128, 2], F32)
            hbs = [tr.sb("hb%d" % i, [128, D], BF16) for i in range(2)]
            hfs = [tr.sb("hf%d" % i, [128, D], F32) for i in range(2)] if final else None
            hTt = [tr.sb("hTt%d" % i, [128, 16, 512], BF16) for i in range(2)]
            for tt in range(nt):
                xt = xts[tt % 2]
                tr.dma("sp", xt[:], x_src[tt * 128:(tt + 1) * 128, :], writes=[xt])
                if y_src is not None:
                    yt = yts[tt % 2]
                    tr.dma("sp", yt[:], y_src[tt], writes=[yt])
                    tr.tt("dve", xt[:], xt[:], yt[:], ALU.add, [xt, yt], [xt])
                if x_dst is not None:
                    tr.dma("pool", x_dst[tt * 128:(tt + 1) * 128, :], xt[:], reads=[xt])
                tr.act(sq[:], xt[:], AF.Square, [xt], [sq])
                tr.red("dve", ssq[:, 0:1], sq[:], [sq], [ssq])
                tr.act(ssq[:, 1:2], ssq[:, 0:1], AF.Sqrt, [ssq], [ssq], bias=RMS_EPS, scale=1.0 / D)
                tr.rcp(ssq[:, 1:2], ssq[:, 1:2], [ssq], [ssq])
                if final:
                    hf = hfs[tt % 2]
                    tr.stt("dve", hf[:], xt[:], ssq[:, 1:2], gt[:], ALU.mult, ALU.mult, [xt, ssq, gt], [hf])
                    tr.dma("pool", out[tt * 128:(tt + 1) * 128, :], hf[:], reads=[hf])
                    continue
                hb = hbs[tt % 2]
                tr.stt("dve", hb[:], xt[:], ssq[:, 1:2], gt[:], ALU.mult, ALU.mult, [xt, ssq, gt], [hb])
                b, s = tt // 4, tt % 4
                ht = hTt[b % 2]
                for half in range(2):
                    pb = tr.bank()
                    pv = pb[:].bitcast(BF16)
                    for k8 in range(8):
                        kc = half * 8 + k8
                        tr.tp(pv[:, k8 * 128:(k8 + 1) * 128], hb[:, kc * 128:(kc + 1) * 128], ident[:], [hb, ident], [pb])
                    tr.cp("act", ht[:, half * 8:(half + 1) * 8, s * 128:(s + 1) * 128],
                          pv.rearrange("p (a b) -> p a b", b=128), [pb], [ht])
                if s == 3:
                    tr.dma("pool", out[b], ht[:], reads=[ht])

    def proj(self, tr, T, src_loader, nk, w_ap, wc, mode, evac_factory):
        nb = T // 512
        with tr.phase():
            wt = tr.sb("wt", [128, nk, wc], BF16)
            stg = [tr.sb("wstg%d" % i, [128, wc], F32) for i in range(2)]
            for kc in range(nk):
                st = stg[kc % 2]
                tr.dma("sp", st[:], w_ap[kc * 128:(kc + 1) * 128, :], writes=[st])
                if kc % 2:
                    tr.cp("act", wt[:, kc, :], st[:], [st], [wt])
                else:
                    tr.cp("dve", wt[:, kc, :], st[:], [st], [wt])
            sbt = [tr.sb("psrc%d" % i, [128, nk, 512], BF16) for i in range(2)]
            evac = evac_factory(tr)
            for b in range(nb):
                s = sbt[b % 2]
                src_loader(tr, b, s)
                if mode == "feat":
                    for ct in range(wc // 128):
                        pb = tr.bank()
                        for kc in range(nk):
                            tr.mm(pb[:], wt[:, kc, ct * 128:(ct + 1) * 128], s[:, kc, :], kc == 0, kc == nk - 1, [wt, s], [pb])
                        evac(b, ct, pb)
                else:
                    n = min(wc, 512)
                    for sub in range(4):
                        for cg in range(max(1, wc // 512)):
                            pb = tr.bank()
                            for kc in range(nk):
                                tr.mm(pb[:, 0:n], s[:, kc, sub * 128:(sub + 1) * 128], wt[:, kc, cg * n:(cg + 1) * n],
                                      kc == 0, kc == nk - 1, [wt, s], [pb])
                            evac(b, sub, cg, pb)

    def src_hT(self):
        def ld(tr, b, s):
            tr.dma("sp", s[:], self.hT[b], writes=[s])
        return ld

    def src_gathered(self, si, kind, kl):
        v = self.og_all_blk(si, kind, kl)

        def ld(tr, b, s):
            for r in range(NCORE):
                tr.dma("sp", s[:, r * kl:(r + 1) * kl, :], v(r, b), writes=[s])
        return ld

    def ev_feat_bf16(self, dst, scale):
        def fac(tr):
            stg = [tr.sb("evq%d" % i, [128, 512], BF16) for i in range(3)]
            cnt = [0]

            def ev(b, ct, pb):
                st = stg[cnt[0] % 3]
                cnt[0] += 1
                if cnt[0] % 2:
                    tr.act(st[:], pb[:], AF.Copy, [pb], [st], scale=scale)
                else:
                    tr.ts("dve", st[:], pb[:], scale, None, ALU.mult, None, [pb], [st])
                tr.dma("pool", dst[ct][b], st[:], reads=[st])
            return ev
        return fac

    def ev_v(self):
        def fac(tr):
            stg = [tr.sb("evv%d" % i, [128, HL, VA], BF16) for i in range(2)]
            for st in stg:
                tr.ms("dve", st[:], 1.0, [st])

            def ev(b, sub, cg, pb):
                tt = b * 4 + sub
                st = stg[tt % 2]
                src = pb[:].rearrange("p (h e) -> p h e", e=256)
                tr.cp("act", st[:, 0:HL, 0:256], src, [pb], [st])
                tr.dma("pool", self.va[tt], st[:], reads=[st])
            return ev
        return fac

    def ev_tok_f32(self, dst, col0, ncg, func, wcg=512):
        def fac(tr):
            stg = [tr.sb("evt%d" % i, [128, ncg * wcg], F32) for i in range(2)]

            def ev(b, sub, cg, pb):
                tt = b * 4 + sub
                st = stg[tt % 2]
                if func is not None:
                    tr.act(st[:, cg * wcg:(cg + 1) * wcg], pb[:, 0:wcg], func, [pb], [st])
                elif cg % 2:
                    tr.cp("act", st[:, cg * wcg:(cg + 1) * wcg], pb[:, 0:wcg], [pb], [st])
                else:
                    tr.cp("dve", st[:, cg * wcg:(cg + 1) * wcg], pb[:, 0:wcg], [pb], [st])
                if cg == ncg - 1:
                    tr.dma("pool", dst[tt][:, col0:col0 + ncg * wcg], st[:], reads=[st])
            return ev
        return fac

    def ev_pc(self, ct0):
        def fac(tr):
            stg = [tr.sb("evp%d" % i, [128, 512], F32) for i in range(3)]
            cnt = [0]

            def ev(b, ct, pb):
                st = stg[cnt[0] % 3]
                cnt[0] += 1
                tr.cp("act" if cnt[0] % 2 else "dve", st[:], pb[:], [pb], [st])
                tr.dma("pool", self.pcs[ct0 + ct][:, b * 512:(b + 1) * 512], st[:], reads=[st])
            return ev
        return fac

    def ev_gates(self, j):
        H = VHL

        def fac(tr):
            dtb = tr.sb("dtb", [128, 2, H], F32)
            nA = tr.sb("nA", [128, 2, H], F32)
            tr.dma("sp", dtb[:, 0, :], bc_rows(self.dn_dt_bias_fwd[j:j + 1, :]), writes=[dtb])
            tr.dma("sp", dtb[:, 1, :], bc_rows(self.dn_dt_bias_bwd[j:j + 1, :]), writes=[dtb])
            tr.dma("sp", nA[:, 0, :], bc_rows(self.dn_a_log_fwd[j:j + 1, :]), writes=[nA])
            tr.dma("sp", nA[:, 1, :], bc_rows(self.dn_a_log_bwd[j:j + 1, :]), writes=[nA])
            tr.act(nA[:], nA[:], AF.Exp, [nA], [nA])
            tr.ts("dve", nA[:], nA[:], -1.0, None, ALU.mult, None, [nA], [nA])
            xa = tr.sb("g_xa", [128, 2, H], F32)
            ax = tr.sb("g_ax", [128, 2, H], F32)
            outs = [tr.sb("g_out%d" % i, [128, 4, H], F32) for i in range(2)]

            def ev(b, sub, cg, pb):
                tt = b * 4 + sub
                o = outs[tt % 2]
                pv = pb[:, 0:4 * H].rearrange("p (a b) -> p a b", b=H)
                for d in range(2):
                    tr.tt("dve", xa[:, d, :], pv[:, 2 * d, :], dtb[:, d, :], ALU.add, [pb, dtb], [xa])
                tr.act(ax[:], xa[:], AF.Abs, [xa], [ax])
                tr.act(ax[:], ax[:], AF.Exp, [ax], [ax], scale=-1.0)
                tr.act(ax[:], ax[:], AF.Ln, [ax], [ax], bias=1.0)
                tr.ts("dve", xa[:], xa[:], 0.0, None, ALU.max, None, [xa], [xa])
                tr.tt("dve", xa[:], xa[:], ax[:], ALU.add, [xa, ax], [xa])
                for d in range(2):
                    tr.tt("dve", o[:, 2 * d, :], xa[:, d, :], nA[:, d, :], ALU.mult, [xa, nA], [o])
                    tr.act(o[:, 2 * d + 1, :], pv[:, 2 * d + 1, :], AF.Sigmoid, [pb], [o])
                tr.dma("pool", self.gb[tt], o[:].rearrange("p a b -> p (a b)"), reads=[o])
            return ev
        return fac

    def attn_core(self, tr, T, j, lam_init, og_view):
        nt = T // 128
        nq = T // 256
        with tr.phase():
            ident = self.load_const_bf16(tr, "ident", C_IDENT)
            Jt = self.load_const_f32(tr, "Jt", C_J, 256)
            Ja = self.load_const_f32(tr, "Ja", C_JABS, 512)
            cb = self.load_const_f32(tr, "cb", C_CB, HL * NCB)
            slp = self.load_const_f32(tr, "slp", C_SLP, 4)
            lv = tr.sb("lv", [128, 512], F32)
            tr.dma("sp", lv[:], bc_rows(self.attn_lambda[j:j + 1, :]), writes=[lv])
            lp = tr.sb("lp", [128, 256], F32)
            ls = tr.sb("ls", [128, 4], F32)
            tr.tt("dve", lp[:, 0:128], lv[:, 0:128], lv[:, 128:256], ALU.mult, [lv], [lp])
            tr.tt("dve", lp[:, 128:256], lv[:, 256:384], lv[:, 384:512], ALU.mult, [lv], [lp])
            tr.red("dve", ls[:, 0:2], lp[:].rearrange("p (a b) -> p a b", b=128), [lp], [ls])
            tr.act(ls[:, 0:2], ls[:, 0:2], AF.Exp, [ls], [ls])
            tr.tt("dve", ls[:, 2:3], ls[:, 1:2], ls[:, 0:1], ALU.subtract, [ls], [ls])
            tr.ts("dve", ls[:, 3:4], ls[:, 2:3], -lam_init, None, ALU.add, None, [ls], [ls])
            sgn = tr.sb("sgn", [128, 256], F32)
            tr.dma("sp", sgn[:], bc_rows(self.attn_subln_g[j:j + 1, :]), writes=[sgn])
            tr.ts("dve", sgn[:], sgn[:], 1.0 - lam_init, None, ALU.mult, None, [sgn], [sgn])

            qts = [tr.sb("aq%d" % i, [128, 2, 256], BF16) for i in range(2)]
            sgts = [tr.sb("asg%d" % i, [128, 2, 256], F32) for i in range(2)]
            kts = [tr.sb("ak%d" % i, [128, 2, 512], BF16) for i in range(3)]
            vts = [tr.sb("av%d" % i, [128, 4, VA], BF16) for i in range(3)]
            tbs = [tr.sb("atb%d" % i, [128, 512], F32) for i in range(2)]
            pbs = [tr.sb("apb%d" % i, [128, 512], BF16) for i in range(3)]
            om = tr.sb("aom", [128, 2, 2, 256], F32)
            rden = tr.sb("arden", [128, 4], F32)
            oc = tr.sb("aoc", [128, 2, 256], F32)
            osq = tr.sb("aosq", [128, 2, 256], F32)
            ost = tr.sb("aost", [128, 4], F32)
            ogb = tr.sb("aogb", [128, 2, 256], BF16)
            ogTt = [tr.sb("aogT%d" % i, [128, 2, 256], BF16) for i in range(2)]
            acc = tr.P[0:4]
            sc = tr.P[4:6]
            tpb = tr.P[6:8]
            nblk = 0
            it = 0
            for h in range(HL):
                W = ATT_W[h]
                mneg = slp[:, 2 * h:2 * h + 1]
                mpos = slp[:, 2 * h + 1:2 * h + 2]
                for qt in range(nq):
                    q0 = qt * 256
                    qtile = qts[it % 2]
                    sgt = sgts[it % 2]
                    ogt = ogTt[it % 2]
                    it += 1
                    qb, qo = qt // 2, (qt % 2) * 256
                    for mp in range(2):
                        tr.dma("sp", qtile[:, mp, :], self.qT[2 * h + mp][qb][:, qo:qo + 256], writes=[qtile])
                    tr.dma("sp", sgt[:], self.sg[qt * 2:qt * 2 + 2, :, h * 256:(h + 1) * 256].rearrange("s p e -> p s e"),
                           writes=[sgt])
                    kt_lo = max(0, (q0 - W) // 128)
                    kt_hi = min(nt, -((-(q0 + 256 + W)) // 128))
                    kts_list = list(range(kt_lo, kt_hi))
                    cur_kb = -1
                    for idx, kt in enumerate(kts_list):
                        kb, ks = kt // 4, kt % 4
                        if kb != cur_kb:
                            cur_kb = kb
                            ktile = kts[nblk % 3]
                            vtile = vts[nblk % 3]
                            nblk += 1
                            for mp in range(2):
                                tr.dma("sp", ktile[:, mp, :], self.kT[2 * h + mp][kb], writes=[ktile])
                            tr.dma("sp", vtile[:], self.va[kb * 4:kb * 4 + 4, :, h, :].rearrange("s p e -> p s e"),
                                   writes=[vtile])
                        k0 = kt * 128
                        sb_ = sc[idx % 2]
                        for mp in range(2):
                            tr.mm(sb_[:, mp * 256:(mp + 1) * 256], ktile[:, mp, ks * 128:(ks + 1) * 128], qtile[:, mp, :],
                                  True, True, [ktile, qtile], [sb_])
                        tb = tbs[idx % 2]
                        pb = pbs[idx % 3]
                        sv = sb_[:].rearrange("p (a b) -> p a b", b=256)
                        tv = tb[:].rearrange("p (a b) -> p a b", b=256)
                        if k0 + 127 < q0:
                            jv = Jt[:].unsqueeze(1).to_broadcast([128, 2, 256])
                            tr.stt("dve", tv, jv, mneg, sv, ALU.mult, ALU.add, [Jt, sb_, slp], [tb])
                            ci = (q0 - k0) // 128
                        elif k0 > q0 + 255:
                            jv = Jt[:].unsqueeze(1).to_broadcast([128, 2, 256])
                            tr.stt("dve", tv, jv, mpos, sv, ALU.mult, ALU.add, [Jt, sb_, slp], [tb])
                            ci = (k0 - q0) // 128
                        else:
                            i_ = (k0 - q0) // 128
                            jv = Ja[:, i_ * 256:(i_ + 1) * 256].unsqueeze(1).to_broadcast([128, 2, 256])
                            tr.stt("dve", tv, jv, mneg, sv, ALU.mult, ALU.add, [Ja, sb_, slp], [tb])
                            ci = 0
                        tr.act(pb[:], tb[:], AF.Exp, [tb, cb], [pb], bias=cb[:, h * NCB + ci:h * NCB + ci + 1])
                        first, last = idx == 0, idx == len(kts_list) - 1
                        for mp in range(2):
                            for qs in range(2):
                                a = acc[mp * 2 + qs]
                                tr.mm(a[:, 0:257], pb[:, mp * 256 + qs * 128:mp * 256 + (qs + 1) * 128], vtile[:, ks, 0:257],
                                      first, last, [pb, vtile], [a])
                    for mp in range(2):
                        for qs in range(2):
                            a = acc[mp * 2 + qs]
                            c_ = mp * 2 + qs
                            tr.rcp(rden[:, c_:c_ + 1], a[:, 256:257], [a], [rden])
                            if qs:
                                tr.act(om[:, mp, qs, :], a[:, 0:256], AF.Copy, [a, rden], [om], scale=rden[:, c_:c_ + 1])
                            else:
                                tr.ts("dve", om[:, mp, qs, :], a[:, 0:256], rden[:, c_:c_ + 1], None, ALU.mult, None, [a, rden], [om])
                    tr.stt("dve", oc[:], om[:, 1, :, :], ls[:, 3:4], om[:, 0, :, :], ALU.mult, ALU.add, [om, ls], [oc])
                    tr.act(osq[:], oc[:], AF.Square, [oc], [osq])
                    tr.red("dve", ost[:, 0:2], osq[:], [osq], [ost])
                    tr.act(ost[:, 2:4], ost[:, 0:2], AF.Sqrt, [ost], [ost], bias=RMS_EPS, scale=1.0 / 256)
                    tr.rcp(ost[:, 2:4], ost[:, 2:4], [ost], [ost])
                    tr.tt("dve", oc[:], oc[:], ost[:, 2:4].unsqueeze(2).to_broadcast([128, 2, 256]), ALU.mult, [oc, ost], [oc])
                    tr.tt("pool", oc[:], oc[:], sgn[:].unsqueeze(1).to_broadcast([128, 2, 256]), ALU.mult, [oc, sgn], [oc])
                    tr.tt("dve", ogb[:], oc[:], sgt[:], ALU.mult, [oc, sgt], [ogb])
                    tb_ = tpb[it % 2]
                    tv_ = tb_[:].bitcast(BF16)
                    for ec in range(2):
                        for qs in range(2):
                            c_ = (ec * 2 + qs) * 128
                            tr.tp(tv_[:, c_:c_ + 128], ogb[:, qs, ec * 128:(ec + 1) * 128], ident[:], [ogb, ident], [tb_])
                    tr.cp("act", ogt[:], tv_[:, 0:512].rearrange("p (a b) -> p a b", b=256), [tb_], [ogt])
                    tr.dma("pool", og_view(qb)[:, 2 * h:2 * h + 2, qo:qo + 256], ogt[:], reads=[ogt])

    def gdn_conv(self, tr, T, j):
        nb = T // 512
        NCT = 2 * KHL + VHL
        with tr.phase():
            ident = self.load_const_bf16(tr, "ident", C_IDENT)
            identf = self.load_const_f32(tr, "identf", C_IDENT)
            onesf = self.load_const_f32(tr, "onesf", C_ONES)
            cwr = tr.sb("cwr", [5, NCT * 128], F32)
            tr.dma("sp", cwr[:], self.dn_conv_w[j], writes=[cwr])
            cw = tr.sb("cw", [128, NCT, 5], F32)
            for ct in range(NCT):
                pb = tr.bank()
                tr.tp(pb[:, 0:5], cwr[0:5, ct * 128:(ct + 1) * 128], identf[0:5, 0:5], [cwr, identf], [pb])
                tr.cp("dve", cw[:, ct, :], pb[:, 0:5], [pb], [cw])
            xins = [tr.sb("cx%d" % i, [128, 516], F32) for i in range(3)]
            accs = [tr.sb("ca%d" % i, [128, 512], F32) for i in range(2)]
            sxs = [tr.sb("cs%d" % i, [128, 512], F32) for i in range(2)]
            sqs = [tr.sb("cq%d" % i, [128, 512], F32) for i in range(2)]
            rns = [tr.sb("cr%d" % i, [128, 512], F32) for i in range(2)]
            xnbs = [tr.sb("cn%d" % i, [128, 512], BF16) for i in range(2)]
            tks = [tr.sb("ct%d" % i, [128, 4, 128], BF16) for i in range(2)]
            it = 0
            for ct in range(NCT):
                for b in range(nb):
                    xin = xins[it % 3]
                    acc = accs[it % 2]
                    sx = sxs[it % 2]
                    sq = sqs[it % 2]
                    rn = rns[it % 2]
                    xnb = xnbs[it % 2]
                    tk = tks[it % 2]
                    it += 1
                    lo = b * 512 - 2
                    hi = b * 512 + 514
                    c0, c1 = 0, 516
                    if b == 0:
                        tr.ms("dve", xin[:, 0:2], 0.0, [xin])
                        lo, c0 = 0, 2
                    if b == nb - 1:
                        tr.ms("dve", xin[:, 514:516], 0.0, [xin])
                        hi, c1 = T, 514
                    tr.dma("sp", xin[:, c0:c1], self.pcs[ct][:, lo:hi], writes=[xin])
                    tr.ts("dve", acc[:], xin[:, 0:512], cw[:, ct, 0:1], None, ALU.mult, None, [xin, cw], [acc])
                    for tap in range(1, 5):
                        tr.stt("dve", acc[:], xin[:, tap:tap + 512], cw[:, ct, tap:tap + 1], acc[:], ALU.mult, ALU.add,
                               [xin, cw, acc], [acc])
                    tr.act(sx[:], acc[:], AF.Silu, [acc], [sx])
                    if ct < 2 * KHL:
                        kh = ct % KHL
                        tr.act(sq[:], sx[:], AF.Square, [sx], [sq])
                        pb = tr.bank()
                        tr.mm(pb[:], onesf[:], sq[:], True, True, [onesf, sq], [pb])
                        tr.act(rn[:], pb[:], AF.Sqrt, [pb], [rn], bias=L2_EPS)
                        tr.rcp(rn[:], rn[:], [rn], [rn])
                        tr.stt("dve", xnb[:], sx[:], (128 ** -0.5 if ct < KHL else 1.0), rn[:], ALU.mult, ALU.mult, [sx, rn], [xnb])
                        dst = self.qn if ct < KHL else self.kn
                        tr.dma("pool", dst[b * 4:b * 4 + 4, :, kh, :].rearrange("s p t -> p s t"),
                               xnb[:].rearrange("p (s t) -> p s t", t=128), reads=[xnb])
                        if ct < KHL:
                            continue
                    else:
                        tr.cp("act", xnb[:], sx[:], [sx], [xnb])
                    pb = tr.bank()
                    pv = pb[:].bitcast(BF16)
                    for s in range(4):
                        tr.tp(pv[:, s * 128:(s + 1) * 128], xnb[:, s * 128:(s + 1) * 128], ident[:], [xnb, ident], [pb])
                    tr.cp("act", tk[:], pv[:, 0:512].rearrange("p (s d) -> p s d", d=128), [pb], [tk])
                    if ct < 2 * KHL:
                        kh = ct - KHL
                        tr.dma("pool", self.ktok[b * 4:b * 4 + 4, :, kh * 128:(kh + 1) * 128].rearrange("s p d -> p s d"),
                               tk[:], reads=[tk])
                    else:
                        vh = ct - 2 * KHL
                        tr.dma("pool", self.vtok[b * 4:b * 4 + 4, :, vh * 128:(vh + 1) * 128].rearrange("s p d -> p s d"),
                               tk[:], reads=[tk])

    def gdn_scan(self, tr, T, j, d, og_view):
        nt = T // 128
        bwd = d == 1
        H = VHL
        with tr.phase():
            ident = self.load_const_bf16(tr, "ident", C_IDENT)
            identf = self.load_const_f32(tr, "identf", C_IDENT)
            onesf = self.load_const_f32(tr, "onesf", C_ONES)
            tri = self.load_const_f32(tr, "tri", C_TRIB if bwd else C_TRIF)
            ntri = self.load_const_f32(tr, "ntri", C_NTRIB if bwd else C_NTRIF)
            mb = self.load_const_f32(tr, "mb", C_MBB if bwd else C_MBF)
            mbs = self.load_const_f32(tr, "mbs", C_MBSB if bwd else C_MBSF)
            gn = tr.sb("gn", [128, 128], F32)
            tr.dma("sp", gn[:], bc_rows(self.dn_norm_g[j:j + 1, :]), writes=[gn])
            S32 = tr.sb("S32", [128, H, 128], F32)
            Sb = tr.sb("Sb", [128, H, 128], BF16)
            tr.ms("dve", S32[:], 0.0, [S32])
            tr.ms("dve", Sb[:], 0.0, [Sb])
            qchs = [tr.sb("qch%d" % i, [128, KHL, 128], BF16) for i in range(2)]
            kchs = [tr.sb("kch%d" % i, [128, KHL, 128], BF16) for i in range(2)]
            ktks = [tr.sb("ktk%d" % i, [128, KHL, 128], BF16) for i in range(2)]
            vtks = [tr.sb("vtk%d" % i, [128, H, 128], BF16) for i in range(2)]
            gbts = [tr.sb("gbt%d" % i, [128, 4, H], F32) for i in range(2)]
            if bwd:
                ofts = [tr.sb("oft%d" % i, [128, H, 128], F32) for i in range(2)]
                szts = [tr.sb("szt%d" % i, [128, H, 128], F32) for i in range(2)]
                ogTts = [tr.sb("sogT%d" % i, [128, H, 128], BF16) for i in range(2)]
            else:
                osts = [tr.sb("ost%d" % i, [128, H, 128], F32) for i in range(2)]
            gc = tr.sb("gc", [128, 2 * H], F32)
            egc = tr.sb("egc", [128, 2 * H], F32)
            dkf = tr.sb("dkf", [128, H], F32)
            bk = tr.sb("bk", [128, H], F32)
            NG = 2
            G = []
            for g_ in range(NG):
                t_ = {}
                for nm, dt_ in (("gbc", F32), ("E", BF16), ("ETs", F32), ("Lb", F32), ("Ub", F32), ("Aq", BF16),
                                ("PA", F32), ("QA", F32), ("N32", F32), ("Nb", BF16), ("bv", BF16), ("bke", BF16),
                                ("kd", BF16), ("u32", F32), ("wTb", BF16), ("vnew", BF16), ("o1", F32), ("ot", F32),
                                ("osq", F32), ("ogb", BF16), ("stmp", F32)):
                    t_[nm] = tr.sb("%s_%d" % (nm, g_), [128, 4, 128], dt_)
                t_["ost"] = tr.sb("gost_%d" % g_, [128, 8], F32)
                G.append(t_)

            def b4(ap2):
                return ap2.unsqueeze(2).to_broadcast([128, 4, 128])

            def fl(buf):
                return buf[:].rearrange("p a b -> p (a b)")

            order = range(nt - 1, -1, -1) if bwd else range(nt)
            for ci, n in enumerate(order):
                qch, kch, ktk, vtk, gbt = qchs[ci % 2], kchs[ci % 2], ktks[ci % 2], vtks[ci % 2], gbts[ci % 2]
                tr.dma("sp", qch[:], self.qn[n], writes=[qch])
                tr.dma("sp", kch[:], self.kn[n], writes=[kch])
                tr.dma("sp", fl(ktk), self.ktok[n], writes=[ktk])
                tr.dma("sp", fl(vtk), self.vtok[n], writes=[vtk])
                tr.dma("sp", fl(gbt), self.gb[n], writes=[gbt])
                if bwd:
                    oft, szt, ogTt = ofts[ci % 2], szts[ci % 2], ogTts[ci % 2]
                    tr.dma("sp", fl(oft), self.of[n], writes=[oft])
                    tr.dma("sp", fl(szt), self.sz[n], writes=[szt])
                else:
                    ostg = osts[ci % 2]
                graw = gbt[:, 2 * d, :]
                beta = gbt[:, 2 * d + 1, :]
                pb = tr.bank()
                tr.mm(pb[:, 0:H], tri[:], graw, True, True, [tri, gbt], [pb])
                tr.mm(pb[:, H:2 * H], onesf[:], graw, True, True, [onesf, gbt], [pb])
                tr.cp("dve", gc[:], pb[:, 0:2 * H], [pb], [gc])
                tr.act(egc[:], gc[:], AF.Exp, [gc], [egc])
                tr.tt("dve", dkf[:], gc[:, H:2 * H], gc[:, 0:H], ALU.subtract, [gc], [dkf])
                tr.act(dkf[:], dkf[:], AF.Exp, [dkf], [dkf])
                tr.tt("dve", bk[:], beta, egc[:, 0:H], ALU.mult, [gbt, egc], [bk])

                def group(gq, t_):
                    gbc, E, ETs, Lb, Ub, Aq = t_["gbc"], t_["E"], t_["ETs"], t_["Lb"], t_["Ub"], t_["Aq"]
                    N32, Nb, bv, bke, kd = t_["N32"], t_["Nb"], t_["bv"], t_["bke"], t_["kd"]
                    u32, wTb, vnew, o1, ot, osq, ogb, stmp, ost = (t_["u32"], t_["wTb"], t_["vnew"], t_["o1"], t_["ot"],
                                                                   t_["osq"], t_["ogb"], t_["stmp"], t_["ost"])
                    h0 = 4 * gq
                    kh0 = 2 * gq
                    tr.cp("dve", gbc[:], b4(graw[:, h0:h0 + 4]), [gbt], [gbc])
                    pA = tr.bank()
                    pB = tr.bank()
                    for hh in range(4):
                        cs = slice(hh * 128, (hh + 1) * 128)
                        tr.mm(pA[:, cs], gbc[:, hh, :], tri[:], True, False, [gbc, tri], [pA])
                        tr.mm(pA[:, cs], ntri[:], gbc[:, hh, :], False, False, [gbc, ntri], [pA])
                        tr.mm(pA[:, cs], identf[:], mb[:], False, True, [identf, mb], [pA])
                        tr.mm(pB[:, cs], tri[:], gbc[:, hh, :], True, False, [gbc, tri], [pB])
                        tr.mm(pB[:, cs], gbc[:, hh, :], ntri[:], False, False, [gbc, ntri], [pB])
                        tr.mm(pB[:, cs], identf[:], mbs[:], False, True, [identf, mbs], [pB])
                    tr.act(fl(E), pA[:], AF.Exp, [pA], [E])
                    tr.act(fl(ETs), pB[:], AF.Exp, [pB], [ETs])
                    pC = tr.bank()
                    for i_ in range(2):
                        tr.mm(pC[:, i_ * 128:(i_ + 1) * 128], kch[:, kh0 + i_, :], kch[:, kh0 + i_, :], True, True, [kch], [pC])
                        tr.mm(pC[:, (2 + i_) * 128:(3 + i_) * 128], kch[:, kh0 + i_, :], qch[:, kh0 + i_, :], True, True,
                              [kch, qch], [pC])
                    for hh in range(4):
                        i_ = hh // 2
                        tr.stt("dve", Lb[:, hh, :], pC[:, i_ * 128:(i_ + 1) * 128], beta[:, h0 + hh:h0 + hh + 1], ETs[:, hh, :],
                               ALU.mult, ALU.mult, [pC, gbt, ETs], [Lb])
                    for i_ in range(2):
                        tr.tt("dve", Aq[:, 2 * i_:2 * i_ + 2, :],
                              pC[:, (2 + i_) * 128:(3 + i_) * 128].unsqueeze(1).to_broadcast([128, 2, 128]),
                              E[:, 2 * i_:2 * i_ + 2, :], ALU.mult, [pC, E], [Aq])
                    yield
                    pT = tr.bank()
                    for hh in range(4):
                        tr.tp(pT[:, hh * 128:(hh + 1) * 128], Lb[:, hh, :], identf[:], [Lb, identf], [pT])
                    tr.cp("act", fl(Ub), pT[:], [pT], [Ub])
                    tr.tt("dve", N32[:], identf[:].unsqueeze(1).to_broadcast([128, 4, 128]), Ub[:], ALU.subtract,
                          [identf, Ub], [N32])
                    yield
                    Pc, Qc = Ub, Lb
                    for k in range(1, 7):
                        if k % 2:
                            Qn, Pn = t_["QA"], t_["PA"]
                        else:
                            Qn, Pn = Lb, Ub
                        pX = tr.bank()
                        for hh in range(4):
                            tr.mm(pX[:, hh * 128:(hh + 1) * 128], Pc[:, hh, :], Qc[:, hh, :], True, True, [Pc, Qc], [pX])
                        if k < 6:
                            pY = tr.bank()
                            for hh in range(4):
                                tr.mm(pY[:, hh * 128:(hh + 1) * 128], Qc[:, hh, :], Pc[:, hh, :], True, True, [Pc, Qc], [pY])
                        tr.cp("act", fl(Qn), pX[:], [pX], [Qn])
                        if k < 6:
                            tr.cp("dve", fl(Pn), pY[:], [pY], [Pn])
                        yield
                        pZ = tr.bank()
                        for hh in range(4):
                            tr.mm(pZ[:, hh * 128:(hh + 1) * 128], Qn[:, hh, :], N32[:, hh, :], True, True, [Qn, N32], [pZ])
                        tr.tt("dve", fl(N32), fl(N32), pZ[:], ALU.add, [N32, pZ], [N32])
                        Pc, Qc = Pn, Qn
                        yield
                    tr.cp("act", Nb[:], N32[:], [N32], [Nb])
                    tr.tt("dve", bv[:], vtk[:, h0:h0 + 4, :], b4(beta[:, h0:h0 + 4]), ALU.mult, [vtk, gbt], [bv])
                    for i_ in range(2):
                        kx = ktk[:, kh0 + i_, :].unsqueeze(1).to_broadcast([128, 2, 128])
                        hs = slice(h0 + 2 * i_, h0 + 2 * i_ + 2)
                        tr.tt("pool", bke[:, 2 * i_:2 * i_ + 2, :], kx, bk[:, hs].unsqueeze(2).to_broadcast([128, 2, 128]),
                              ALU.mult, [ktk, bk], [bke])
                        tr.tt("pool", kd[:, 2 * i_:2 * i_ + 2, :], kx, dkf[:, hs].unsqueeze(2).to_broadcast([128, 2, 128]),
                              ALU.mult, [ktk, dkf], [kd])
                    yield
                    pU = tr.bank()
                    pW = tr.bank()
                    for hh in range(4):
                        cs = slice(hh * 128, (hh + 1) * 128)
                        tr.mm(pU[:, cs], Nb[:, hh, :], bv[:, hh, :], True, True, [Nb, bv], [pU])
                        tr.mm(pW[:, cs], bke[:, hh, :], Nb[:, hh, :], True, True, [Nb, bke], [pW])
                    tr.cp("act", fl(u32), pU[:], [pU], [u32])
                    tr.cp("dve", fl(wTb), pW[:], [pW], [wTb])
                    yield
                    pA2 = tr.bank()
                    for hh in range(4):
                        tr.mm(pA2[:, hh * 128:(hh + 1) * 128], wTb[:, hh, :], Sb[:, h0 + hh, :], True, True, [wTb, Sb], [pA2])
                    tr.stt("dve", fl(vnew), pA2[:], -1.0, fl(u32), ALU.mult, ALU.add, [pA2, u32], [vnew])
                    yield
                    pB1 = tr.bank()
                    pB2 = tr.bank()
                    pC2 = tr.bank()
                    for hh in range(4):
                        cs = slice(hh * 128, (hh + 1) * 128)
                        tr.mm(pB1[:, cs], qch[:, kh0 + hh // 2, :], Sb[:, h0 + hh, :], True, True, [qch, Sb], [pB1])
                        tr.mm(pB2[:, cs], Aq[:, hh, :], vnew[:, hh, :], True, True, [Aq, vnew], [pB2])
                        tr.mm(pC2[:, cs], kd[:, hh, :], vnew[:, hh, :], True, True, [kd, vnew], [pC2])
                    tr.tt("dve", o1[:], pB1[:].rearrange("p (a b) -> p a b", b=128), b4(egc[:, h0:h0 + 4]), ALU.mult,
                          [pB1, egc], [o1])
                    tr.tt("pool", stmp[:], S32[:, h0:h0 + 4, :], b4(egc[:, H + h0:H + h0 + 4]), ALU.mult, [S32, egc], [stmp])
                    tr.tt("dve", S32[:, h0:h0 + 4, :], stmp[:], pC2[:].rearrange("p (a b) -> p a b", b=128), ALU.add,
                          [stmp, pC2], [S32])
                    tr.cp("act", Sb[:, h0:h0 + 4, :], S32[:, h0:h0 + 4, :], [S32], [Sb])
                    if not bwd:
                        tr.tt("dve", ostg[:, h0:h0 + 4, :], o1[:], pB2[:].rearrange("p (a b) -> p a b", b=128), ALU.add,
                              [o1, pB2], [ostg])
                        return
                    tr.tt("dve", ot[:], o1[:], pB2[:].rearrange("p (a b) -> p a b", b=128), ALU.add, [o1, pB2], [ot])
                    yield
                    tr.tt("pool", ot[:], ot[:], oft[:, h0:h0 + 4, :], ALU.add, [ot, oft], [ot])
                    tr.act(osq[:], ot[:], AF.Square, [ot], [osq])
                    tr.red("dve", ost[:, 0:4], osq[:], [osq], [ost])
                    tr.act(ost[:, 4:8], ost[:, 0:4], AF.Sqrt, [ost], [ost], bias=RMS_EPS, scale=1.0 / 128)
                    tr.rcp(ost[:, 4:8], ost[:, 4:8], [ost], [ost])
                    tr.tt("dve", ot[:], ot[:], b4(ost[:, 4:8]), ALU.mult, [ot, ost], [ot])
                    tr.tt("pool", ot[:], ot[:], gn[:].unsqueeze(1).to_broadcast([128, 4, 128]), ALU.mult, [ot, gn], [ot])
                    tr.tt("dve", ogb[:], ot[:], szt[:, h0:h0 + 4, :], ALU.mult, [ot, szt], [ogb])
                    yield
                    pT2 = tr.bank()
                    pT2v = pT2[:].bitcast(BF16)
                    for hh in range(4):
                        tr.tp(pT2v[:, hh * 128:(hh + 1) * 128], ogb[:, hh, :], ident[:], [ogb, ident], [pT2])
                    tr.cp("act", ogTt[:, h0:h0 + 4, :].rearrange("p a b -> p (a b)"), pT2v[:, 0:512], [pT2], [ogTt])

                gens = [group(gq, G[gq % NG]) for gq in range(H // 4)]
                live = list(gens)
                while live:
                    nxt = []
                    for g_ in live:
                        try:
                            next(g_)
                            nxt.append(g_)
                        except StopIteration:
                            pass
                    live = nxt
                if bwd:
                    tr.dma("pool", og_view(n // 4)[:, :, (n % 4) * 128:(n % 4 + 1) * 128], ogTt[:], reads=[ogTt])
                else:
                    tr.dma("pool", self.of[n], fl(ostg), reads=[ostg])

    def build(self, depth=4):
        nc = self.nc
        with contextlib.ExitStack() as es:
            tr = self.tr = TR(nc, es)
            AQ = HL * 256
            KW = KHL * 128
            VW = VHL * 128
            for si, T in enumerate(self.seqs):
                self.resnorm(tr, T, self.xin[si], None, self.xs, self.norm_g[0:1, :], False, self.hT)
                for i in range(depth):
                    j = i // 2
                    if i % 2 == 0:
                        w = self.attn_w_in[j]
                        ogv = self.og_loc_blk(si, "a", 2 * HL)
                        self.proj(tr, T, self.src_hT(), 16, w[:, 0:AQ], AQ, "feat", self.ev_feat_bf16(self.qT, 128 ** -0.5))
                        self.proj(tr, T, self.src_hT(), 16, w[:, AQ:2 * AQ], AQ, "feat", self.ev_feat_bf16(self.kT, 1.0))
                        self.proj(tr, T, self.src_hT(), 16, w[:, 2 * AQ:3 * AQ], AQ, "tok", self.ev_v())
                        self.proj(tr, T, self.src_hT(), 16, w[:, 3 * AQ:4 * AQ], AQ, "tok", self.ev_tok_f32(self.sg, 0, 1, AF.Silu))
                        self.attn_core(tr, T, j, 0.8 - 0.6 * math.exp(-0.3 * i), ogv)
                        tr.allgather(list(zip(self.og_loc[si, "a"], self.og_all[si, "a"])))
                        self.proj(tr, T, self.src_gathered(si, "a", 2 * HL), 16, self.attn_w_out[j], 2048, "tok",
                                  self.ev_tok_f32(self.y, 0, 4, None))
                    else:
                        w = self.dn_w_in[j]
                        ogv = self.og_loc_blk(si, "g", VHL)
                        self.proj(tr, T, self.src_hT(), 16, w[:, 0:KW], KW, "feat", self.ev_pc(0))
                        self.proj(tr, T, self.src_hT(), 16, w[:, KW:2 * KW], KW, "feat", self.ev_pc(KHL))
                        self.proj(tr, T, self.src_hT(), 16, w[:, 2 * KW:2 * KW + VW], VW, "feat", self.ev_pc(2 * KHL))
                        self.proj(tr, T, self.src_hT(), 16, w[:, 2 * KW + VW:2 * KW + 2 * VW], VW, "tok",
                                  self.ev_tok_f32(self.sz, 0, VW // 512, AF.Silu))
                        self.proj(tr, T, self.src_hT(), 16, w[:, 2 * KW + 2 * VW:2 * KW + 2 * VW + 4 * VHL], 4 * VHL, "tok",
                                  self.ev_gates(j))
                        import os
                        skip = os.environ.get("DBG_SKIP", "")
                        if "conv" not in skip:
                            self.gdn_conv(tr, T, j)
                        if "scanf" not in skip:
                            self.gdn_scan(tr, T, j, 0, ogv)
                        if "scanb" not in skip:
                            self.gdn_scan(tr, T, j, 1, ogv)
                        tr.allgather(list(zip(self.og_loc[si, "g"], self.og_all[si, "g"])))
                        for c2 in range(2):
                            self.proj(tr, T, self.src_gathered(si, "g", VHL), 32, self.dn_w_out[j][:, c2 * 1024:(c2 + 1) * 1024],
                                      1024, "tok", self.ev_tok_f32(self.y, c2 * 1024, 2, None))
                    last = i == depth - 1
                    if last:
                        self.resnorm(tr, T, self.xs, self.y, None, self.final_norm_g[0:1, :], True, self.yout[si])
                    else:
                        self.resnorm(tr, T, self.xs, self.y, self.xs, self.norm_g[i + 1:i + 2, :], False, self.hT)
        return nc


def core_weights(w, core):
    f = lambda a: np.ascontiguousarray(a, np.float32)
    heads = core_heads(core)
    o = {}
    awi = w["attn_w_in"]
    cols = []
    for base in (0, 2048, 4096, 6144):
        for h in heads:
            cols.append(np.arange(base + h * 256, base + (h + 1) * 256))
    o["attn_w_in"] = f(awi[:, :, np.concatenate(cols)])
    perm = np.concatenate([np.arange(h * 256, (h + 1) * 256) for r in range(NCORE) for h in core_heads(r)])
    o["attn_w_out"] = f(w["attn_w_out"][:, perm, :])
    dwi = w["dn_w_in"]
    kh = np.arange(core * KHL * 128, (core + 1) * KHL * 128)
    vh = np.arange(core * VHL * 128, (core + 1) * VHL * 128)
    ab = np.concatenate([12288 + t * 32 + np.arange(core * VHL, (core + 1) * VHL) for t in range(4)])
    o["dn_w_in"] = f(dwi[:, :, np.concatenate([kh, 2048 + kh, 4096 + vh, 8192 + vh, ab])])
    o["dn_conv_w"] = f(w["dn_conv_w"][:, :, np.concatenate([kh, 2048 + kh, 4096 + vh])])
    for k in ("dn_a_log_fwd", "dn_dt_bias_fwd", "dn_a_log_bwd", "dn_dt_bias_bwd"):
        o[k] = f(w[k][:, core * VHL:(core + 1) * VHL])
    o["dn_w_out"] = f(w["dn_w_out"])
    o["norm_g"] = f(w["norm_g"])
    o["attn_lambda"] = f(w["attn_lambda"]).reshape(2, 512)
    o["attn_subln_g"] = f(w["attn_subln_g"])
    o["dn_norm_g"] = f(w["dn_norm_g"])
    o["final_norm_g"] = f(w["final_norm_g"]).reshape(1, D)
    o["cst"] = make_consts(core)
    return o


def make_in_maps(xs, w):
    maps = []
    for c in range(NCORE):
        m = core_weights(w, c)
        for i, x in enumerate(xs):
            m["x%d" % i] = np.ascontiguousarray(x, np.float32)
        maps.append(m)
    return maps


def kernel(x_prompt, x_sample, **w):
    prog = Prog(SEQS)
    nc = prog.build()
    in_maps = make_in_maps([x_prompt[0], x_sample[0], x_sample[1]], w)
    res = run_bass_kernel_spmd(nc, in_maps, core_ids=list(range(NCORE)))
    y_prompt = np.asarray(res.results[0]["y0"], np.float32)[None]
    y_sample = np.stack([np.asarray(res.results[0]["y1"], np.float32), np.asarray(res.results[0]["y2"], np.float32)], axis=0)
    return (y_prompt, y_sample)
```

```python
import contextlib
import math
import numpy as np
import concourse.bass as bass
import concourse.mybir as mybir
from concourse.bass_utils import run_bass_kernel_spmd

F32 = mybir.dt.float32
BF16 = mybir.dt.bfloat16
AF = mybir.ActivationFunctionType
ALU = mybir.AluOpType
AX = mybir.AxisListType

D = 2048
SEQS = (16384, 4096, 4096)
NCORE = 4
HL = 2
KHL = 4
VHL = 8
ATT_W = (1280, 1 << 30)
NSLOT = 8
SAME_ENGINE_SYNC = True
INV_F32 = True
ATT_THRESH = 80.0
VA = 264
RMS_EPS = 1e-6
L2_EPS = 1e-6
NEG = -30000.0

C_IDENT, C_ONES, C_TRIF, C_TRIB, C_NTRIF, C_NTRIB, C_MBF, C_MBB, C_MBSF, C_MBSB = [i * 128 for i in range(10)]
C_J = 1280
C_JABS = C_J + 256
C_CB = C_JABS + 512
NCB = 132
C_SLP = C_CB + HL * NCB
C_END = C_SLP + 4


def core_heads(core):
    return (core, 7 - core)


def make_consts(core):
    c = np.zeros((128, C_END), np.float32)
    p = np.arange(128)[:, None].astype(np.float64)
    i = np.arange(128)[None, :].astype(np.float64)
    c[:, C_IDENT:C_IDENT + 128] = (p == i)
    c[:, C_ONES:C_ONES + 128] = 1.0
    c[:, C_TRIF:C_TRIF + 128] = (p <= i)
    c[:, C_TRIB:C_TRIB + 128] = (p >= i)
    c[:, C_NTRIF:C_NTRIF + 128] = -1.0 * (p <= i)
    c[:, C_NTRIB:C_NTRIB + 128] = -1.0 * (p >= i)
    c[:, C_MBF:C_MBF + 128] = np.where(i >= p, 0.0, NEG)
    c[:, C_MBB:C_MBB + 128] = np.where(i <= p, 0.0, NEG)
    c[:, C_MBSF:C_MBSF + 128] = np.where(p > i, 0.0, NEG)
    c[:, C_MBSB:C_MBSB + 128] = np.where(p < i, 0.0, NEG)
    j = np.arange(256)[None, :].astype(np.float64)
    c[:, C_J:C_J + 256] = j - p
    c[:, C_JABS:C_JABS + 256] = np.abs(j - p)
    c[:, C_JABS + 256:C_JABS + 512] = np.abs(j - p - 128)
    for sl, h in enumerate(core_heads(core)):
        m = 2.0 ** (-(h + 1))
        c[:, C_CB + sl * NCB:C_CB + (sl + 1) * NCB] = -m * 128.0 * np.arange(NCB)[None, :]
        c[:, C_SLP + 2 * sl] = -m
        c[:, C_SLP + 2 * sl + 1] = m
    return c


class Sem:
    __slots__ = ("h", "v")

    def __init__(self, nc, es, name):
        self.h = es.enter_context(nc.semaphore(name))
        self.v = 0


class Buf:
    __slots__ = ("w", "r", "t")

    def __init__(self, t):
        self.w = None
        self.r = {}
        self.t = t

    def __getitem__(self, idx):
        return self.t[idx]


class TR:
    def __init__(self, nc, es):
        self.nc = nc
        self.es = es
        self.pes = None
        self.engs = ("pe", "act", "dve", "pool", "sp")
        self.sem = {e: Sem(nc, es, "s_" + e) for e in ("pe", "act", "dve", "pool")}
        self.slots = {q: [Sem(nc, es, "d_%s%d" % (q, i)) for i in range(NSLOT)] for q in ("sp", "act", "pool")}
        self.slot_i = {q: 0 for q in self.slots}
        self.known = {e: {} for e in self.engs}
        self.q = {e: [] for e in self.engs}
        self.nins = 0
        self.pend = []
        self.ccsem = None
        self.P = [Buf(es.enter_context(nc.psum_tensor("P%d" % i, [128, 512], F32))) for i in range(8)]
        self.pi = 0

    def bank(self):
        b = self.P[self.pi]
        self.pi = (self.pi + 1) % 8
        return b

    @contextlib.contextmanager
    def phase(self):
        with contextlib.ExitStack() as pes:
            self.pes = pes
            yield
            self.barrier()
            self.emit()
            self.pes = None

    def emit(self):
        with self.nc.Block() as block:
            for e, sect in (("sp", block.sync), ("pe", block.tensor), ("act", block.scalar),
                            ("dve", block.vector), ("pool", block.gpsimd)):
                lst = self.q[e]
                if not lst:
                    continue

                def body(eng, lst=lst):
                    for f in lst:
                        f(eng)
                sect(body)
                self.nins += len(lst)
        self.q = {e: [] for e in self.engs}

    def sb(self, name, shape, dt):
        es = self.pes if self.pes is not None else self.es
        self.uid = getattr(self, "uid", 0) + 1
        return Buf(es.enter_context(self.nc.sbuf_tensor("%s_%d" % (name, self.uid), shape, dt)))

    def wait(self, e, so, val):
        k = self.known[e]
        if k.get(so, 0) >= val:
            return
        self.pend.append((so.h, val))
        k[so] = val

    def flush(self, e, keep_last):
        p = self.pend
        self.pend = []
        last = None
        if keep_last and p:
            last = p.pop()
        for h, val in p:
            self.q[e].append(lambda eng, h=h, val=val: eng.wait_ge(h, val))
        return last

    def _deps(self, e, reads, writes, own):
        same = SAME_ENGINE_SYNC and e != "pe"
        for b in reads:
            if b.w is not None:
                so, v = b.w
                if so is own and not same:
                    continue
                self.wait(e, so, v)
        for b in writes:
            if b.w is not None:
                so, v = b.w
                if not (so is own and not same):
                    self.wait(e, so, v)
            for so, v in b.r.items():
                if so is own:
                    continue
                self.wait(e, so, v)

    def op(self, e, fn, reads=(), writes=()):
        own = self.sem[e]
        self._deps(e, reads, writes, own)
        lw = self.flush(e, True)
        own.v += 1
        h = own.h
        if lw is None:
            self.q[e].append(lambda eng, fn=fn, h=h: fn(eng).then_inc(h, 1))
        else:
            self.q[e].append(lambda eng, fn=fn, h=h, wh=lw[0], wv=lw[1]: _winc(fn(eng), wh, wv, h, 1))
        for b in reads:
            b.r[own] = own.v
        for b in writes:
            b.w = (own, own.v)
            b.r = {}

    def dma(self, q, out, in_, reads=(), writes=()):
        sl = self.slots[q]
        i = self.slot_i[q]
        self.slot_i[q] = (i + 1) % NSLOT
        so = sl[i]
        self._deps(q, reads, writes, None)
        self.wait(q, so, so.v)
        lw = self.flush(q, True)
        so.v += 16
        h = so.h
        if lw is None:
            self.q[q].append(lambda eng, o=out, i_=in_, h=h: eng.dma_start(out=o, in_=i_).then_inc(h, 16))
        else:
            self.q[q].append(lambda eng, o=out, i_=in_, h=h, wh=lw[0], wv=lw[1]:
                             _winc(eng.dma_start(out=o, in_=i_), wh, wv, h, 16))
        for b in reads:
            b.r[so] = so.v
        for b in writes:
            b.w = (so, so.v)
            b.r = {}

    def barrier(self):
        allsems = list(self.sem.values()) + [s for sl in self.slots.values() for s in sl]
        for e in self.engs:
            for so in allsems:
                if so.v > 0:
                    self.wait(e, so, so.v)
            self.flush(e, False)

    def allgather(self, pairs):
        if self.ccsem is None:
            self.ccsem = Sem(self.nc, self.es, "ccsem")
        so = self.ccsem
        h = so.h
        for src, dst in pairs:
            so.v += 1
            self.q["pool"].append(lambda eng, h=h, s_=src, d_=dst: eng.collective_compute(
                "AllGather", ALU.bypass, replica_groups=[list(range(NCORE))],
                ins=[s_.ap().opt()], outs=[d_.ap().opt()]).then_inc(h, 1))
        for e in self.engs:
            self.wait(e, so, so.v)
            self.flush(e, False)
        self.emit()

    def mm(self, out, lhsT, rhs, start, stop, reads, writes):
        self.op("pe", lambda e, o=out, l=lhsT, r=rhs, s=start, t=stop: e.matmul(o, lhsT=l, rhs=r, start=s, stop=t),
                reads, writes)

    def tp(self, out, in_, ident, reads, writes):
        self.op("pe", lambda e, o=out, i=in_, d=ident: e.transpose(o, i, d), reads, writes)

    def act(self, out, in_, func, reads, writes, bias=0.0, scale=1.0):
        self.op("act", lambda e, o=out, i=in_, f=func, b=bias, s=scale: e.activation(out=o, in_=i, func=f, bias=b, scale=s),
                reads, writes)

    def tt(self, eng, out, in0, in1, op, reads, writes):
        self.op(eng, lambda e, o=out, a=in0, b=in1, p=op: e.tensor_tensor(out=o, in0=a, in1=b, op=p), reads, writes)

    def ts(self, eng, out, in0, s1, s2, op0, op1, reads, writes):
        if s2 is None:
            self.op(eng, lambda e, o=out, a=in0, s=s1, p=op0: e.tensor_scalar(out=o, in0=a, scalar1=s, scalar2=None, op0=p),
                    reads, writes)
        else:
            self.op(eng, lambda e, o=out, a=in0, s=s1, u=s2, p=op0, q=op1:
                    e.tensor_scalar(out=o, in0=a, scalar1=s, scalar2=u, op0=p, op1=q), reads, writes)

    def stt(self, eng, out, in0, scalar, in1, op0, op1, reads, writes):
        self.op(eng, lambda e, o=out, a=in0, s=scalar, b=in1, p=op0, q=op1:
                e.scalar_tensor_tensor(out=o, in0=a, scalar=s, in1=b, op0=p, op1=q), reads, writes)

    def cp(self, eng, out, in_, reads, writes):
        if eng == "act":
            self.op("act", lambda e, o=out, i=in_: e.copy(out=o, in_=i), reads, writes)
        else:
            self.op(eng, lambda e, o=out, i=in_: e.tensor_copy(out=o, in_=i), reads, writes)

    def red(self, eng, out, in_, reads, writes):
        self.op(eng, lambda e, o=out, i=in_: e.reduce_sum(out=o, in_=i, axis=AX.X), reads, writes)

    def rcp(self, out, in_, reads, writes):
        self.op("dve", lambda e, o=out, i=in_: e.reciprocal(out=o, in_=i), reads, writes)

    def ms(self, eng, ap, val, writes):
        self.op(eng, lambda e, a=ap, v=val: e.memset(a, v), (), writes)


def _winc(ins, wh, wv, h, inc):
    ins.wait_op(wh, wv, "sem-ge")
    return ins.then_inc(h, inc)


def bc_rows(ap2d):
    return ap2d.partition_broadcast(128).rearrange("p a b -> p (a b)")


class Prog:
    def __init__(self, seqs, dbg=()):
        self.seqs = seqs
        self.dbg = set(dbg)
        nc = self.nc = bass.Bass("TRN2", target_bir_lowering=False)
        Tm = self.Tm = max(seqs)
        nb, nt = Tm // 512, Tm // 128
        ei = lambda n, s: nc.dram_tensor(n, s, F32, kind="ExternalInput")
        self.xin = [ei("x%d" % i, [T, D]) for i, T in enumerate(seqs)]
        self.yout = [nc.dram_tensor("y%d" % i, [T, D], F32, kind="ExternalOutput") for i, T in enumerate(seqs)]
        self.norm_g = ei("norm_g", [4, D])
        self.attn_w_in = ei("attn_w_in", [2, D, 8 * HL * 128])
        self.attn_lambda = ei("attn_lambda", [2, 512])
        self.attn_subln_g = ei("attn_subln_g", [2, 256])
        self.attn_w_out = ei("attn_w_out", [2, 2048, 2048])
        self.dn_w_in = ei("dn_w_in", [2, D, 2 * KHL * 128 + 2 * VHL * 128 + 4 * VHL])
        self.dn_conv_w = ei("dn_conv_w", [2, 5, 2 * KHL * 128 + VHL * 128])
        self.dn_a_log_fwd = ei("dn_a_log_fwd", [2, VHL])
        self.dn_dt_bias_fwd = ei("dn_dt_bias_fwd", [2, VHL])
        self.dn_a_log_bwd = ei("dn_a_log_bwd", [2, VHL])
        self.dn_dt_bias_bwd = ei("dn_dt_bias_bwd", [2, VHL])
        self.dn_norm_g = ei("dn_norm_g", [2, 128])
        self.dn_w_out = ei("dn_w_out", [2, 4096, 2048])
        self.final_norm_g = ei("final_norm_g", [1, D])
        self.cst = ei("cst", [128, C_END])

        def scr(n, s, dt):
            return nc.dram_tensor(n, s, dt, kind=("ExternalOutput" if n in self.dbg else "Internal"))
        self.xs = scr("xs", [Tm, D], F32)
        self.hT = scr("hT", [nb, 128, 16, 512], BF16)
        self.y = scr("ysc", [nt, 128, 2048], F32)
        self.qT = scr("qT", [2 * HL, nb, 128, 512], BF16)
        self.kT = scr("kT", [2 * HL, nb, 128, 512], BF16)
        self.va = scr("va", [nt, 128, HL, VA], BF16)
        self.sg = scr("sg", [nt, 128, HL * 256], F32)
        self.pcs = scr("pcs", [2 * KHL + VHL, 128, Tm], F32)
        self.qn = scr("qn", [nt, 128, KHL, 128], BF16)
        self.kn = scr("kn", [nt, 128, KHL, 128], BF16)
        self.ktok = scr("ktok", [nt, 128, KHL * 128], BF16)
        self.vtok = scr("vtok", [nt, 128, VHL * 128], BF16)
        self.sz = scr("sz", [nt, 128, VHL * 128], F32)
        self.gb = scr("gb", [nt, 128, 4 * VHL], F32)
        self.of = scr("of", [nt, 128, VHL * 128], F32)
        self.og_loc = {}
        self.og_all = {}
        self.og_bpc = {"a": max(1, 1024 // (128 * 2 * HL)), "g": max(1, 1024 // (128 * VHL))}
        for si, T in enumerate(seqs):
            nb_ = T // 512
            for kind, kl in (("a", 2 * HL), ("g", VHL)):
                bpc = self.og_bpc[kind]
                nch = -(-nb_ // bpc)
                szs = [min(bpc, nb_ - c * bpc) * 128 * kl for c in range(nch)]
                self.og_loc[si, kind] = [scr("ogl_%d%s%d" % (si, kind, c), [szs[c], 512], BF16) for c in range(nch)]
                self.og_all[si, kind] = [scr("oga_%d%s%d" % (si, kind, c), [NCORE * szs[c], 512], BF16) for c in range(nch)]

    def og_loc_blk(self, si, kind, kl):
        bpc = self.og_bpc[kind]

        def f(b):
            c, bl = divmod(b, bpc)
            return self.og_loc[si, kind][c].ap().rearrange("(b p k) t -> b p k t", p=128, k=kl)[bl]
        return f

    def og_all_blk(self, si, kind, kl):
        bpc = self.og_bpc[kind]

        def f(r, b):
            c, bl = divmod(b, bpc)
            return self.og_all[si, kind][c].ap().rearrange("(r b p k) t -> r b p k t", r=NCORE, p=128, k=kl)[r, bl]
        return f

    def load_const_bf16(self, tr, name, col, n=128):
        st = tr.sb(name + "_f", [128, n], F32)
        tr.dma("sp", st[:], self.cst[:, col:col + n], writes=[st])
        b = tr.sb(name, [128, n], BF16)
        tr.cp("dve", b[:], st[:], [st], [b])
        return b

    def load_const_f32(self, tr, name, col, n=128):
        st = tr.sb(name, [128, n], F32)
        tr.dma("sp", st[:], self.cst[:, col:col + n], writes=[st])
        return st

    def resnorm(self, tr, T, x_src, y_src, x_dst, g_row, final, out):
        nt = T // 128
        with tr.phase():
            gt = tr.sb("gt", [128, D], F32)
            tr.dma("sp", gt[:], bc_rows(g_row), writes=[gt])
            ident = self.load_const_bf16(tr, "ident", C_IDENT)
            xts = [tr.sb("xt%d" % i, [128, D], F32) for i in range(2)]
            yts = [tr.sb("yt%d" % i, [128, D], F32) for i in range(2)]
            sq = tr.sb("sq", [128, D], F32)
            ssq = tr.sb("ssq", [128, 2], F32)
            hbs = [tr.sb("hb%d" % i, [128, D], BF16) for i in range(2)]
            hfs = [tr.sb("hf%d" % i, [128, D], F32) for i in range(2)] if final else None
            hTt = [tr.sb("hTt%d" % i, [128, 16, 512], BF16) for i in range(2)]
            for tt in range(nt):
                xt = xts[tt % 2]
                tr.dma("sp", xt[:], x_src[tt * 128:(tt + 1) * 128, :], writes=[xt])
                if y_src is not None:
                    yt = yts[tt % 2]
                    tr.dma("sp", yt[:], y_src[tt], writes=[yt])
                    tr.tt("dve", xt[:], xt[:], yt[:], ALU.add, [xt, yt], [xt])
                if x_dst is not None:
                    tr.dma("pool", x_dst[tt * 128:(tt + 1) * 128, :], xt[:], reads=[xt])
                tr.act(sq[:], xt[:], AF.Square, [xt], [sq])
                tr.red("dve", ssq[:, 0:1], sq[:], [sq], [ssq])
                tr.act(ssq[:, 1:2], ssq[:, 0:1], AF.Sqrt, [ssq], [ssq], bias=RMS_EPS, scale=1.0 / D)
                tr.rcp(ssq[:, 1:2], ssq[:, 1:2], [ssq], [ssq])
                if final:
                    hf = hfs[tt % 2]
                    tr.stt("dve", hf[:], xt[:], ssq[:, 1:2], gt[:], ALU.mult, ALU.mult, [xt, ssq, gt], [hf])
                    tr.dma("pool", out[tt * 128:(tt + 1) * 128, :], hf[:], reads=[hf])
                    continue
                hb = hbs[tt % 2]
                tr.stt("dve", hb[:], xt[:], ssq[:, 1:2], gt[:], ALU.mult, ALU.mult, [xt, ssq, gt], [hb])
                b, s = tt // 4, tt % 4
                ht = hTt[b % 2]
                for half in range(2):
                    pb = tr.bank()
                    pv = pb[:].bitcast(BF16)
                    for k8 in range(8):
                        kc = half * 8 + k8
                        tr.tp(pv[:, k8 * 128:(k8 + 1) * 128], hb[:, kc * 128:(kc + 1) * 128], ident[:], [hb, ident], [pb])
                    tr.cp("act", ht[:, half * 8:(half + 1) * 8, s * 128:(s + 1) * 128],
                          pv.rearrange("p (a b) -> p a b", b=128), [pb], [ht])
                if s == 3:
                    tr.dma("pool", out[b], ht[:], reads=[ht])

    def proj(self, tr, T, src_loader, nk, w_ap, wc, mode, evac_factory):
        nb = T // 512
        with tr.phase():
            wt = tr.sb("wt", [128, nk, wc], BF16)
            stg = [tr.sb("wstg%d" % i, [128, wc], F32) for i in range(2)]
            for kc in range(nk):
                st = stg[kc % 2]
                tr.dma("sp", st[:], w_ap[kc * 128:(kc + 1) * 128, :], writes=[st])
                if kc % 2:
                    tr.cp("act", wt[:, kc, :], st[:], [st], [wt])
                else:
                    tr.cp("dve", wt[:, kc, :], st[:], [st], [wt])
            sbt = [tr.sb("psrc%d" % i, [128, nk, 512], BF16) for i in range(2)]
            evac = evac_factory(tr)
            for b in range(nb):
                s = sbt[b % 2]
                src_loader(tr, b, s)
                if mode == "feat":
                    for ct in range(wc // 128):
                        pb = tr.bank()
                        for kc in range(nk):
                            tr.mm(pb[:], wt[:, kc, ct * 128:(ct + 1) * 128], s[:, kc, :], kc == 0, kc == nk - 1, [wt, s], [pb])
                        evac(b, ct, pb)
                else:
                    n = min(wc, 512)
                    for sub in range(4):
                        for cg in range(max(1, wc // 512)):
                            pb = tr.bank()
                            for kc in range(nk):
                                tr.mm(pb[:, 0:n], s[:, kc, sub * 128:(sub + 1) * 128], wt[:, kc, cg * n:(cg + 1) * n],
                                      kc == 0, kc == nk - 1, [wt, s], [pb])
                            evac(b, sub, cg, pb)

    def src_hT(self):
        def ld(tr, b, s):
            tr.dma("sp", s[:], self.hT[b], writes=[s])
        return ld

    def src_gathered(self, si, kind, kl):
        v = self.og_all_blk(si, kind, kl)

        def ld(tr, b, s):
            for r in range(NCORE):
                tr.dma("sp", s[:, r * kl:(r + 1) * kl, :], v(r, b), writes=[s])
        return ld

    def ev_feat_bf16(self, dst, scale):
        def fac(tr):
            stg = [tr.sb("evq%d" % i, [128, 512], BF16) for i in range(3)]
            cnt = [0]

            def ev(b, ct, pb):
                st = stg[cnt[0] % 3]
                cnt[0] += 1
                if cnt[0] % 2:
                    tr.act(st[:], pb[:], AF.Copy, [pb], [st], scale=scale)
                else:
                    tr.ts("dve", st[:], pb[:], scale, None, ALU.mult, None, [pb], [st])
                tr.dma("pool", dst[ct][b], st[:], reads=[st])
            return ev
        return fac

    def ev_v(self):
        def fac(tr):
            stg = [tr.sb("evv%d" % i, [128, HL, VA], BF16) for i in range(2)]
            for st in stg:
                tr.ms("dve", st[:], 1.0, [st])

            def ev(b, sub, cg, pb):
                tt = b * 4 + sub
                st = stg[tt % 2]
                src = pb[:].rearrange("p (h e) -> p h e", e=256)
                tr.cp("act", st[:, 0:HL, 0:256], src, [pb], [st])
                tr.dma("pool", self.va[tt], st[:], reads=[st])
            return ev
        return fac

    def ev_tok_f32(self, dst, col0, ncg, func, wcg=512):
        def fac(tr):
            stg = [tr.sb("evt%d" % i, [128, ncg * wcg], F32) for i in range(2)]

            def ev(b, sub, cg, pb):
                tt = b * 4 + sub
                st = stg[tt % 2]
                if func is not None:
                    tr.act(st[:, cg * wcg:(cg + 1) * wcg], pb[:, 0:wcg], func, [pb], [st])
                elif cg % 2:
                    tr.cp("act", st[:, cg * wcg:(cg + 1) * wcg], pb[:, 0:wcg], [pb], [st])
                else:
                    tr.cp("dve", st[:, cg * wcg:(cg + 1) * wcg], pb[:, 0:wcg], [pb], [st])
                if cg == ncg - 1:
                    tr.dma("pool", dst[tt][:, col0:col0 + ncg * wcg], st[:], reads=[st])
            return ev
        return fac

    def ev_pc(self, ct0):
        def fac(tr):
            stg = [tr.sb("evp%d" % i, [128, 512], F32) for i in range(3)]
            cnt = [0]

            def ev(b, ct, pb):
                st = stg[cnt[0] % 3]
                cnt[0] += 1
                tr.cp("act" if cnt[0] % 2 else "dve", st[:], pb[:], [pb], [st])
                tr.dma("pool", self.pcs[ct0 + ct][:, b * 512:(b + 1) * 512], st[:], reads=[st])
            return ev
        return fac

    def ev_gates(self, j):
        H = VHL

        def fac(tr):
            dtb = tr.sb("dtb", [128, 2, H], F32)
            nA = tr.sb("nA", [128, 2, H], F32)
            tr.dma("sp", dtb[:, 0, :], bc_rows(self.dn_dt_bias_fwd[j:j + 1, :]), writes=[dtb])
            tr.dma("sp", dtb[:, 1, :], bc_rows(self.dn_dt_bias_bwd[j:j + 1, :]), writes=[dtb])
            tr.dma("sp", nA[:, 0, :], bc_rows(self.dn_a_log_fwd[j:j + 1, :]), writes=[nA])
            tr.dma("sp", nA[:, 1, :], bc_rows(self.dn_a_log_bwd[j:j + 1, :]), writes=[nA])
            tr.act(nA[:], nA[:], AF.Exp, [nA], [nA])
            tr.ts("dve", nA[:], nA[:], -1.0, None, ALU.mult, None, [nA], [nA])
            xa = tr.sb("g_xa", [128, 2, H], F32)
            ax = tr.sb("g_ax", [128, 2, H], F32)
            outs = [tr.sb("g_out%d" % i, [128, 4, H], F32) for i in range(2)]

            def ev(b, sub, cg, pb):
                tt = b * 4 + sub
                o = outs[tt % 2]
                pv = pb[:, 0:4 * H].rearrange("p (a b) -> p a b", b=H)
                for d in range(2):
                    tr.tt("dve", xa[:, d, :], pv[:, 2 * d, :], dtb[:, d, :], ALU.add, [pb, dtb], [xa])
                tr.act(ax[:], xa[:], AF.Abs, [xa], [ax])
                tr.act(ax[:], ax[:], AF.Exp, [ax], [ax], scale=-1.0)
                tr.act(ax[:], ax[:], AF.Ln, [ax], [ax], bias=1.0)
                tr.ts("dve", xa[:], xa[:], 0.0, None, ALU.max, None, [xa], [xa])
                tr.tt("dve", xa[:], xa[:], ax[:], ALU.add, [xa, ax], [xa])
                for d in range(2):
                    tr.tt("dve", o[:, 2 * d, :], xa[:, d, :], nA[:, d, :], ALU.mult, [xa, nA], [o])
                    tr.act(o[:, 2 * d + 1, :], pv[:, 2 * d + 1, :], AF.Sigmoid, [pb], [o])
                tr.dma("pool", self.gb[tt], o[:].rearrange("p a b -> p (a b)"), reads=[o])
            return ev
        return fac

    def attn_core(self, tr, T, j, lam_init, og_view):
        nt = T // 128
        nq = T // 256
        with tr.phase():
            ident = self.load_const_bf16(tr, "ident", C_IDENT)
            Jt = self.load_const_f32(tr, "Jt", C_J, 256)
            Ja = self.load_const_f32(tr, "Ja", C_JABS, 512)
            cb = self.load_const_f32(tr, "cb", C_CB, HL * NCB)
            slp = self.load_const_f32(tr, "slp", C_SLP, 4)
            lv = tr.sb("lv", [128, 512], F32)
            tr.dma("sp", lv[:], bc_rows(self.attn_lambda[j:j + 1, :]), writes=[lv])
            lp = tr.sb("lp", [128, 256], F32)
            ls = tr.sb("ls", [128, 4], F32)
            tr.tt("dve", lp[:, 0:128], lv[:, 0:128], lv[:, 128:256], ALU.mult, [lv], [lp])
            tr.tt("dve", lp[:, 128:256], lv[:, 256:384], lv[:, 384:512], ALU.mult, [lv], [lp])
            tr.red("dve", ls[:, 0:2], lp[:].rearrange("p (a b) -> p a b", b=128), [lp], [ls])
            tr.act(ls[:, 0:2], ls[:, 0:2], AF.Exp, [ls], [ls])
            tr.tt("dve", ls[:, 2:3], ls[:, 1:2], ls[:, 0:1], ALU.subtract, [ls], [ls])
            tr.ts("dve", ls[:, 3:4], ls[:, 2:3], -lam_init, None, ALU.add, None, [ls], [ls])
            sgn = tr.sb("sgn", [128, 256], F32)
            tr.dma("sp", sgn[:], bc_rows(self.attn_subln_g[j:j + 1, :]), writes=[sgn])
            tr.ts("dve", sgn[:], sgn[:], 1.0 - lam_init, None, ALU.mult, None, [sgn], [sgn])

            qts = [tr.sb("aq%d" % i, [128, 2, 256], BF16) for i in range(2)]
            sgts = [tr.sb("asg%d" % i, [128, 2, 256], F32) for i in range(2)]
            kts = [tr.sb("ak%d" % i, [128, 2, 512], BF16) for i in range(3)]
            vts = [tr.sb("av%d" % i, [128, 4, VA], BF16) for i in range(3)]
            tbs = [tr.sb("atb%d" % i, [128, 512], F32) for i in range(3)]
            pbs = [tr.sb("apb%d" % i, [128, 512], BF16) for i in range(3)]
            om = tr.sb("aom", [128, 2, 2, 256], F32)
            rden = tr.sb("arden", [128, 4], F32)
            oc = tr.sb("aoc", [128, 2, 256], F32)
            osq = tr.sb("aosq", [128, 2, 256], F32)
            ost = tr.sb("aost", [128, 4], F32)
            ogb = tr.sb("aogb", [128, 2, 256], BF16)
            ogTt = [tr.sb("aogT%d" % i, [128, 2, 256], BF16) for i in range(2)]
            acc = tr.P[0:4]
            sc = tr.P[4:7]
            tpb = tr.P[7:8]
            nblk = 0
            it = 0
            for h in range(HL):
                W = ATT_W[h]
                mneg = slp[:, 2 * h:2 * h + 1]
                mpos = slp[:, 2 * h + 1:2 * h + 2]
                for qt in range(nq):
                    q0 = qt * 256
                    qtile = qts[it % 2]
                    sgt = sgts[it % 2]
                    ogt = ogTt[it % 2]
                    it += 1
                    qb, qo = qt // 2, (qt % 2) * 256
                    for mp in range(2):
                        tr.dma("sp", qtile[:, mp, :], self.qT[2 * h + mp][qb][:, qo:qo + 256], writes=[qtile])
                    tr.dma("sp", sgt[:], self.sg[qt * 2:qt * 2 + 2, :, h * 256:(h + 1) * 256].rearrange("s p e -> p s e"),
                           writes=[sgt])
                    kt_lo = max(0, (q0 - W) // 128)
                    kt_hi = min(nt, -((-(q0 + 256 + W)) // 128))
                    kts_list = list(range(kt_lo, kt_hi))
                    nk_ = len(kts_list)
                    info = {}
                    cur = {"kb": -1, "kt": None, "vt": None}

                    def issue_qk(idx):
                        nonlocal nblk
                        kt = kts_list[idx]
                        kb, ks = kt // 4, kt % 4
                        if kb != cur["kb"]:
                            cur["kb"] = kb
                            cur["kt"] = kts[nblk % 3]
                            cur["vt"] = vts[nblk % 3]
                            nblk += 1
                            for mp in range(2):
                                tr.dma("sp", cur["kt"][:, mp, :], self.kT[2 * h + mp][kb], writes=[cur["kt"]])
                            tr.dma("sp", cur["vt"][:], self.va[kb * 4:kb * 4 + 4, :, h, :].rearrange("s p e -> p s e"),
                                   writes=[cur["vt"]])
                        ktile, vtile = cur["kt"], cur["vt"]
                        sb_ = sc[idx % 3]
                        for mp in range(2):
                            tr.mm(sb_[:, mp * 256:(mp + 1) * 256], ktile[:, mp, ks * 128:(ks + 1) * 128], qtile[:, mp, :],
                                  True, True, [ktile, qtile], [sb_])
                        info[idx] = (vtile, ks, kt * 128, sb_)

                    def issue_soft(idx):
                        vtile, ks, k0, sb_ = info[idx]
                        tb = tbs[idx % 3]
                        pb = pbs[idx % 3]
                        sv = sb_[:].rearrange("p (a b) -> p a b", b=256)
                        tv = tb[:].rearrange("p (a b) -> p a b", b=256)
                        if k0 + 127 < q0:
                            jv = Jt[:].unsqueeze(1).to_broadcast([128, 2, 256])
                            tr.stt("dve", tv, jv, mneg, sv, ALU.mult, ALU.add, [Jt, sb_, slp], [tb])
                            ci = (q0 - k0) // 128
                        elif k0 > q0 + 255:
                            jv = Jt[:].unsqueeze(1).to_broadcast([128, 2, 256])
                            tr.stt("dve", tv, jv, mpos, sv, ALU.mult, ALU.add, [Jt, sb_, slp], [tb])
                            ci = (k0 - q0) // 128
                        else:
                            i_ = (k0 - q0) // 128
                            jv = Ja[:, i_ * 256:(i_ + 1) * 256].unsqueeze(1).to_broadcast([128, 2, 256])
                            tr.stt("dve", tv, jv, mneg, sv, ALU.mult, ALU.add, [Ja, sb_, slp], [tb])
                            ci = 0
                        tr.act(pb[:], tb[:], AF.Exp, [tb, cb], [pb], bias=cb[:, h * NCB + ci:h * NCB + ci + 1])

                    def issue_av(idx):
                        vtile, ks, k0, sb_ = info.pop(idx)
                        pb = pbs[idx % 3]
                        first, last = idx == 0, idx == nk_ - 1
                        for mp in range(2):
                            for qs in range(2):
                                a = acc[mp * 2 + qs]
                                tr.mm(a[:, 0:257], pb[:, mp * 256 + qs * 128:mp * 256 + (qs + 1) * 128], vtile[:, ks, 0:257],
                                      first, last, [pb, vtile], [a])

                    LOOK = 2
                    for idx in range(min(LOOK, nk_)):
                        issue_qk(idx)
                    for idx in range(nk_):
                        issue_soft(idx)
                        if idx + LOOK < nk_:
                            issue_qk(idx + LOOK)
                        issue_av(idx)
                    for mp in range(2):
                        for qs in range(2):
                            a = acc[mp * 2 + qs]
                            c_ = mp * 2 + qs
                            tr.rcp(rden[:, c_:c_ + 1], a[:, 256:257], [a], [rden])
                            if qs:
                                tr.act(om[:, mp, qs, :], a[:, 0:256], AF.Copy, [a, rden], [om], scale=rden[:, c_:c_ + 1])
                            else:
                                tr.ts("dve", om[:, mp, qs, :], a[:, 0:256], rden[:, c_:c_ + 1], None, ALU.mult, None, [a, rden], [om])
                    tr.stt("dve", oc[:], om[:, 1, :, :], ls[:, 3:4], om[:, 0, :, :], ALU.mult, ALU.add, [om, ls], [oc])
                    tr.act(osq[:], oc[:], AF.Square, [oc], [osq])
                    tr.red("dve", ost[:, 0:2], osq[:], [osq], [ost])
                    tr.act(ost[:, 2:4], ost[:, 0:2], AF.Sqrt, [ost], [ost], bias=RMS_EPS, scale=1.0 / 256)
                    tr.rcp(ost[:, 2:4], ost[:, 2:4], [ost], [ost])
                    tr.tt("dve", oc[:], oc[:], ost[:, 2:4].unsqueeze(2).to_broadcast([128, 2, 256]), ALU.mult, [oc, ost], [oc])
                    tr.tt("pool", oc[:], oc[:], sgn[:].unsqueeze(1).to_broadcast([128, 2, 256]), ALU.mult, [oc, sgn], [oc])
                    tr.tt("dve", ogb[:], oc[:], sgt[:], ALU.mult, [oc, sgt], [ogb])
                    tb_ = tpb[0]
                    tv_ = tb_[:].bitcast(BF16)
                    for ec in range(2):
                        for qs in range(2):
                            c_ = (ec * 2 + qs) * 128
                            tr.tp(tv_[:, c_:c_ + 128], ogb[:, qs, ec * 128:(ec + 1) * 128], ident[:], [ogb, ident], [tb_])
                    tr.cp("act", ogt[:], tv_[:, 0:512].rearrange("p (a b) -> p a b", b=256), [tb_], [ogt])
                    tr.dma("pool", og_view(qb)[:, 2 * h:2 * h + 2, qo:qo + 256], ogt[:], reads=[ogt])

    def gdn_conv(self, tr, T, j):
        nb = T // 512
        NCT = 2 * KHL + VHL
        with tr.phase():
            ident = self.load_const_bf16(tr, "ident", C_IDENT)
            identf = self.load_const_f32(tr, "identf", C_IDENT)
            onesf = self.load_const_f32(tr, "onesf", C_ONES)
            cwr = tr.sb("cwr", [5, NCT * 128], F32)
            tr.dma("sp", cwr[:], self.dn_conv_w[j], writes=[cwr])
            cw = tr.sb("cw", [128, NCT, 5], F32)
            for ct in range(NCT):
                pb = tr.bank()
                tr.tp(pb[:, 0:5], cwr[0:5, ct * 128:(ct + 1) * 128], identf[0:5, 0:5], [cwr, identf], [pb])
                tr.cp("dve", cw[:, ct, :], pb[:, 0:5], [pb], [cw])
            xins = [tr.sb("cx%d" % i, [128, 516], F32) for i in range(3)]
            accs = [tr.sb("ca%d" % i, [128, 512], F32) for i in range(2)]
            sxs = [tr.sb("cs%d" % i, [128, 512], F32) for i in range(2)]
            sqs = [tr.sb("cq%d" % i, [128, 512], F32) for i in range(2)]
            rns = [tr.sb("cr%d" % i, [128, 512], F32) for i in range(2)]
            xnbs = [tr.sb("cn%d" % i, [128, 512], BF16) for i in range(2)]
            tks = [tr.sb("ct%d" % i, [128, 4, 128], BF16) for i in range(2)]
            it = 0
            for ct in range(NCT):
                for b in range(nb):
                    xin = xins[it % 3]
                    acc = accs[it % 2]
                    sx = sxs[it % 2]
                    sq = sqs[it % 2]
                    rn = rns[it % 2]
                    xnb = xnbs[it % 2]
                    tk = tks[it % 2]
                    it += 1
                    lo = b * 512 - 2
                    hi = b * 512 + 514
                    c0, c1 = 0, 516
                    if b == 0:
                        tr.ms("dve", xin[:, 0:2], 0.0, [xin])
                        lo, c0 = 0, 2
                    if b == nb - 1:
                        tr.ms("dve", xin[:, 514:516], 0.0, [xin])
                        hi, c1 = T, 514
                    tr.dma("sp", xin[:, c0:c1], self.pcs[ct][:, lo:hi], writes=[xin])
                    tr.ts("dve", acc[:], xin[:, 0:512], cw[:, ct, 0:1], None, ALU.mult, None, [xin, cw], [acc])
                    for tap in range(1, 5):
                        tr.stt("dve", acc[:], xin[:, tap:tap + 512], cw[:, ct, tap:tap + 1], acc[:], ALU.mult, ALU.add,
                               [xin, cw, acc], [acc])
                    tr.act(sx[:], acc[:], AF.Silu, [acc], [sx])
                    if ct < 2 * KHL:
                        kh = ct % KHL
                        tr.act(sq[:], sx[:], AF.Square, [sx], [sq])
                        pb = tr.bank()
                        tr.mm(pb[:], onesf[:], sq[:], True, True, [onesf, sq], [pb])
                        tr.act(rn[:], pb[:], AF.Sqrt, [pb], [rn], bias=L2_EPS)
                        tr.rcp(rn[:], rn[:], [rn], [rn])
                        tr.stt("dve", xnb[:], sx[:], (128 ** -0.5 if ct < KHL else 1.0), rn[:], ALU.mult, ALU.mult, [sx, rn], [xnb])
                        dst = self.qn if ct < KHL else self.kn
                        tr.dma("pool", dst[b * 4:b * 4 + 4, :, kh, :].rearrange("s p t -> p s t"),
                               xnb[:].rearrange("p (s t) -> p s t", t=128), reads=[xnb])
                        if ct < KHL:
                            continue
                    else:
                        tr.cp("act", xnb[:], sx[:], [sx], [xnb])
                    pb = tr.bank()
                    pv = pb[:].bitcast(BF16)
                    for s in range(4):
                        tr.tp(pv[:, s * 128:(s + 1) * 128], xnb[:, s * 128:(s + 1) * 128], ident[:], [xnb, ident], [pb])
                    tr.cp("act", tk[:], pv[:, 0:512].rearrange("p (s d) -> p s d", d=128), [pb], [tk])
                    if ct < 2 * KHL:
                        kh = ct - KHL
                        tr.dma("pool", self.ktok[b * 4:b * 4 + 4, :, kh * 128:(kh + 1) * 128].rearrange("s p d -> p s d"),
                               tk[:], reads=[tk])
                    else:
                        vh = ct - 2 * KHL
                        tr.dma("pool", self.vtok[b * 4:b * 4 + 4, :, vh * 128:(vh + 1) * 128].rearrange("s p d -> p s d"),
                               tk[:], reads=[tk])

    def gdn_scan(self, tr, T, j, d, og_view):
        nt = T // 128
        bwd = d == 1
        H = VHL
        with tr.phase():
            ident = self.load_const_bf16(tr, "ident", C_IDENT)
            identf = self.load_const_f32(tr, "identf", C_IDENT)
            onesf = self.load_const_f32(tr, "onesf", C_ONES)
            tri = self.load_const_f32(tr, "tri", C_TRIB if bwd else C_TRIF)
            ntri = self.load_const_f32(tr, "ntri", C_NTRIB if bwd else C_NTRIF)
            mb = self.load_const_f32(tr, "mb", C_MBB if bwd else C_MBF)
            mbs = self.load_const_f32(tr, "mbs", C_MBSB if bwd else C_MBSF)
            gn = tr.sb("gn", [128, 128], F32)
            tr.dma("sp", gn[:], bc_rows(self.dn_norm_g[j:j + 1, :]), writes=[gn])
            S32 = tr.sb("S32", [128, H, 128], F32)
            Sb = tr.sb("Sb", [128, H, 128], BF16)
            tr.ms("dve", S32[:], 0.0, [S32])
            tr.ms("dve", Sb[:], 0.0, [Sb])
            qchs = [tr.sb("qch%d" % i, [128, KHL, 128], BF16) for i in range(2)]
            kchs = [tr.sb("kch%d" % i, [128, KHL, 128], BF16) for i in range(2)]
            ktks = [tr.sb("ktk%d" % i, [128, KHL, 128], BF16) for i in range(2)]
            vtks = [tr.sb("vtk%d" % i, [128, H, 128], BF16) for i in range(2)]
            gbts = [tr.sb("gbt%d" % i, [128, 4, H], F32) for i in range(2)]
            if bwd:
                ofts = [tr.sb("oft%d" % i, [128, H, 128], F32) for i in range(2)]
                szts = [tr.sb("szt%d" % i, [128, H, 128], F32) for i in range(2)]
                ogTts = [tr.sb("sogT%d" % i, [128, H, 128], BF16) for i in range(2)]
            else:
                osts = [tr.sb("ost%d" % i, [128, H, 128], F32) for i in range(2)]
            gc = tr.sb("gc", [128, 2 * H], F32)
            egc = tr.sb("egc", [128, 2 * H], F32)
            dkf = tr.sb("dkf", [128, H], F32)
            bk = tr.sb("bk", [128, H], F32)
            NG = 2
            G = []
            for g_ in range(NG):
                t_ = {}
                for nm, dt_ in (("gbc", F32), ("E", BF16), ("ETs", F32), ("Lb", F32), ("Ub", F32), ("Aq", BF16),
                                ("PA", F32), ("QA", F32), ("N32", F32), ("Nb", BF16), ("bv", BF16), ("bke", BF16),
                                ("kd", BF16), ("u32", F32), ("wTb", BF16), ("vnew", BF16), ("o1", F32), ("ot", F32),
                                ("osq", F32), ("ogb", BF16), ("stmp", F32)):
                    t_[nm] = tr.sb("%s_%d" % (nm, g_), [128, 4, 128], dt_)
                t_["ost"] = tr.sb("gost_%d" % g_, [128, 8], F32)
                G.append(t_)

            def b4(ap2):
                return ap2.unsqueeze(2).to_broadcast([128, 4, 128])

            def fl(buf):
                return buf[:].rearrange("p a b -> p (a b)")

            order = range(nt - 1, -1, -1) if bwd else range(nt)
            for ci, n in enumerate(order):
                qch, kch, ktk, vtk, gbt = qchs[ci % 2], kchs[ci % 2], ktks[ci % 2], vtks[ci % 2], gbts[ci % 2]
                tr.dma("sp", qch[:], self.qn[n], writes=[qch])
                tr.dma("sp", kch[:], self.kn[n], writes=[kch])
                tr.dma("sp", fl(ktk), self.ktok[n], writes=[ktk])
                tr.dma("sp", fl(vtk), self.vtok[n], writes=[vtk])
                tr.dma("sp", fl(gbt), self.gb[n], writes=[gbt])
                if bwd:
                    oft, szt, ogTt = ofts[ci % 2], szts[ci % 2], ogTts[ci % 2]
                    tr.dma("sp", fl(oft), self.of[n], writes=[oft])
                    tr.dma("sp", fl(szt), self.sz[n], writes=[szt])
                else:
                    ostg = osts[ci % 2]
                graw = gbt[:, 2 * d, :]
                beta = gbt[:, 2 * d + 1, :]
                pb = tr.bank()
                tr.mm(pb[:, 0:H], tri[:], graw, True, True, [tri, gbt], [pb])
                tr.mm(pb[:, H:2 * H], onesf[:], graw, True, True, [onesf, gbt], [pb])
                tr.cp("dve", gc[:], pb[:, 0:2 * H], [pb], [gc])
                tr.act(egc[:], gc[:], AF.Exp, [gc], [egc])
                tr.tt("dve", dkf[:], gc[:, H:2 * H], gc[:, 0:H], ALU.subtract, [gc], [dkf])
                tr.act(dkf[:], dkf[:], AF.Exp, [dkf], [dkf])
                tr.tt("dve", bk[:], beta, egc[:, 0:H], ALU.mult, [gbt, egc], [bk])

                def group(gq, t_):
                    gbc, E, ETs, Lb, Ub, Aq = t_["gbc"], t_["E"], t_["ETs"], t_["Lb"], t_["Ub"], t_["Aq"]
                    N32, Nb, bv, bke, kd = t_["N32"], t_["Nb"], t_["bv"], t_["bke"], t_["kd"]
                    u32, wTb, vnew, o1, ot, osq, ogb, stmp, ost = (t_["u32"], t_["wTb"], t_["vnew"], t_["o1"], t_["ot"],
                                                                   t_["osq"], t_["ogb"], t_["stmp"], t_["ost"])
                    h0 = 4 * gq
                    kh0 = 2 * gq
                    tr.cp("dve", gbc[:], b4(graw[:, h0:h0 + 4]), [gbt], [gbc])
                    pA = tr.bank()
                    pB = tr.bank()
                    for hh in range(4):
                        cs = slice(hh * 128, (hh + 1) * 128)
                        tr.mm(pA[:, cs], gbc[:, hh, :], tri[:], True, False, [gbc, tri], [pA])
                        tr.mm(pA[:, cs], ntri[:], gbc[:, hh, :], False, False, [gbc, ntri], [pA])
                        tr.mm(pA[:, cs], identf[:], mb[:], False, True, [identf, mb], [pA])
                        tr.mm(pB[:, cs], tri[:], gbc[:, hh, :], True, False, [gbc, tri], [pB])
                        tr.mm(pB[:, cs], gbc[:, hh, :], ntri[:], False, False, [gbc, ntri], [pB])
                        tr.mm(pB[:, cs], identf[:], mbs[:], False, True, [identf, mbs], [pB])
                    tr.act(fl(E), pA[:], AF.Exp, [pA], [E])
                    tr.act(fl(ETs), pB[:], AF.Exp, [pB], [ETs])
                    pC = tr.bank()
                    for i_ in range(2):
                        tr.mm(pC[:, i_ * 128:(i_ + 1) * 128], kch[:, kh0 + i_, :], kch[:, kh0 + i_, :], True, True, [kch], [pC])
                        tr.mm(pC[:, (2 + i_) * 128:(3 + i_) * 128], kch[:, kh0 + i_, :], qch[:, kh0 + i_, :], True, True,
                              [kch, qch], [pC])
                    for hh in range(4):
                        i_ = hh // 2
                        tr.stt("dve", Lb[:, hh, :], pC[:, i_ * 128:(i_ + 1) * 128], beta[:, h0 + hh:h0 + hh + 1], ETs[:, hh, :],
                               ALU.mult, ALU.mult, [pC, gbt, ETs], [Lb])
                    for i_ in range(2):
                        tr.tt("dve", Aq[:, 2 * i_:2 * i_ + 2, :],
                              pC[:, (2 + i_) * 128:(3 + i_) * 128].unsqueeze(1).to_broadcast([128, 2, 128]),
                              E[:, 2 * i_:2 * i_ + 2, :], ALU.mult, [pC, E], [Aq])
                    yield
                    pT = tr.bank()
                    for hh in range(4):
                        tr.tp(pT[:, hh * 128:(hh + 1) * 128], Lb[:, hh, :], identf[:], [Lb, identf], [pT])
                    tr.cp("act", fl(Ub), pT[:], [pT], [Ub])
                    tr.tt("dve", N32[:], identf[:].unsqueeze(1).to_broadcast([128, 4, 128]), Ub[:], ALU.subtract,
                          [identf, Ub], [N32])
                    yield
                    Pc, Qc = Ub, Lb
                    for k in range(1, 7):
                        if k % 2:
                            Qn, Pn = t_["QA"], t_["PA"]
                        else:
                            Qn, Pn = Lb, Ub
                        pX = tr.bank()
                        for hh in range(4):
                            tr.mm(pX[:, hh * 128:(hh + 1) * 128], Pc[:, hh, :], Qc[:, hh, :], True, True, [Pc, Qc], [pX])
                        if k < 6:
                            pY = tr.bank()
                            for hh in range(4):
                                tr.mm(pY[:, hh * 128:(hh + 1) * 128], Qc[:, hh, :], Pc[:, hh, :], True, True, [Pc, Qc], [pY])
                        tr.cp("act", fl(Qn), pX[:], [pX], [Qn])
                        if k < 6:
                            tr.cp("dve", fl(Pn), pY[:], [pY], [Pn])
                        yield
                        pZ = tr.bank()
                        for hh in range(4):
                            tr.mm(pZ[:, hh * 128:(hh + 1) * 128], Qn[:, hh, :], N32[:, hh, :], True, True, [Qn, N32], [pZ])
                        tr.tt("dve", fl(N32), fl(N32), pZ[:], ALU.add, [N32, pZ], [N32])
                        Pc, Qc = Pn, Qn
                        yield
                    tr.cp("act", Nb[:], N32[:], [N32], [Nb])
                    tr.tt("dve", bv[:], vtk[:, h0:h0 + 4, :], b4(beta[:, h0:h0 + 4]), ALU.mult, [vtk, gbt], [bv])
                    for i_ in range(2):
                        kx = ktk[:, kh0 + i_, :].unsqueeze(1).to_broadcast([128, 2, 128])
                        hs = slice(h0 + 2 * i_, h0 + 2 * i_ + 2)
                        tr.tt("pool", bke[:, 2 * i_:2 * i_ + 2, :], kx, bk[:, hs].unsqueeze(2).to_broadcast([128, 2, 128]),
                              ALU.mult, [ktk, bk], [bke])
                        tr.tt("pool", kd[:, 2 * i_:2 * i_ + 2, :], kx, dkf[:, hs].unsqueeze(2).to_broadcast([128, 2, 128]),
                              ALU.mult, [ktk, dkf], [kd])
                    yield
                    pU = tr.bank()
                    pW = tr.bank()
                    for hh in range(4):
                        cs = slice(hh * 128, (hh + 1) * 128)
                        tr.mm(pU[:, cs], Nb[:, hh, :], bv[:, hh, :], True, True, [Nb, bv], [pU])
                        tr.mm(pW[:, cs], bke[:, hh, :], Nb[:, hh, :], True, True, [Nb, bke], [pW])
                    tr.cp("act", fl(u32), pU[:], [pU], [u32])
                    tr.cp("dve", fl(wTb), pW[:], [pW], [wTb])
                    yield
                    pA2 = tr.bank()
                    for hh in range(4):
                        tr.mm(pA2[:, hh * 128:(hh + 1) * 128], wTb[:, hh, :], Sb[:, h0 + hh, :], True, True, [wTb, Sb], [pA2])
                    tr.stt("dve", fl(vnew), pA2[:], -1.0, fl(u32), ALU.mult, ALU.add, [pA2, u32], [vnew])
                    yield
                    pB1 = tr.bank()
                    pB2 = tr.bank()
                    pC2 = tr.bank()
                    for hh in range(4):
                        cs = slice(hh * 128, (hh + 1) * 128)
                        tr.mm(pB1[:, cs], qch[:, kh0 + hh // 2, :], Sb[:, h0 + hh, :], True, True, [qch, Sb], [pB1])
                        tr.mm(pB2[:, cs], Aq[:, hh, :], vnew[:, hh, :], True, True, [Aq, vnew], [pB2])
                        tr.mm(pC2[:, cs], kd[:, hh, :], vnew[:, hh, :], True, True, [kd, vnew], [pC2])
                    tr.tt("dve", o1[:], pB1[:].rearrange("p (a b) -> p a b", b=128), b4(egc[:, h0:h0 + 4]), ALU.mult,
                          [pB1, egc], [o1])
                    tr.tt("pool", stmp[:], S32[:, h0:h0 + 4, :], b4(egc[:, H + h0:H + h0 + 4]), ALU.mult, [S32, egc], [stmp])
                    tr.tt("dve", S32[:, h0:h0 + 4, :], stmp[:], pC2[:].rearrange("p (a b) -> p a b", b=128), ALU.add,
                          [stmp, pC2], [S32])
                    tr.cp("act", Sb[:, h0:h0 + 4, :], S32[:, h0:h0 + 4, :], [S32], [Sb])
                    if not bwd:
                        tr.tt("dve", ostg[:, h0:h0 + 4, :], o1[:], pB2[:].rearrange("p (a b) -> p a b", b=128), ALU.add,
                              [o1, pB2], [ostg])
                        return
                    tr.tt("dve", ot[:], o1[:], pB2[:].rearrange("p (a b) -> p a b", b=128), ALU.add, [o1, pB2], [ot])
                    yield
                    tr.tt("pool", ot[:], ot[:], oft[:, h0:h0 + 4, :], ALU.add, [ot, oft], [ot])
                    tr.act(osq[:], ot[:], AF.Square, [ot], [osq])
                    tr.red("dve", ost[:, 0:4], osq[:], [osq], [ost])
                    tr.act(ost[:, 4:8], ost[:, 0:4], AF.Sqrt, [ost], [ost], bias=RMS_EPS, scale=1.0 / 128)
                    tr.rcp(ost[:, 4:8], ost[:, 4:8], [ost], [ost])
                    tr.tt("dve", ot[:], ot[:], b4(ost[:, 4:8]), ALU.mult, [ot, ost], [ot])
                    tr.tt("pool", ot[:], ot[:], gn[:].unsqueeze(1).to_broadcast([128, 4, 128]), ALU.mult, [ot, gn], [ot])
                    tr.tt("dve", ogb[:], ot[:], szt[:, h0:h0 + 4, :], ALU.mult, [ot, szt], [ogb])
                    yield
                    pT2 = tr.bank()
                    pT2v = pT2[:].bitcast(BF16)
                    for hh in range(4):
                        tr.tp(pT2v[:, hh * 128:(hh + 1) * 128], ogb[:, hh, :], ident[:], [ogb, ident], [pT2])
                    tr.cp("act", ogTt[:, h0:h0 + 4, :].rearrange("p a b -> p (a b)"), pT2v[:, 0:512], [pT2], [ogTt])

                gens = [group(gq, G[gq % NG]) for gq in range(H // 4)]
                live = list(gens)
                while live:
                    nxt = []
                    for g_ in live:
                        try:
                            next(g_)
                            nxt.append(g_)
                        except StopIteration:
                            pass
                    live = nxt
                if bwd:
                    tr.dma("pool", og_view(n // 4)[:, :, (n % 4) * 128:(n % 4 + 1) * 128], ogTt[:], reads=[ogTt])
                else:
                    tr.dma("pool", self.of[n], fl(ostg), reads=[ostg])

    def build(self, depth=4):
        nc = self.nc
        with contextlib.ExitStack() as es:
            tr = self.tr = TR(nc, es)
            AQ = HL * 256
            KW = KHL * 128
            VW = VHL * 128
            for si, T in enumerate(self.seqs):
                self.resnorm(tr, T, self.xin[si], None, self.xs, self.norm_g[0:1, :], False, self.hT)
                for i in range(depth):
                    j = i // 2
                    if i % 2 == 0:
                        w = self.attn_w_in[j]
                        ogv = self.og_loc_blk(si, "a", 2 * HL)
                        self.proj(tr, T, self.src_hT(), 16, w[:, 0:AQ], AQ, "feat", self.ev_feat_bf16(self.qT, 128 ** -0.5))
                        self.proj(tr, T, self.src_hT(), 16, w[:, AQ:2 * AQ], AQ, "feat", self.ev_feat_bf16(self.kT, 1.0))
                        self.proj(tr, T, self.src_hT(), 16, w[:, 2 * AQ:3 * AQ], AQ, "tok", self.ev_v())
                        self.proj(tr, T, self.src_hT(), 16, w[:, 3 * AQ:4 * AQ], AQ, "tok", self.ev_tok_f32(self.sg, 0, 1, AF.Silu))
                        self.attn_core(tr, T, j, 0.8 - 0.6 * math.exp(-0.3 * i), ogv)
                        tr.allgather(list(zip(self.og_loc[si, "a"], self.og_all[si, "a"])))
                        self.proj(tr, T, self.src_gathered(si, "a", 2 * HL), 16, self.attn_w_out[j], 2048, "tok",
                                  self.ev_tok_f32(self.y, 0, 4, None))
                    else:
                        w = self.dn_w_in[j]
                        ogv = self.og_loc_blk(si, "g", VHL)
                        self.proj(tr, T, self.src_hT(), 16, w[:, 0:KW], KW, "feat", self.ev_pc(0))
                        self.proj(tr, T, self.src_hT(), 16, w[:, KW:2 * KW], KW, "feat", self.ev_pc(KHL))
                        self.proj(tr, T, self.src_hT(), 16, w[:, 2 * KW:2 * KW + VW], VW, "feat", self.ev_pc(2 * KHL))
                        self.proj(tr, T, self.src_hT(), 16, w[:, 2 * KW + VW:2 * KW + 2 * VW], VW, "tok",
                                  self.ev_tok_f32(self.sz, 0, VW // 512, AF.Silu))
                        self.proj(tr, T, self.src_hT(), 16, w[:, 2 * KW + 2 * VW:2 * KW + 2 * VW + 4 * VHL], 4 * VHL, "tok",
                                  self.ev_gates(j))
                        import os
                        skip = os.environ.get("DBG_SKIP", "")
                        if "conv" not in skip:
                            self.gdn_conv(tr, T, j)
                        if "scanf" not in skip:
                            self.gdn_scan(tr, T, j, 0, ogv)
                        if "scanb" not in skip:
                            self.gdn_scan(tr, T, j, 1, ogv)
                        tr.allgather(list(zip(self.og_loc[si, "g"], self.og_all[si, "g"])))
                        for c2 in range(2):
                            self.proj(tr, T, self.src_gathered(si, "g", VHL), 32, self.dn_w_out[j][:, c2 * 1024:(c2 + 1) * 1024],
                                      1024, "tok", self.ev_tok_f32(self.y, c2 * 1024, 2, None))
                    last = i == depth - 1
                    if last:
                        self.resnorm(tr, T, self.xs, self.y, None, self.final_norm_g[0:1, :], True, self.yout[si])
                    else:
                        self.resnorm(tr, T, self.xs, self.y, self.xs, self.norm_g[i + 1:i + 2, :], False, self.hT)
        return nc


def core_weights(w, core):
    f = lambda a: np.ascontiguousarray(a, np.float32)
    heads = core_heads(core)
    o = {}
    awi = w["attn_w_in"]
    cols = []
    for base in (0, 2048, 4096, 6144):
        for h in heads:
            cols.append(np.arange(base + h * 256, base + (h + 1) * 256))
    o["attn_w_in"] = f(awi[:, :, np.concatenate(cols)])
    perm = np.concatenate([np.arange(h * 256, (h + 1) * 256) for r in range(NCORE) for h in core_heads(r)])
    o["attn_w_out"] = f(w["attn_w_out"][:, perm, :])
    dwi = w["dn_w_in"]
    kh = np.arange(core * KHL * 128, (core + 1) * KHL * 128)
    vh = np.arange(core * VHL * 128, (core + 1) * VHL * 128)
    ab = np.concatenate([12288 + t * 32 + np.arange(core * VHL, (core + 1) * VHL) for t in range(4)])
    o["dn_w_in"] = f(dwi[:, :, np.concatenate([kh, 2048 + kh, 4096 + vh, 8192 + vh, ab])])
    o["dn_conv_w"] = f(w["dn_conv_w"][:, :, np.concatenate([kh, 2048 + kh, 4096 + vh])])
    for k in ("dn_a_log_fwd", "dn_dt_bias_fwd", "dn_a_log_bwd", "dn_dt_bias_bwd"):
        o[k] = f(w[k][:, core * VHL:(core + 1) * VHL])
    o["dn_w_out"] = f(w["dn_w_out"])
    o["norm_g"] = f(w["norm_g"])
    o["attn_lambda"] = f(w["attn_lambda"]).reshape(2, 512)
    o["attn_subln_g"] = f(w["attn_subln_g"])
    o["dn_norm_g"] = f(w["dn_norm_g"])
    o["final_norm_g"] = f(w["final_norm_g"]).reshape(1, D)
    o["cst"] = make_consts(core)
    return o


def make_in_maps(xs, w):
    maps = []
    for c in range(NCORE):
        m = core_weights(w, c)
        for i, x in enumerate(xs):
            m["x%d" % i] = np.ascontiguousarray(x, np.float32)
        maps.append(m)
    return maps


def kernel(x_prompt, x_sample, **w):
    prog = Prog(SEQS)
    nc = prog.build()
    in_maps = make_in_maps([x_prompt[0], x_sample[0], x_sample[1]], w)
    res = run_bass_kernel_spmd(nc, in_maps, core_ids=list(range(NCORE)))
    y_prompt = np.asarray(res.results[0]["y0"], np.float32)[None]
    y_sample = np.stack([np.asarray(res.results[0]["y1"], np.float32), np.asarray(res.results[0]["y2"], np.float32)], axis=0)
    return (y_prompt, y_sample)
```

```python
import contextlib
import math
import numpy as np
import concourse.bass as bass
import concourse.mybir as mybir
from concourse.bass_utils import run_bass_kernel_spmd

F32 = mybir.dt.float32
BF16 = mybir.dt.bfloat16
AF = mybir.ActivationFunctionType
ALU = mybir.AluOpType
AX = mybir.AxisListType

D = 2048
SEQS = (16384, 4096, 4096)
NCORE = 4
HL = 2
KHL = 4
VHL = 8
ATT_W = (1280, 1 << 30)
NSLOT = 8
SAME_ENGINE_SYNC = True
INV_F32 = True
ATT_THRESH = 80.0
VA = 264
RMS_EPS = 1e-6
L2_EPS = 1e-6
NEG = -30000.0

C_IDENT, C_ONES, C_TRIF, C_TRIB, C_NTRIF, C_NTRIB, C_MBF, C_MBB, C_MBSF, C_MBSB = [i * 128 for i in range(10)]
C_J = 1280
C_JABS = C_J + 256
C_CB = C_JABS + 512
NCB = 132
C_SLP = C_CB + HL * NCB
C_END = C_SLP + 4


def core_heads(core):
    return (core, 7 - core)


def make_consts(core):
    c = np.zeros((128, C_END), np.float32)
    p = np.arange(128)[:, None].astype(np.float64)
    i = np.arange(128)[None, :].astype(np.float64)
    c[:, C_IDENT:C_IDENT + 128] = (p == i)
    c[:, C_ONES:C_ONES + 128] = 1.0
    c[:, C_TRIF:C_TRIF + 128] = (p <= i)
    c[:, C_TRIB:C_TRIB + 128] = (p >= i)
    c[:, C_NTRIF:C_NTRIF + 128] = -1.0 * (p <= i)
    c[:, C_NTRIB:C_NTRIB + 128] = -1.0 * (p >= i)
    c[:, C_MBF:C_MBF + 128] = np.where(i >= p, 0.0, NEG)
    c[:, C_MBB:C_MBB + 128] = np.where(i <= p, 0.0, NEG)
    c[:, C_MBSF:C_MBSF + 128] = np.where(p > i, 0.0, NEG)
    c[:, C_MBSB:C_MBSB + 128] = np.where(p < i, 0.0, NEG)
    j = np.arange(256)[None, :].astype(np.float64)
    c[:, C_J:C_J + 256] = j - p
    c[:, C_JABS:C_JABS + 256] = np.abs(j - p)
    c[:, C_JABS + 256:C_JABS + 512] = np.abs(j - p - 128)
    for sl, h in enumerate(core_heads(core)):
        m = 2.0 ** (-(h + 1))
        c[:, C_CB + sl * NCB:C_CB + (sl + 1) * NCB] = -m * 128.0 * np.arange(NCB)[None, :]
        c[:, C_SLP + 2 * sl] = -m
        c[:, C_SLP + 2 * sl + 1] = m
    return c


class Sem:
    __slots__ = ("h", "v")

    def __init__(self, nc, es, name):
        self.h = es.enter_context(nc.semaphore(name))
        self.v = 0


class Buf:
    __slots__ = ("w", "r", "t")

    def __init__(self, t):
        self.w = None
        self.r = {}
        self.t = t

    def __getitem__(self, idx):
        return self.t[idx]


class TR:
    def __init__(self, nc, es):
        self.nc = nc
        self.es = es
        self.pes = None
        self.engs = ("pe", "act", "dve", "pool", "sp")
        self.sem = {e: Sem(nc, es, "s_" + e) for e in ("pe", "act", "dve", "pool")}
        self.slots = {q: [Sem(nc, es, "d_%s%d" % (q, i)) for i in range(NSLOT)] for q in ("sp", "act", "pool")}
        self.slot_i = {q: 0 for q in self.slots}
        self.known = {e: {} for e in self.engs}
        self.q = {e: [] for e in self.engs}
        self.nins = 0
        self.pend = []
        self.ccsem = None
        self.P = [Buf(es.enter_context(nc.psum_tensor("P%d" % i, [128, 512], F32))) for i in range(8)]
        self.pi = 0

    def bank(self):
        b = self.P[self.pi]
        self.pi = (self.pi + 1) % 8
        return b

    @contextlib.contextmanager
    def phase(self):
        with contextlib.ExitStack() as pes:
            self.pes = pes
            yield
            self.barrier()
            self.emit()
            self.pes = None

    def emit(self):
        with self.nc.Block() as block:
            for e, sect in (("sp", block.sync), ("pe", block.tensor), ("act", block.scalar),
                            ("dve", block.vector), ("pool", block.gpsimd)):
                lst = self.q[e]
                if not lst:
                    continue

                def body(eng, lst=lst):
                    for f in lst:
                        f(eng)
                sect(body)
                self.nins += len(lst)
        self.q = {e: [] for e in self.engs}

    def sb(self, name, shape, dt):
        es = self.pes if self.pes is not None else self.es
        self.uid = getattr(self, "uid", 0) + 1
        return Buf(es.enter_context(self.nc.sbuf_tensor("%s_%d" % (name, self.uid), shape, dt)))

    def wait(self, e, so, val):
        k = self.known[e]
        if k.get(so, 0) >= val:
            return
        self.pend.append((so.h, val))
        k[so] = val

    def flush(self, e, keep_last):
        p = self.pend
        self.pend = []
        last = None
        if keep_last and p:
            last = p.pop()
        for h, val in p:
            self.q[e].append(lambda eng, h=h, val=val: eng.wait_ge(h, val))
        return last

    def _deps(self, e, reads, writes, own):
        same = SAME_ENGINE_SYNC and e != "pe"
        for b in reads:
            if b.w is not None:
                so, v = b.w
                if so is own and not same:
                    continue
                self.wait(e, so, v)
        for b in writes:
            if b.w is not None:
                so, v = b.w
                if not (so is own and not same):
                    self.wait(e, so, v)
            for so, v in b.r.items():
                if so is own:
                    continue
                self.wait(e, so, v)

    def op(self, e, fn, reads=(), writes=()):
        own = self.sem[e]
        self._deps(e, reads, writes, own)
        lw = self.flush(e, True)
        own.v += 1
        h = own.h
        if lw is None:
            self.q[e].append(lambda eng, fn=fn, h=h: fn(eng).then_inc(h, 1))
        else:
            self.q[e].append(lambda eng, fn=fn, h=h, wh=lw[0], wv=lw[1]: _winc(fn(eng), wh, wv, h, 1))
        for b in reads:
            b.r[own] = own.v
        for b in writes:
            b.w = (own, own.v)
            b.r = {}

    def dma(self, q, out, in_, reads=(), writes=()):
        sl = self.slots[q]
        i = self.slot_i[q]
        self.slot_i[q] = (i + 1) % NSLOT
        so = sl[i]
        self._deps(q, reads, writes, None)
        self.wait(q, so, so.v)
        lw = self.flush(q, True)
        so.v += 16
        h = so.h
        if lw is None:
            self.q[q].append(lambda eng, o=out, i_=in_, h=h: eng.dma_start(out=o, in_=i_).then_inc(h, 16))
        else:
            self.q[q].append(lambda eng, o=out, i_=in_, h=h, wh=lw[0], wv=lw[1]:
                             _winc(eng.dma_start(out=o, in_=i_), wh, wv, h, 16))
        for b in reads:
            b.r[so] = so.v
        for b in writes:
            b.w = (so, so.v)
            b.r = {}

    def barrier(self):
        allsems = list(self.sem.values()) + [s for sl in self.slots.values() for s in sl]
        for e in self.engs:
            for so in allsems:
                if so.v > 0:
                    self.wait(e, so, so.v)
            self.flush(e, False)

    def allgather(self, pairs):
        if self.ccsem is None:
            self.ccsem = Sem(self.nc, self.es, "ccsem")
        so = self.ccsem
        h = so.h
        for src, dst in pairs:
            so.v += 1
            self.q["pool"].append(lambda eng, h=h, s_=src, d_=dst: eng.collective_compute(
                "AllGather", ALU.bypass, replica_groups=[list(range(NCORE))],
                ins=[s_.ap().opt()], outs=[d_.ap().opt()]).then_inc(h, 1))
        for e in self.engs:
            self.wait(e, so, so.v)
            self.flush(e, False)
        self.emit()

    def mm(self, out, lhsT, rhs, start, stop, reads, writes):
        self.op("pe", lambda e, o=out, l=lhsT, r=rhs, s=start, t=stop: e.matmul(o, lhsT=l, rhs=r, start=s, stop=t),
                reads, writes)

    def tp(self, out, in_, ident, reads, writes):
        self.op("pe", lambda e, o=out, i=in_, d=ident: e.transpose(o, i, d), reads, writes)

    def act(self, out, in_, func, reads, writes, bias=0.0, scale=1.0):
        self.op("act", lambda e, o=out, i=in_, f=func, b=bias, s=scale: e.activation(out=o, in_=i, func=f, bias=b, scale=s),
                reads, writes)

    def tt(self, eng, out, in0, in1, op, reads, writes):
        self.op(eng, lambda e, o=out, a=in0, b=in1, p=op: e.tensor_tensor(out=o, in0=a, in1=b, op=p), reads, writes)

    def ts(self, eng, out, in0, s1, s2, op0, op1, reads, writes):
        if s2 is None:
            self.op(eng, lambda e, o=out, a=in0, s=s1, p=op0: e.tensor_scalar(out=o, in0=a, scalar1=s, scalar2=None, op0=p),
                    reads, writes)
        else:
            self.op(eng, lambda e, o=out, a=in0, s=s1, u=s2, p=op0, q=op1:
                    e.tensor_scalar(out=o, in0=a, scalar1=s, scalar2=u, op0=p, op1=q), reads, writes)

    def stt(self, eng, out, in0, scalar, in1, op0, op1, reads, writes):
        self.op(eng, lambda e, o=out, a=in0, s=scalar, b=in1, p=op0, q=op1:
                e.scalar_tensor_tensor(out=o, in0=a, scalar=s, in1=b, op0=p, op1=q), reads, writes)

    def cp(self, eng, out, in_, reads, writes):
        if eng == "act":
            self.op("act", lambda e, o=out, i=in_: e.copy(out=o, in_=i), reads, writes)
        else:
            self.op(eng, lambda e, o=out, i=in_: e.tensor_copy(out=o, in_=i), reads, writes)

    def red(self, eng, out, in_, reads, writes):
        self.op(eng, lambda e, o=out, i=in_: e.reduce_sum(out=o, in_=i, axis=AX.X), reads, writes)

    def rcp(self, out, in_, reads, writes):
        self.op("dve", lambda e, o=out, i=in_: e.reciprocal(out=o, in_=i), reads, writes)

    def ms(self, eng, ap, val, writes):
        self.op(eng, lambda e, a=ap, v=val: e.memset(a, v), (), writes)


def _winc(ins, wh, wv, h, inc):
    ins.wait_op(wh, wv, "sem-ge")
    return ins.then_inc(h, inc)


def bc_rows(ap2d):
    return ap2d.partition_broadcast(128).rearrange("p a b -> p (a b)")


class Prog:
    def __init__(self, seqs, dbg=()):
        self.seqs = seqs
        self.dbg = set(dbg)
        nc = self.nc = bass.Bass("TRN2", target_bir_lowering=False)
        Tm = self.Tm = max(seqs)
        nb, nt = Tm // 512, Tm // 128
        ei = lambda n, s: nc.dram_tensor(n, s, F32, kind="ExternalInput")
        self.xin = [ei("x%d" % i, [T, D]) for i, T in enumerate(seqs)]
        self.yout = [nc.dram_tensor("y%d" % i, [T, D], F32, kind="ExternalOutput") for i, T in enumerate(seqs)]
        self.norm_g = ei("norm_g", [4, D])
        self.attn_w_in = ei("attn_w_in", [2, D, 8 * HL * 128])
        self.attn_lambda = ei("attn_lambda", [2, 512])
        self.attn_subln_g = ei("attn_subln_g", [2, 256])
        self.attn_w_out = ei("attn_w_out", [2, 2048, 2048])
        self.dn_w_in = ei("dn_w_in", [2, D, 2 * KHL * 128 + 2 * VHL * 128 + 4 * VHL])
        self.dn_conv_w = ei("dn_conv_w", [2, 5, 2 * KHL * 128 + VHL * 128])
        self.dn_a_log_fwd = ei("dn_a_log_fwd", [2, VHL])
        self.dn_dt_bias_fwd = ei("dn_dt_bias_fwd", [2, VHL])
        self.dn_a_log_bwd = ei("dn_a_log_bwd", [2, VHL])
        self.dn_dt_bias_bwd = ei("dn_dt_bias_bwd", [2, VHL])
        self.dn_norm_g = ei("dn_norm_g", [2, 128])
        self.dn_w_out = ei("dn_w_out", [2, 4096, 2048])
        self.final_norm_g = ei("final_norm_g", [1, D])
        self.cst = ei("cst", [128, C_END])

        def scr(n, s, dt):
            return nc.dram_tensor(n, s, dt, kind=("ExternalOutput" if n in self.dbg else "Internal"))
        self.xs = scr("xs", [Tm, D], F32)
        self.hT = scr("hT", [nb, 128, 16, 512], BF16)
        self.y = scr("ysc", [nt, 128, 2048], F32)
        self.qT = scr("qT", [2 * HL, nb, 128, 512], BF16)
        self.kT = scr("kT", [2 * HL, nb, 128, 512], BF16)
        self.va = scr("va", [nt, 128, HL, VA], BF16)
        self.sg = scr("sg", [nt, 128, HL * 256], F32)
        self.pcs = scr("pcs", [2 * KHL + VHL, 128, Tm], F32)
        self.qn = scr("qn", [nt, 128, KHL, 128], BF16)
        self.kn = scr("kn", [nt, 128, KHL, 128], BF16)
        self.ktok = scr("ktok", [nt, 128, KHL * 128], BF16)
        self.vtok = scr("vtok", [nt, 128, VHL * 128], BF16)
        self.sz = scr("sz", [nt, 128, VHL * 128], F32)
        self.gb = scr("gb", [nt, 128, 4 * VHL], F32)
        self.of = scr("of", [nt, 128, VHL * 128], F32)
        self.og_loc = {}
        self.og_all = {}
        self.og_bpc = {"a": max(1, 1024 // (128 * 2 * HL)), "g": max(1, 1024 // (128 * VHL))}
        for si, T in enumerate(seqs):
            nb_ = T // 512
            for kind, kl in (("a", 2 * HL), ("g", VHL)):
                bpc = self.og_bpc[kind]
                nch = -(-nb_ // bpc)
                szs = [min(bpc, nb_ - c * bpc) * 128 * kl for c in range(nch)]
                self.og_loc[si, kind] = [scr("ogl_%d%s%d" % (si, kind, c), [szs[c], 512], BF16) for c in range(nch)]
                self.og_all[si, kind] = [scr("oga_%d%s%d" % (si, kind, c), [NCORE * szs[c], 512], BF16) for c in range(nch)]

    def og_loc_blk(self, si, kind, kl):
        bpc = self.og_bpc[kind]

        def f(b):
            c, bl = divmod(b, bpc)
            return self.og_loc[si, kind][c].ap().rearrange("(b p k) t -> b p k t", p=128, k=kl)[bl]
        return f

    def og_all_blk(self, si, kind, kl):
        bpc = self.og_bpc[kind]

        def f(r, b):
            c, bl = divmod(b, bpc)
            return self.og_all[si, kind][c].ap().rearrange("(r b p k) t -> r b p k t", r=NCORE, p=128, k=kl)[r, bl]
        return f

    def load_const_bf16(self, tr, name, col, n=128):
        st = tr.sb(name + "_f", [128, n], F32)
        tr.dma("sp", st[:], self.cst[:, col:col + n], writes=[st])
        b = tr.sb(name, [128, n], BF16)
        tr.cp("dve", b[:], st[:], [st], [b])
        return b

    def load_const_f32(self, tr, name, col, n=128):
        st = tr.sb(name, [128, n], F32)
        tr.dma("sp", st[:], self.cst[:, col:col + n], writes=[st])
        return st

    def resnorm(self, tr, T, x_src, y_src, x_dst, g_row, final, out):
        nt = T // 128
        with tr.phase():
            gt = tr.sb("gt", [128, D], F32)
            tr.dma("sp", gt[:], bc_rows(g_row), writes=[gt])
            ident = self.load_const_bf16(tr, "ident", C_IDENT)
            xts = [tr.sb("xt%d" % i, [128, D], F32) for i in range(2)]
            yts = [tr.sb("yt%d" % i, [128, D], F32) for i in range(2)]
            sq = tr.sb("sq", [128, D], F32)
            ssq = tr.sb("ssq", [128, 2], F32)
            hbs = [tr.sb("hb%d" % i, [128, D], BF16) for i in range(2)]
            hfs = [tr.sb("hf%d" % i, [128, D], F32) for i in range(2)] if final else None
            hTt = [tr.sb("hTt%d" % i, [128, 16, 512], BF16) for i in range(2)]
            for tt in range(nt):
                xt = xts[tt % 2]
                tr.dma("sp", xt[:], x_src[tt * 128:(tt + 1) * 128, :], writes=[xt])
                if y_src is not None:
                    yt = yts[tt % 2]
                    tr.dma("sp", yt[:], y_src[tt], writes=[yt])
                    tr.tt("dve", xt[:], xt[:], yt[:], ALU.add, [xt, yt], [xt])
                if x_dst is not None:
                    tr.dma("pool", x_dst[tt * 128:(tt + 1) * 128, :], xt[:], reads=[xt])
                tr.act(sq[:], xt[:], AF.Square, [xt], [sq])
                tr.red("dve", ssq[:, 0:1], sq[:], [sq], [ssq])
                tr.act(ssq[:, 1:2], ssq[:, 0:1], AF.Sqrt, [ssq], [ssq], bias=RMS_EPS, scale=1.0 / D)
                tr.rcp(ssq[:, 1:2], ssq[:, 1:2], [ssq], [ssq])
                if final:
                    hf = hfs[tt % 2]
                    tr.stt("dve", hf[:], xt[:], ssq[:, 1:2], gt[:], ALU.mult, ALU.mult, [xt, ssq, gt], [hf])
                    tr.dma("pool", out[tt * 128:(tt + 1) * 128, :], hf[:], reads=[hf])
                    continue
                hb = hbs[tt % 2]
                tr.stt("dve", hb[:], xt[:], ssq[:, 1:2], gt[:], ALU.mult, ALU.mult, [xt, ssq, gt], [hb])
                b, s = tt // 4, tt % 4
                ht = hTt[b % 2]
                for half in range(2):
                    pb = tr.bank()
                    pv = pb[:].bitcast(BF16)
                    for k8 in range(8):
                        kc = half * 8 + k8
                        tr.tp(pv[:, k8 * 128:(k8 + 1) * 128], hb[:, kc * 128:(kc + 1) * 128], ident[:], [hb, ident], [pb])
                    tr.cp("act", ht[:, half * 8:(half + 1) * 8, s * 128:(s + 1) * 128],
                          pv.rearrange("p (a b) -> p a b", b=128), [pb], [ht])
                if s == 3:
                    tr.dma("pool", out[b], ht[:], reads=[ht])

    def proj(self, tr, T, src_loader, nk, w_ap, wc, mode, evac_factory):
        nb = T // 512
        with tr.phase():
            wt = tr.sb("wt", [128, nk, wc], BF16)
            stg = [tr.sb("wstg%d" % i, [128, wc], F32) for i in range(2)]
            for kc in range(nk):
                st = stg[kc % 2]
                tr.dma("sp", st[:], w_ap[kc * 128:(kc + 1) * 128, :], writes=[st])
                if kc % 2:
                    tr.cp("act", wt[:, kc, :], st[:], [st], [wt])
                else:
                    tr.cp("dve", wt[:, kc, :], st[:], [st], [wt])
            sbt = [tr.sb("psrc%d" % i, [128, nk, 512], BF16) for i in range(2)]
            evac = evac_factory(tr)
            for b in range(nb):
                s = sbt[b % 2]
                src_loader(tr, b, s)
                if mode == "feat":
                    for ct in range(wc // 128):
                        pb = tr.bank()
                        for kc in range(nk):
                            tr.mm(pb[:], wt[:, kc, ct * 128:(ct + 1) * 128], s[:, kc, :], kc == 0, kc == nk - 1, [wt, s], [pb])
                        evac(b, ct, pb)
                else:
                    n = min(wc, 512)
                    for sub in range(4):
                        for cg in range(max(1, wc // 512)):
                            pb = tr.bank()
                            for kc in range(nk):
                                tr.mm(pb[:, 0:n], s[:, kc, sub * 128:(sub + 1) * 128], wt[:, kc, cg * n:(cg + 1) * n],
                                      kc == 0, kc == nk - 1, [wt, s], [pb])
                            evac(b, sub, cg, pb)

    def src_hT(self):
        def ld(tr, b, s):
            tr.dma("sp", s[:], self.hT[b], writes=[s])
        return ld

    def src_gathered(self, si, kind, kl):
        v = self.og_all_blk(si, kind, kl)

        def ld(tr, b, s):
            for r in range(NCORE):
                tr.dma("sp", s[:, r * kl:(r + 1) * kl, :], v(r, b), writes=[s])
        return ld

    def ev_feat_bf16(self, dst, scale):
        def fac(tr):
            stg = [tr.sb("evq%d" % i, [128, 512], BF16) for i in range(3)]
            cnt = [0]

            def ev(b, ct, pb):
                st = stg[cnt[0] % 3]
                cnt[0] += 1
                if cnt[0] % 2:
                    tr.act(st[:], pb[:], AF.Copy, [pb], [st], scale=scale)
                else:
                    tr.ts("dve", st[:], pb[:], scale, None, ALU.mult, None, [pb], [st])
                tr.dma("pool", dst[ct][b], st[:], reads=[st])
            return ev
        return fac

    def ev_v(self):
        def fac(tr):
            stg = [tr.sb("evv%d" % i, [128, HL, VA], BF16) for i in range(2)]
            for st in stg:
                tr.ms("dve", st[:], 1.0, [st])

            def ev(b, sub, cg, pb):
                tt = b * 4 + sub
                st = stg[tt % 2]
                src = pb[:].rearrange("p (h e) -> p h e", e=256)
                tr.cp("act", st[:, 0:HL, 0:256], src, [pb], [st])
                tr.dma("pool", self.va[tt], st[:], reads=[st])
            return ev
        return fac

    def ev_tok_f32(self, dst, col0, ncg, func, wcg=512):
        def fac(tr):
            stg = [tr.sb("evt%d" % i, [128, ncg * wcg], F32) for i in range(2)]

            def ev(b, sub, cg, pb):
                tt = b * 4 + sub
                st = stg[tt % 2]
                if func is not None:
                    tr.act(st[:, cg * wcg:(cg + 1) * wcg], pb[:, 0:wcg], func, [pb], [st])
                elif cg % 2:
                    tr.cp("act", st[:, cg * wcg:(cg + 1) * wcg], pb[:, 0:wcg], [pb], [st])
                else:
                    tr.cp("dve", st[:, cg * wcg:(cg + 1) * wcg], pb[:, 0:wcg], [pb], [st])
                if cg == ncg - 1:
                    tr.dma("pool", dst[tt][:, col0:col0 + ncg * wcg], st[:], reads=[st])
            return ev
        return fac

    def ev_pc(self, ct0):
        def fac(tr):
            stg = [tr.sb("evp%d" % i, [128, 512], F32) for i in range(3)]
            cnt = [0]

            def ev(b, ct, pb):
                st = stg[cnt[0] % 3]
                cnt[0] += 1
                tr.cp("act" if cnt[0] % 2 else "dve", st[:], pb[:], [pb], [st])
                tr.dma("pool", self.pcs[ct0 + ct][:, b * 512:(b + 1) * 512], st[:], reads=[st])
            return ev
        return fac

    def ev_gates(self, j):
        H = VHL

        def fac(tr):
            dtb = tr.sb("dtb", [128, 2, H], F32)
            nA = tr.sb("nA", [128, 2, H], F32)
            tr.dma("sp", dtb[:, 0, :], bc_rows(self.dn_dt_bias_fwd[j:j + 1, :]), writes=[dtb])
            tr.dma("sp", dtb[:, 1, :], bc_rows(self.dn_dt_bias_bwd[j:j + 1, :]), writes=[dtb])
            tr.dma("sp", nA[:, 0, :], bc_rows(self.dn_a_log_fwd[j:j + 1, :]), writes=[nA])
            tr.dma("sp", nA[:, 1, :], bc_rows(self.dn_a_log_bwd[j:j + 1, :]), writes=[nA])
            tr.act(nA[:], nA[:], AF.Exp, [nA], [nA])
            tr.ts("dve", nA[:], nA[:], -1.0, None, ALU.mult, None, [nA], [nA])
            xa = tr.sb("g_xa", [128, 2, H], F32)
            ax = tr.sb("g_ax", [128, 2, H], F32)
            outs = [tr.sb("g_out%d" % i, [128, 4, H], F32) for i in range(2)]

            def ev(b, sub, cg, pb):
                tt = b * 4 + sub
                o = outs[tt % 2]
                pv = pb[:, 0:4 * H].rearrange("p (a b) -> p a b", b=H)
                for d in range(2):
                    tr.tt("dve", xa[:, d, :], pv[:, 2 * d, :], dtb[:, d, :], ALU.add, [pb, dtb], [xa])
                tr.act(ax[:], xa[:], AF.Abs, [xa], [ax])
                tr.act(ax[:], ax[:], AF.Exp, [ax], [ax], scale=-1.0)
                tr.act(ax[:], ax[:], AF.Ln, [ax], [ax], bias=1.0)
                tr.ts("dve", xa[:], xa[:], 0.0, None, ALU.max, None, [xa], [xa])
                tr.tt("dve", xa[:], xa[:], ax[:], ALU.add, [xa, ax], [xa])
                for d in range(2):
                    tr.tt("dve", o[:, 2 * d, :], xa[:, d, :], nA[:, d, :], ALU.mult, [xa, nA], [o])
                    tr.act(o[:, 2 * d + 1, :], pv[:, 2 * d + 1, :], AF.Sigmoid, [pb], [o])
                tr.dma("pool", self.gb[tt], o[:].rearrange("p a b -> p (a b)"), reads=[o])
            return ev
        return fac

    def attn_core(self, tr, T, j, lam_init, og_view):
        nt = T // 128
        nq = T // 256
        with tr.phase():
            ident = self.load_const_bf16(tr, "ident", C_IDENT)
            Jt = self.load_const_f32(tr, "Jt", C_J, 256)
            Ja = self.load_const_f32(tr, "Ja", C_JABS, 512)
            cb = self.load_const_f32(tr, "cb", C_CB, HL * NCB)
            slp = self.load_const_f32(tr, "slp", C_SLP, 4)
            lv = tr.sb("lv", [128, 512], F32)
            tr.dma("sp", lv[:], bc_rows(self.attn_lambda[j:j + 1, :]), writes=[lv])
            lp = tr.sb("lp", [128, 256], F32)
            ls = tr.sb("ls", [128, 4], F32)
            tr.tt("dve", lp[:, 0:128], lv[:, 0:128], lv[:, 128:256], ALU.mult, [lv], [lp])
            tr.tt("dve", lp[:, 128:256], lv[:, 256:384], lv[:, 384:512], ALU.mult, [lv], [lp])
            tr.red("dve", ls[:, 0:2], lp[:].rearrange("p (a b) -> p a b", b=128), [lp], [ls])
            tr.act(ls[:, 0:2], ls[:, 0:2], AF.Exp, [ls], [ls])
            tr.tt("dve", ls[:, 2:3], ls[:, 1:2], ls[:, 0:1], ALU.subtract, [ls], [ls])
            tr.ts("dve", ls[:, 3:4], ls[:, 2:3], -lam_init, None, ALU.add, None, [ls], [ls])
            sgn = tr.sb("sgn", [128, 256], F32)
            tr.dma("sp", sgn[:], bc_rows(self.attn_subln_g[j:j + 1, :]), writes=[sgn])
            tr.ts("dve", sgn[:], sgn[:], 1.0 - lam_init, None, ALU.mult, None, [sgn], [sgn])

            qts = [tr.sb("aq%d" % i, [128, 2, 256], BF16) for i in range(2)]
            sgts = [tr.sb("asg%d" % i, [128, 2, 256], F32) for i in range(2)]
            kts = [tr.sb("ak%d" % i, [128, 2, 512], BF16) for i in range(4)]
            vts = [tr.sb("av%d" % i, [128, 4, VA], BF16) for i in range(4)]
            tbs = [tr.sb("atb%d" % i, [128, 512], F32) for i in range(4)]
            pbs = [tr.sb("apb%d" % i, [128, 512], BF16) for i in range(4)]
            om = tr.sb("aom", [128, 2, 2, 256], F32)
            rden = tr.sb("arden", [128, 4], F32)
            oc = tr.sb("aoc", [128, 2, 256], F32)
            osq = tr.sb("aosq", [128, 2, 256], F32)
            ost = tr.sb("aost", [128, 4], F32)
            ogb = tr.sb("aogb", [128, 2, 256], BF16)
            ogTt = [tr.sb("aogT%d" % i, [128, 2, 256], BF16) for i in range(2)]
            acc = tr.P[0:4]
            sc = tr.P[4:8]
            tpb = tr.P[4:5]
            nblk = 0
            it = 0
            for h in range(HL):
                W = ATT_W[h]
                mneg = slp[:, 2 * h:2 * h + 1]
                mpos = slp[:, 2 * h + 1:2 * h + 2]
                for qt in range(nq):
                    q0 = qt * 256
                    qtile = qts[it % 2]
                    sgt = sgts[it % 2]
                    ogt = ogTt[it % 2]
                    it += 1
                    qb, qo = qt // 2, (qt % 2) * 256
                    for mp in range(2):
                        tr.dma("sp", qtile[:, mp, :], self.qT[2 * h + mp][qb][:, qo:qo + 256], writes=[qtile])
                    tr.dma("sp", sgt[:], self.sg[qt * 2:qt * 2 + 2, :, h * 256:(h + 1) * 256].rearrange("s p e -> p s e"),
                           writes=[sgt])
                    kt_lo = max(0, (q0 - W) // 128)
                    kt_hi = min(nt, -((-(q0 + 256 + W)) // 128))
                    kts_list = list(range(kt_lo, kt_hi))
                    nk_ = len(kts_list)
                    info = {}
                    cur = {"kb": -1, "kt": None, "vt": None}

                    def issue_qk(idx):
                        nonlocal nblk
                        kt = kts_list[idx]
                        kb, ks = kt // 4, kt % 4
                        if kb != cur["kb"]:
                            cur["kb"] = kb
                            cur["kt"] = kts[nblk % 4]
                            cur["vt"] = vts[nblk % 4]
                            nblk += 1
                            for mp in range(2):
                                tr.dma("sp", cur["kt"][:, mp, :], self.kT[2 * h + mp][kb], writes=[cur["kt"]])
                            tr.dma("sp", cur["vt"][:], self.va[kb * 4:kb * 4 + 4, :, h, :].rearrange("s p e -> p s e"),
                                   writes=[cur["vt"]])
                        ktile, vtile = cur["kt"], cur["vt"]
                        sb_ = sc[idx % 4]
                        for mp in range(2):
                            tr.mm(sb_[:, mp * 256:(mp + 1) * 256], ktile[:, mp, ks * 128:(ks + 1) * 128], qtile[:, mp, :],
                                  True, True, [ktile, qtile], [sb_])
                        info[idx] = (vtile, ks, kt * 128, sb_)

                    def issue_soft(idx):
                        vtile, ks, k0, sb_ = info[idx]
                        tb = tbs[idx % 4]
                        pb = pbs[idx % 4]
                        sv = sb_[:].rearrange("p (a b) -> p a b", b=256)
                        tv = tb[:].rearrange("p (a b) -> p a b", b=256)
                        if k0 + 127 < q0:
                            jv = Jt[:].unsqueeze(1).to_broadcast([128, 2, 256])
                            tr.stt("dve", tv, jv, mneg, sv, ALU.mult, ALU.add, [Jt, sb_, slp], [tb])
                            ci = (q0 - k0) // 128
                        elif k0 > q0 + 255:
                            jv = Jt[:].unsqueeze(1).to_broadcast([128, 2, 256])
                            tr.stt("dve", tv, jv, mpos, sv, ALU.mult, ALU.add, [Jt, sb_, slp], [tb])
                            ci = (k0 - q0) // 128
                        else:
                            i_ = (k0 - q0) // 128
                            jv = Ja[:, i_ * 256:(i_ + 1) * 256].unsqueeze(1).to_broadcast([128, 2, 256])
                            tr.stt("dve", tv, jv, mneg, sv, ALU.mult, ALU.add, [Ja, sb_, slp], [tb])
                            ci = 0
                        tr.act(pb[:], tb[:], AF.Exp, [tb, cb], [pb], bias=cb[:, h * NCB + ci:h * NCB + ci + 1])

                    def issue_av(idx):
                        vtile, ks, k0, sb_ = info.pop(idx)
                        pb = pbs[idx % 4]
                        first, last = idx == 0, idx == nk_ - 1
                        for mp in range(2):
                            for qs in range(2):
                                a = acc[mp * 2 + qs]
                                tr.mm(a[:, 0:257], pb[:, mp * 256 + qs * 128:mp * 256 + (qs + 1) * 128], vtile[:, ks, 0:257],
                                      first, last, [pb, vtile], [a])

                    LOOK = 3
                    for idx in range(min(LOOK, nk_)):
                        issue_qk(idx)
                    for idx in range(nk_):
                        issue_soft(idx)
                        if idx + LOOK < nk_:
                            issue_qk(idx + LOOK)
                        issue_av(idx)
                    for mp in range(2):
                        for qs in range(2):
                            a = acc[mp * 2 + qs]
                            c_ = mp * 2 + qs
                            tr.rcp(rden[:, c_:c_ + 1], a[:, 256:257], [a], [rden])
                            if qs:
                                tr.act(om[:, mp, qs, :], a[:, 0:256], AF.Copy, [a, rden], [om], scale=rden[:, c_:c_ + 1])
                            else:
                                tr.ts("dve", om[:, mp, qs, :], a[:, 0:256], rden[:, c_:c_ + 1], None, ALU.mult, None, [a, rden], [om])
                    tr.stt("dve", oc[:], om[:, 1, :, :], ls[:, 3:4], om[:, 0, :, :], ALU.mult, ALU.add, [om, ls], [oc])
                    tr.act(osq[:], oc[:], AF.Square, [oc], [osq])
                    tr.red("dve", ost[:, 0:2], osq[:], [osq], [ost])
                    tr.act(ost[:, 2:4], ost[:, 0:2], AF.Sqrt, [ost], [ost], bias=RMS_EPS, scale=1.0 / 256)
                    tr.rcp(ost[:, 2:4], ost[:, 2:4], [ost], [ost])
                    tr.tt("dve", oc[:], oc[:], ost[:, 2:4].unsqueeze(2).to_broadcast([128, 2, 256]), ALU.mult, [oc, ost], [oc])
                    tr.tt("pool", oc[:], oc[:], sgn[:].unsqueeze(1).to_broadcast([128, 2, 256]), ALU.mult, [oc, sgn], [oc])
                    tr.tt("dve", ogb[:], oc[:], sgt[:], ALU.mult, [oc, sgt], [ogb])
                    tb_ = tpb[0]
                    tv_ = tb_[:].bitcast(BF16)
                    for ec in range(2):
                        for qs in range(2):
                            c_ = (ec * 2 + qs) * 128
                            tr.tp(tv_[:, c_:c_ + 128], ogb[:, qs, ec * 128:(ec + 1) * 128], ident[:], [ogb, ident], [tb_])
                    tr.cp("act", ogt[:], tv_[:, 0:512].rearrange("p (a b) -> p a b", b=256), [tb_], [ogt])
                    tr.dma("pool", og_view(qb)[:, 2 * h:2 * h + 2, qo:qo + 256], ogt[:], reads=[ogt])

    def gdn_conv(self, tr, T, j):
        nb = T // 512
        NCT = 2 * KHL + VHL
        with tr.phase():
            ident = self.load_const_bf16(tr, "ident", C_IDENT)
            identf = self.load_const_f32(tr, "identf", C_IDENT)
            onesf = self.load_const_f32(tr, "onesf", C_ONES)
            cwr = tr.sb("cwr", [5, NCT * 128], F32)
            tr.dma("sp", cwr[:], self.dn_conv_w[j], writes=[cwr])
            cw = tr.sb("cw", [128, NCT, 5], F32)
            for ct in range(NCT):
                pb = tr.bank()
                tr.tp(pb[:, 0:5], cwr[0:5, ct * 128:(ct + 1) * 128], identf[0:5, 0:5], [cwr, identf], [pb])
                tr.cp("dve", cw[:, ct, :], pb[:, 0:5], [pb], [cw])
            dg = tr.sb("cdg", [128, 5, 128], F32)
            xins = [tr.sb("cx%d" % i, [128, 516], F32) for i in range(3)]
            sxs = [tr.sb("cs%d" % i, [128, 512], F32) for i in range(2)]
            sqs = [tr.sb("cq%d" % i, [128, 512], F32) for i in range(2)]
            rns = [tr.sb("cr%d" % i, [128, 512], F32) for i in range(2)]
            xnbs = [tr.sb("cn%d" % i, [128, 512], BF16) for i in range(2)]
            tks = [tr.sb("ct%d" % i, [128, 4, 128], BF16) for i in range(2)]
            it = 0
            for ct in range(NCT):
                for b in range(nb):
                    xin = xins[it % 3]
                    sx = sxs[it % 2]
                    sq = sqs[it % 2]
                    rn = rns[it % 2]
                    xnb = xnbs[it % 2]
                    tk = tks[it % 2]
                    it += 1
                    lo = b * 512 - 2
                    hi = b * 512 + 514
                    c0, c1 = 0, 516
                    if b == 0:
                        tr.ms("dve", xin[:, 0:2], 0.0, [xin])
                        lo, c0 = 0, 2
                    if b == nb - 1:
                        tr.ms("dve", xin[:, 514:516], 0.0, [xin])
                        hi, c1 = T, 514
                    tr.dma("sp", xin[:, c0:c1], self.pcs[ct][:, lo:hi], writes=[xin])
                    if b == 0:
                        for tap in range(5):
                            tr.ts("dve", dg[:, tap, :], identf[:], cw[:, ct, tap:tap + 1], None, ALU.mult, None, [identf, cw], [dg])
                    acc = tr.bank()
                    for tap in range(5):
                        tr.mm(acc[:], dg[:, tap, :], xin[:, tap:tap + 512], tap == 0, tap == 4, [dg, xin], [acc])
                    tr.act(sx[:], acc[:], AF.Silu, [acc], [sx])
                    if ct < 2 * KHL:
                        kh = ct % KHL
                        tr.act(sq[:], sx[:], AF.Square, [sx], [sq])
                        pb = tr.bank()
                        tr.mm(pb[:], onesf[:], sq[:], True, True, [onesf, sq], [pb])
                        tr.act(rn[:], pb[:], AF.Sqrt, [pb], [rn], bias=L2_EPS)
                        tr.rcp(rn[:], rn[:], [rn], [rn])
                        tr.stt("dve", xnb[:], sx[:], (128 ** -0.5 if ct < KHL else 1.0), rn[:], ALU.mult, ALU.mult, [sx, rn], [xnb])
                        dst = self.qn if ct < KHL else self.kn
                        tr.dma("pool", dst[b * 4:b * 4 + 4, :, kh, :].rearrange("s p t -> p s t"),
                               xnb[:].rearrange("p (s t) -> p s t", t=128), reads=[xnb])
                        if ct < KHL:
                            continue
                    else:
                        tr.cp("act", xnb[:], sx[:], [sx], [xnb])
                    pb = tr.bank()
                    pv = pb[:].bitcast(BF16)
                    for s in range(4):
                        tr.tp(pv[:, s * 128:(s + 1) * 128], xnb[:, s * 128:(s + 1) * 128], ident[:], [xnb, ident], [pb])
                    tr.cp("act", tk[:], pv[:, 0:512].rearrange("p (s d) -> p s d", d=128), [pb], [tk])
                    if ct < 2 * KHL:
                        kh = ct - KHL
                        tr.dma("pool", self.ktok[b * 4:b * 4 + 4, :, kh * 128:(kh + 1) * 128].rearrange("s p d -> p s d"),
                               tk[:], reads=[tk])
                    else:
                        vh = ct - 2 * KHL
                        tr.dma("pool", self.vtok[b * 4:b * 4 + 4, :, vh * 128:(vh + 1) * 128].rearrange("s p d -> p s d"),
                               tk[:], reads=[tk])

    def gdn_scan(self, tr, T, j, d, og_view):
        nt = T // 128
        bwd = d == 1
        H = VHL
        with tr.phase():
            ident = self.load_const_bf16(tr, "ident", C_IDENT)
            identf = self.load_const_f32(tr, "identf", C_IDENT)
            onesf = self.load_const_f32(tr, "onesf", C_ONES)
            tri = self.load_const_f32(tr, "tri", C_TRIB if bwd else C_TRIF)
            ntri = self.load_const_f32(tr, "ntri", C_NTRIB if bwd else C_NTRIF)
            mb = self.load_const_f32(tr, "mb", C_MBB if bwd else C_MBF)
            mbs = self.load_const_f32(tr, "mbs", C_MBSB if bwd else C_MBSF)
            gn = tr.sb("gn", [128, 128], F32)
            tr.dma("sp", gn[:], bc_rows(self.dn_norm_g[j:j + 1, :]), writes=[gn])
            S32 = tr.sb("S32", [128, H, 128], F32)
            Sb = tr.sb("Sb", [128, H, 128], BF16)
            tr.ms("dve", S32[:], 0.0, [S32])
            tr.ms("dve", Sb[:], 0.0, [Sb])
            NB = 3
            qchs = [tr.sb("qch%d" % i, [128, KHL, 128], BF16) for i in range(NB)]
            kchs = [tr.sb("kch%d" % i, [128, KHL, 128], BF16) for i in range(NB)]
            ktks = [tr.sb("ktk%d" % i, [128, KHL, 128], BF16) for i in range(NB)]
            vtks = [tr.sb("vtk%d" % i, [128, H, 128], BF16) for i in range(NB)]
            gbts = [tr.sb("gbt%d" % i, [128, 4, H], F32) for i in range(NB)]
            if bwd:
                ofts = [tr.sb("oft%d" % i, [128, H, 128], F32) for i in range(NB)]
                szts = [tr.sb("szt%d" % i, [128, H, 128], F32) for i in range(NB)]
                ogTts = [tr.sb("sogT%d" % i, [128, H, 128], BF16) for i in range(NB)]
            else:
                osts = [tr.sb("ost%d" % i, [128, H, 128], F32) for i in range(NB)]
            gcs = [tr.sb("gc%d" % i, [128, 2 * H], F32) for i in range(NB)]
            egcs = [tr.sb("egc%d" % i, [128, 2 * H], F32) for i in range(NB)]
            dkfs = [tr.sb("dkf%d" % i, [128, H], F32) for i in range(NB)]
            bks = [tr.sb("bk%d" % i, [128, H], F32) for i in range(NB)]
            NSETS = 4
            G = []
            for g_ in range(NSETS):
                t_ = {}
                for nm, dt_ in (("gbc", F32), ("E", BF16), ("ETs", F32), ("Lb", F32), ("Ub", F32), ("Aq", BF16),
                                ("PA", F32), ("QA", F32), ("N32", F32), ("Nb", BF16), ("bv", BF16), ("bke", BF16),
                                ("kd", BF16), ("u32", F32), ("wTb", BF16), ("vnew", BF16), ("o1", F32),
                                ("ogb", BF16), ("stmp", F32)):
                    t_[nm] = tr.sb("%s_%d" % (nm, g_), [128, 4, 128], dt_)
                t_["ost"] = tr.sb("gost_%d" % g_, [128, 8], F32)
                G.append(t_)

            def b4(ap2):
                return ap2.unsqueeze(2).to_broadcast([128, 4, 128])

            def fl(buf):
                return buf[:].rearrange("p a b -> p (a b)")

            order = range(nt - 1, -1, -1) if bwd else range(nt)
            live = []
            setno = [0]

            def finish(cx):
                cx["remaining"] -= 1
                if cx["remaining"]:
                    return
                n_ = cx["n"]
                if bwd:
                    tr.dma("pool", og_view(n_ // 4)[:, :, (n_ % 4) * 128:(n_ % 4 + 1) * 128], cx["ogTt"][:], reads=[cx["ogTt"]])
                else:
                    tr.dma("pool", self.of[n_], fl(cx["ostg"]), reads=[cx["ostg"]])

            for ci, n in enumerate(order):
                qch, kch, ktk, vtk, gbt = qchs[ci % NB], kchs[ci % NB], ktks[ci % NB], vtks[ci % NB], gbts[ci % NB]
                gc, egc, dkf, bk = gcs[ci % NB], egcs[ci % NB], dkfs[ci % NB], bks[ci % NB]
                cx = {"n": n, "remaining": H // 4}
                tr.dma("sp", qch[:], self.qn[n], writes=[qch])
                tr.dma("sp", kch[:], self.kn[n], writes=[kch])
                tr.dma("sp", fl(ktk), self.ktok[n], writes=[ktk])
                tr.dma("sp", fl(vtk), self.vtok[n], writes=[vtk])
                tr.dma("sp", fl(gbt), self.gb[n], writes=[gbt])
                if bwd:
                    oft, szt, ogTt = ofts[ci % NB], szts[ci % NB], ogTts[ci % NB]
                    cx["ogTt"] = ogTt
                    tr.dma("sp", fl(oft), self.of[n], writes=[oft])
                    tr.dma("sp", fl(szt), self.sz[n], writes=[szt])
                else:
                    ostg = osts[ci % NB]
                    cx["ostg"] = ostg
                graw = gbt[:, 2 * d, :]
                beta = gbt[:, 2 * d + 1, :]
                pb = tr.bank()
                tr.mm(pb[:, 0:H], tri[:], graw, True, True, [tri, gbt], [pb])
                tr.mm(pb[:, H:2 * H], onesf[:], graw, True, True, [onesf, gbt], [pb])
                tr.cp("dve", gc[:], pb[:, 0:2 * H], [pb], [gc])
                tr.act(egc[:], gc[:], AF.Exp, [gc], [egc])
                tr.tt("dve", dkf[:], gc[:, H:2 * H], gc[:, 0:H], ALU.subtract, [gc], [dkf])
                tr.act(dkf[:], dkf[:], AF.Exp, [dkf], [dkf])
                tr.tt("dve", bk[:], beta, egc[:, 0:H], ALU.mult, [gbt, egc], [bk])

                def group(gq, t_, cx=cx, qch=qch, kch=kch, ktk=ktk, vtk=vtk, gbt=gbt, gc=gc, egc=egc, dkf=dkf, bk=bk,
                          graw=graw, beta=beta, oft=(oft if bwd else None), szt=(szt if bwd else None),
                          ogTt=(ogTt if bwd else None), ostg=(None if bwd else ostg)):
                    gbc, E, ETs, Lb, Ub, Aq = t_["gbc"], t_["E"], t_["ETs"], t_["Lb"], t_["Ub"], t_["Aq"]
                    N32, Nb, bv, bke, kd = t_["N32"], t_["Nb"], t_["bv"], t_["bke"], t_["kd"]
                    u32, wTb, vnew, o1, ogb, stmp, ost = (t_["u32"], t_["wTb"], t_["vnew"], t_["o1"], t_["ogb"], t_["stmp"],
                                                          t_["ost"])
                    ot = o1
                    osq = stmp
                    h0 = 4 * gq
                    kh0 = 2 * gq
                    tr.cp("dve", gbc[:], b4(graw[:, h0:h0 + 4]), [gbt], [gbc])
                    pA = tr.bank()
                    pB = tr.bank()
                    for hh in range(4):
                        cs = slice(hh * 128, (hh + 1) * 128)
                        tr.mm(pA[:, cs], gbc[:, hh, :], tri[:], True, False, [gbc, tri], [pA])
                        tr.mm(pA[:, cs], ntri[:], gbc[:, hh, :], False, False, [gbc, ntri], [pA])
                        tr.mm(pA[:, cs], identf[:], mb[:], False, True, [identf, mb], [pA])
                        tr.mm(pB[:, cs], tri[:], gbc[:, hh, :], True, False, [gbc, tri], [pB])
                        tr.mm(pB[:, cs], gbc[:, hh, :], ntri[:], False, False, [gbc, ntri], [pB])
                        tr.mm(pB[:, cs], identf[:], mbs[:], False, True, [identf, mbs], [pB])
                    tr.act(fl(E), pA[:], AF.Exp, [pA], [E])
                    tr.act(fl(ETs), pB[:], AF.Exp, [pB], [ETs])
                    pC = tr.bank()
                    for i_ in range(2):
                        tr.mm(pC[:, i_ * 128:(i_ + 1) * 128], kch[:, kh0 + i_, :], kch[:, kh0 + i_, :], True, True, [kch], [pC])
                        tr.mm(pC[:, (2 + i_) * 128:(3 + i_) * 128], kch[:, kh0 + i_, :], qch[:, kh0 + i_, :], True, True,
                              [kch, qch], [pC])
                    for hh in range(4):
                        i_ = hh // 2
                        tr.stt("dve", Lb[:, hh, :], pC[:, i_ * 128:(i_ + 1) * 128], beta[:, h0 + hh:h0 + hh + 1], ETs[:, hh, :],
                               ALU.mult, ALU.mult, [pC, gbt, ETs], [Lb])
                    for i_ in range(2):
                        tr.tt("dve", Aq[:, 2 * i_:2 * i_ + 2, :],
                              pC[:, (2 + i_) * 128:(3 + i_) * 128].unsqueeze(1).to_broadcast([128, 2, 128]),
                              E[:, 2 * i_:2 * i_ + 2, :], ALU.mult, [pC, E], [Aq])
                    yield
                    pT = tr.bank()
                    for hh in range(4):
                        tr.tp(pT[:, hh * 128:(hh + 1) * 128], Lb[:, hh, :], identf[:], [Lb, identf], [pT])
                    tr.cp("act", fl(Ub), pT[:], [pT], [Ub])
                    tr.tt("dve", N32[:], identf[:].unsqueeze(1).to_broadcast([128, 4, 128]), Ub[:], ALU.subtract,
                          [identf, Ub], [N32])
                    yield
                    Pc, Qc = Ub, Lb
                    for k in range(1, 7):
                        if k % 2:
                            Qn, Pn = t_["QA"], t_["PA"]
                        else:
                            Qn, Pn = Lb, Ub
                        pX = tr.bank()
                        for hh in range(4):
                            tr.mm(pX[:, hh * 128:(hh + 1) * 128], Pc[:, hh, :], Qc[:, hh, :], True, True, [Pc, Qc], [pX])
                        if k < 6:
                            pY = tr.bank()
                            for hh in range(4):
                                tr.mm(pY[:, hh * 128:(hh + 1) * 128], Qc[:, hh, :], Pc[:, hh, :], True, True, [Pc, Qc], [pY])
                        tr.cp("act", fl(Qn), pX[:], [pX], [Qn])
                        if k < 6:
                            tr.cp("dve", fl(Pn), pY[:], [pY], [Pn])
                        yield
                        pZ = tr.bank()
                        for hh in range(4):
                            tr.mm(pZ[:, hh * 128:(hh + 1) * 128], Qn[:, hh, :], N32[:, hh, :], True, True, [Qn, N32], [pZ])
                        tr.tt("dve", fl(N32), fl(N32), pZ[:], ALU.add, [N32, pZ], [N32])
                        Pc, Qc = Pn, Qn
                        yield
                    tr.cp("act", Nb[:], N32[:], [N32], [Nb])
                    tr.tt("dve", bv[:], vtk[:, h0:h0 + 4, :], b4(beta[:, h0:h0 + 4]), ALU.mult, [vtk, gbt], [bv])
                    for i_ in range(2):
                        kx = ktk[:, kh0 + i_, :].unsqueeze(1).to_broadcast([128, 2, 128])
                        hs = slice(h0 + 2 * i_, h0 + 2 * i_ + 2)
                        tr.tt("pool", bke[:, 2 * i_:2 * i_ + 2, :], kx, bk[:, hs].unsqueeze(2).to_broadcast([128, 2, 128]),
                              ALU.mult, [ktk, bk], [bke])
                        tr.tt("pool", kd[:, 2 * i_:2 * i_ + 2, :], kx, dkf[:, hs].unsqueeze(2).to_broadcast([128, 2, 128]),
                              ALU.mult, [ktk, dkf], [kd])
                    yield
                    pU = tr.bank()
                    pW = tr.bank()
                    for hh in range(4):
                        cs = slice(hh * 128, (hh + 1) * 128)
                        tr.mm(pU[:, cs], Nb[:, hh, :], bv[:, hh, :], True, True, [Nb, bv], [pU])
                        tr.mm(pW[:, cs], bke[:, hh, :], Nb[:, hh, :], True, True, [Nb, bke], [pW])
                    tr.cp("act", fl(u32), pU[:], [pU], [u32])
                    tr.cp("dve", fl(wTb), pW[:], [pW], [wTb])
                    yield
                    pA2 = tr.bank()
                    for hh in range(4):
                        tr.mm(pA2[:, hh * 128:(hh + 1) * 128], wTb[:, hh, :], Sb[:, h0 + hh, :], True, True, [wTb, Sb], [pA2])
                    tr.stt("dve", fl(vnew), pA2[:], -1.0, fl(u32), ALU.mult, ALU.add, [pA2, u32], [vnew])
                    yield
                    pB1 = tr.bank()
                    pB2 = tr.bank()
                    pC2 = tr.bank()
                    for hh in range(4):
                        cs = slice(hh * 128, (hh + 1) * 128)
                        tr.mm(pB1[:, cs], qch[:, kh0 + hh // 2, :], Sb[:, h0 + hh, :], True, True, [qch, Sb], [pB1])
                        tr.mm(pB2[:, cs], Aq[:, hh, :], vnew[:, hh, :], True, True, [Aq, vnew], [pB2])
                        tr.mm(pC2[:, cs], kd[:, hh, :], vnew[:, hh, :], True, True, [kd, vnew], [pC2])
                    tr.tt("dve", o1[:], pB1[:].rearrange("p (a b) -> p a b", b=128), b4(egc[:, h0:h0 + 4]), ALU.mult,
                          [pB1, egc], [o1])
                    tr.tt("pool", stmp[:], S32[:, h0:h0 + 4, :], b4(egc[:, H + h0:H + h0 + 4]), ALU.mult, [S32, egc], [stmp])
                    tr.tt("dve", S32[:, h0:h0 + 4, :], stmp[:], pC2[:].rearrange("p (a b) -> p a b", b=128), ALU.add,
                          [stmp, pC2], [S32])
                    tr.cp("act", Sb[:, h0:h0 + 4, :], S32[:, h0:h0 + 4, :], [S32], [Sb])
                    if not bwd:
                        tr.tt("dve", ostg[:, h0:h0 + 4, :], o1[:], pB2[:].rearrange("p (a b) -> p a b", b=128), ALU.add,
                              [o1, pB2], [ostg])
                        finish(cx)
                        return
                    tr.tt("dve", ot[:], o1[:], pB2[:].rearrange("p (a b) -> p a b", b=128), ALU.add, [o1, pB2], [ot])
                    yield
                    tr.tt("pool", ot[:], ot[:], oft[:, h0:h0 + 4, :], ALU.add, [ot, oft], [ot])
                    tr.act(osq[:], ot[:], AF.Square, [ot], [osq])
                    tr.red("dve", ost[:, 0:4], osq[:], [osq], [ost])
                    tr.act(ost[:, 4:8], ost[:, 0:4], AF.Sqrt, [ost], [ost], bias=RMS_EPS, scale=1.0 / 128)
                    tr.rcp(ost[:, 4:8], ost[:, 4:8], [ost], [ost])
                    tr.tt("dve", ot[:], ot[:], b4(ost[:, 4:8]), ALU.mult, [ot, ost], [ot])
                    tr.tt("pool", ot[:], ot[:], gn[:].unsqueeze(1).to_broadcast([128, 4, 128]), ALU.mult, [ot, gn], [ot])
                    tr.tt("dve", ogb[:], ot[:], szt[:, h0:h0 + 4, :], ALU.mult, [ot, szt], [ogb])
                    yield
                    pT2 = tr.bank()
                    pT2v = pT2[:].bitcast(BF16)
                    for hh in range(4):
                        tr.tp(pT2v[:, hh * 128:(hh + 1) * 128], ogb[:, hh, :], ident[:], [ogb, ident], [pT2])
                    tr.cp("act", ogTt[:, h0:h0 + 4, :].rearrange("p a b -> p (a b)"), pT2v[:, 0:512], [pT2], [ogTt])
                    finish(cx)

                for gq in range(H // 4):
                    live.append(group(gq, G[setno[0] % NSETS]))
                    setno[0] += 1
                def step_all():
                    nonlocal live
                    nxt = []
                    for g_ in live:
                        try:
                            next(g_)
                            nxt.append(g_)
                        except StopIteration:
                            pass
                    live = nxt

                if ci == 0:
                    for _ in range(9):
                        step_all()
                while len(live) > (H // 4 if ci < nt - 1 else 0):
                    step_all()

    def build(self, depth=4):
        nc = self.nc
        with contextlib.ExitStack() as es:
            tr = self.tr = TR(nc, es)
            AQ = HL * 256
            KW = KHL * 128
            VW = VHL * 128
            for si, T in enumerate(self.seqs):
                self.resnorm(tr, T, self.xin[si], None, self.xs, self.norm_g[0:1, :], False, self.hT)
                for i in range(depth):
                    j = i // 2
                    if i % 2 == 0:
                        w = self.attn_w_in[j]
                        ogv = self.og_loc_blk(si, "a", 2 * HL)
                        self.proj(tr, T, self.src_hT(), 16, w[:, 0:AQ], AQ, "feat", self.ev_feat_bf16(self.qT, 128 ** -0.5))
                        self.proj(tr, T, self.src_hT(), 16, w[:, AQ:2 * AQ], AQ, "feat", self.ev_feat_bf16(self.kT, 1.0))
                        self.proj(tr, T, self.src_hT(), 16, w[:, 2 * AQ:3 * AQ], AQ, "tok", self.ev_v())
                        self.proj(tr, T, self.src_hT(), 16, w[:, 3 * AQ:4 * AQ], AQ, "tok", self.ev_tok_f32(self.sg, 0, 1, AF.Silu))
                        self.attn_core(tr, T, j, 0.8 - 0.6 * math.exp(-0.3 * i), ogv)
                        tr.allgather(list(zip(self.og_loc[si, "a"], self.og_all[si, "a"])))
                        self.proj(tr, T, self.src_gathered(si, "a", 2 * HL), 16, self.attn_w_out[j], 2048, "tok",
                                  self.ev_tok_f32(self.y, 0, 4, None))
                    else:
                        w = self.dn_w_in[j]
                        ogv = self.og_loc_blk(si, "g", VHL)
                        self.proj(tr, T, self.src_hT(), 16, w[:, 0:KW], KW, "feat", self.ev_pc(0))
                        self.proj(tr, T, self.src_hT(), 16, w[:, KW:2 * KW], KW, "feat", self.ev_pc(KHL))
                        self.proj(tr, T, self.src_hT(), 16, w[:, 2 * KW:2 * KW + VW], VW, "feat", self.ev_pc(2 * KHL))
                        self.proj(tr, T, self.src_hT(), 16, w[:, 2 * KW + VW:2 * KW + 2 * VW], VW, "tok",
                                  self.ev_tok_f32(self.sz, 0, VW // 512, AF.Silu))
                        self.proj(tr, T, self.src_hT(), 16, w[:, 2 * KW + 2 * VW:2 * KW + 2 * VW + 4 * VHL], 4 * VHL, "tok",
                                  self.ev_gates(j))
                        import os
                        skip = os.environ.get("DBG_SKIP", "")
                        if "conv" not in skip:
                            self.gdn_conv(tr, T, j)
                        if "scanf" not in skip:
                            self.gdn_scan(tr, T, j, 0, ogv)
                        if "scanb" not in skip:
                            self.gdn_scan(tr, T, j, 1, ogv)
                        tr.allgather(list(zip(self.og_loc[si, "g"], self.og_all[si, "g"])))
                        for c2 in range(2):
                            self.proj(tr, T, self.src_gathered(si, "g", VHL), 32, self.dn_w_out[j][:, c2 * 1024:(c2 + 1) * 1024],
                                      1024, "tok", self.ev_tok_f32(self.y, c2 * 1024, 2, None))
                    last = i == depth - 1
                    if last:
                        self.resnorm(tr, T, self.xs, self.y, None, self.final_norm_g[0:1, :], True, self.yout[si])
                    else:
                        self.resnorm(tr, T, self.xs, self.y, self.xs, self.norm_g[i + 1:i + 2, :], False, self.hT)
        return nc


def core_weights(w, core):
    f = lambda a: np.ascontiguousarray(a, np.float32)
    heads = core_heads(core)
    o = {}
    awi = w["attn_w_in"]
    cols = []
    for base in (0, 2048, 4096, 6144):
        for h in heads:
            cols.append(np.arange(base + h * 256, base + (h + 1) * 256))
    o["attn_w_in"] = f(awi[:, :, np.concatenate(cols)])
    perm = np.concatenate([np.arange(h * 256, (h + 1) * 256) for r in range(NCORE) for h in core_heads(r)])
    o["attn_w_out"] = f(w["attn_w_out"][:, perm, :])
    dwi = w["dn_w_in"]
    kh = np.arange(core * KHL * 128, (core + 1) * KHL * 128)
    vh = np.arange(core * VHL * 128, (core + 1) * VHL * 128)
    ab = np.concatenate([12288 + t * 32 + np.arange(core * VHL, (core + 1) * VHL) for t in range(4)])
    o["dn_w_in"] = f(dwi[:, :, np.concatenate([kh, 2048 + kh, 4096 + vh, 8192 + vh, ab])])
    o["dn_conv_w"] = f(w["dn_conv_w"][:, :, np.concatenate([kh, 2048 + kh, 4096 + vh])])
    for k in ("dn_a_log_fwd", "dn_dt_bias_fwd", "dn_a_log_bwd", "dn_dt_bias_bwd"):
        o[k] = f(w[k][:, core * VHL:(core + 1) * VHL])
    o["dn_w_out"] = f(w["dn_w_out"])
    o["norm_g"] = f(w["norm_g"])
    o["attn_lambda"] = f(w["attn_lambda"]).reshape(2, 512)
    o["attn_subln_g"] = f(w["attn_subln_g"])
    o["dn_norm_g"] = f(w["dn_norm_g"])
    o["final_norm_g"] = f(w["final_norm_g"]).reshape(1, D)
    o["cst"] = make_consts(core)
    return o


def make_in_maps(xs, w):
    maps = []
    for c in range(NCORE):
        m = core_weights(w, c)
        for i, x in enumerate(xs):
            m["x%d" % i] = np.ascontiguousarray(x, np.float32)
        maps.append(m)
    return maps


def kernel(x_prompt, x_sample, **w):
    prog = Prog(SEQS)
    nc = prog.build()
    in_maps = make_in_maps([x_prompt[0], x_sample[0], x_sample[1]], w)
    res = run_bass_kernel_spmd(nc, in_maps, core_ids=list(range(NCORE)))
    y_prompt = np.asarray(res.results[0]["y0"], np.float32)[None]
    y_sample = np.stack([np.asarray(res.results[0]["y1"], np.float32), np.asarray(res.results[0]["y2"], np.float32)], axis=0)
    return (y_prompt, y_sample)
```

```python
import contextlib
import math
import numpy as np
import concourse.bass as bass
import concourse.mybir as mybir
from concourse.bass_utils import run_bass_kernel_spmd

F32 = mybir.dt.float32
BF16 = mybir.dt.bfloat16
AF = mybir.ActivationFunctionType
ALU = mybir.AluOpType
AX = mybir.AxisListType

D = 2048
SEQS = (16384, 4096, 4096)
NCORE = 4
HL = 2
KHL = 4
VHL = 8
ATT_W = (1280, 1 << 30)
NSLOT = 8
SAME_ENGINE_SYNC = True
INV_F32 = True
RES_ACCUM = True
ATT_THRESH = 80.0
VA = 264
RMS_EPS = 1e-6
L2_EPS = 1e-6
NEG = -30000.0

C_IDENT, C_ONES, C_TRIF, C_TRIB, C_NTRIF, C_NTRIB, C_MBF, C_MBB, C_MBSF, C_MBSB = [i * 128 for i in range(10)]
C_J = 1280
C_JABS = C_J + 256
C_CB = C_JABS + 512
NCB = 132
C_SLP = C_CB + HL * NCB
C_END = C_SLP + 4


def core_heads(core):
    return (core, 7 - core)


def make_consts(core):
    c = np.zeros((128, C_END), np.float32)
    p = np.arange(128)[:, None].astype(np.float64)
    i = np.arange(128)[None, :].astype(np.float64)
    c[:, C_IDENT:C_IDENT + 128] = (p == i)
    c[:, C_ONES:C_ONES + 128] = 1.0
    c[:, C_TRIF:C_TRIF + 128] = (p <= i)
    c[:, C_TRIB:C_TRIB + 128] = (p >= i)
    c[:, C_NTRIF:C_NTRIF + 128] = -1.0 * (p <= i)
    c[:, C_NTRIB:C_NTRIB + 128] = -1.0 * (p >= i)
    c[:, C_MBF:C_MBF + 128] = np.where(i >= p, 0.0, NEG)
    c[:, C_MBB:C_MBB + 128] = np.where(i <= p, 0.0, NEG)
    c[:, C_MBSF:C_MBSF + 128] = np.where(p > i, 0.0, NEG)
    c[:, C_MBSB:C_MBSB + 128] = np.where(p < i, 0.0, NEG)
    j = np.arange(256)[None, :].astype(np.float64)
    c[:, C_J:C_J + 256] = j - p
    c[:, C_JABS:C_JABS + 256] = np.abs(j - p)
    c[:, C_JABS + 256:C_JABS + 512] = np.abs(j - p - 128)
    for sl, h in enumerate(core_heads(core)):
        m = 2.0 ** (-(h + 1))
        c[:, C_CB + sl * NCB:C_CB + (sl + 1) * NCB] = -m * 128.0 * np.arange(NCB)[None, :]
        c[:, C_SLP + 2 * sl] = -m
        c[:, C_SLP + 2 * sl + 1] = m
    return c


class Sem:
    __slots__ = ("h", "v")

    def __init__(self, nc, es, name):
        self.h = es.enter_context(nc.semaphore(name))
        self.v = 0


class Buf:
    __slots__ = ("w", "r", "t")

    def __init__(self, t):
        self.w = None
        self.r = {}
        self.t = t

    def __getitem__(self, idx):
        return self.t[idx]


class TR:
    def __init__(self, nc, es):
        self.nc = nc
        self.es = es
        self.pes = None
        self.engs = ("pe", "act", "dve", "pool", "sp")
        self.sem = {e: Sem(nc, es, "s_" + e) for e in ("pe", "act", "dve", "pool")}
        self.slots = {q: [Sem(nc, es, "d_%s%d" % (q, i)) for i in range(NSLOT)] for q in ("sp", "act", "pool")}
        self.slot_i = {q: 0 for q in self.slots}
        self.known = {e: {} for e in self.engs}
        self.q = {e: [] for e in self.engs}
        self.nins = 0
        self.pend = []
        self.ccsem = None
        self.P = [Buf(es.enter_context(nc.psum_tensor("P%d" % i, [128, 512], F32))) for i in range(8)]
        self.pi = 0

    def bank(self):
        b = self.P[self.pi]
        self.pi = (self.pi + 1) % 8
        return b

    @contextlib.contextmanager
    def phase(self):
        with contextlib.ExitStack() as pes:
            self.pes = pes
            yield
            self.barrier()
            self.emit()
            self.pes = None

    def emit(self):
        with self.nc.Block() as block:
            for e, sect in (("sp", block.sync), ("pe", block.tensor), ("act", block.scalar),
                            ("dve", block.vector), ("pool", block.gpsimd)):
                lst = self.q[e]
                if not lst:
                    continue

                def body(eng, lst=lst):
                    for f in lst:
                        f(eng)
                sect(body)
                self.nins += len(lst)
        self.q = {e: [] for e in self.engs}

    def sb(self, name, shape, dt):
        es = self.pes if self.pes is not None else self.es
        self.uid = getattr(self, "uid", 0) + 1
        return Buf(es.enter_context(self.nc.sbuf_tensor("%s_%d" % (name, self.uid), shape, dt)))

    def wait(self, e, so, val):
        k = self.known[e]
        if k.get(so, 0) >= val:
            return
        self.pend.append((so.h, val))
        k[so] = val

    def flush(self, e, keep_last):
        p = self.pend
        self.pend = []
        last = None
        if keep_last and p:
            last = p.pop()
        for h, val in p:
            self.q[e].append(lambda eng, h=h, val=val: eng.wait_ge(h, val))
        return last

    def _deps(self, e, reads, writes, own):
        same = SAME_ENGINE_SYNC and e != "pe"
        for b in reads:
            if b.w is not None:
                so, v = b.w
                if so is own and not same:
                    continue
                self.wait(e, so, v)
        for b in writes:
            if b.w is not None:
                so, v = b.w
                if not (so is own and not same):
                    self.wait(e, so, v)
            for so, v in b.r.items():
                if so is own:
                    continue
                self.wait(e, so, v)

    def op(self, e, fn, reads=(), writes=()):
        own = self.sem[e]
        self._deps(e, reads, writes, own)
        lw = self.flush(e, True)
        own.v += 1
        h = own.h
        if lw is None:
            self.q[e].append(lambda eng, fn=fn, h=h: fn(eng).then_inc(h, 1))
        else:
            self.q[e].append(lambda eng, fn=fn, h=h, wh=lw[0], wv=lw[1]: _winc(fn(eng), wh, wv, h, 1))
        for b in reads:
            b.r[own] = own.v
        for b in writes:
            b.w = (own, own.v)
            b.r = {}

    def dma(self, q, out, in_, reads=(), writes=()):
        sl = self.slots[q]
        i = self.slot_i[q]
        self.slot_i[q] = (i + 1) % NSLOT
        so = sl[i]
        self._deps(q, reads, writes, None)
        self.wait(q, so, so.v)
        lw = self.flush(q, True)
        so.v += 16
        h = so.h
        if lw is None:
            self.q[q].append(lambda eng, o=out, i_=in_, h=h: eng.dma_start(out=o, in_=i_).then_inc(h, 16))
        else:
            self.q[q].append(lambda eng, o=out, i_=in_, h=h, wh=lw[0], wv=lw[1]:
                             _winc(eng.dma_start(out=o, in_=i_), wh, wv, h, 16))
        for b in reads:
            b.r[so] = so.v
        for b in writes:
            b.w = (so, so.v)
            b.r = {}

    def barrier(self):
        allsems = list(self.sem.values()) + [s for sl in self.slots.values() for s in sl]
        for e in self.engs:
            for so in allsems:
                if so.v > 0:
                    self.wait(e, so, so.v)
            self.flush(e, False)

    def allgather(self, pairs):
        if self.ccsem is None:
            self.ccsem = Sem(self.nc, self.es, "ccsem")
        so = self.ccsem
        h = so.h
        for src, dst in pairs:
            so.v += 1
            self.q["pool"].append(lambda eng, h=h, s_=src, d_=dst: eng.collective_compute(
                "AllGather", ALU.bypass, replica_groups=[list(range(NCORE))],
                ins=[s_.ap().opt()], outs=[d_.ap().opt()]).then_inc(h, 1))
        for e in self.engs:
            self.wait(e, so, so.v)
            self.flush(e, False)
        self.emit()

    def mm(self, out, lhsT, rhs, start, stop, reads, writes):
        self.op("pe", lambda e, o=out, l=lhsT, r=rhs, s=start, t=stop: e.matmul(o, lhsT=l, rhs=r, start=s, stop=t),
                reads, writes)

    def tp(self, out, in_, ident, reads, writes):
        self.op("pe", lambda e, o=out, i=in_, d=ident: e.transpose(o, i, d), reads, writes)

    def act(self, out, in_, func, reads, writes, bias=0.0, scale=1.0):
        self.op("act", lambda e, o=out, i=in_, f=func, b=bias, s=scale: e.activation(out=o, in_=i, func=f, bias=b, scale=s),
                reads, writes)

    def tt(self, eng, out, in0, in1, op, reads, writes):
        self.op(eng, lambda e, o=out, a=in0, b=in1, p=op: e.tensor_tensor(out=o, in0=a, in1=b, op=p), reads, writes)

    def ts(self, eng, out, in0, s1, s2, op0, op1, reads, writes):
        if s2 is None:
            self.op(eng, lambda e, o=out, a=in0, s=s1, p=op0: e.tensor_scalar(out=o, in0=a, scalar1=s, scalar2=None, op0=p),
                    reads, writes)
        else:
            self.op(eng, lambda e, o=out, a=in0, s=s1, u=s2, p=op0, q=op1:
                    e.tensor_scalar(out=o, in0=a, scalar1=s, scalar2=u, op0=p, op1=q), reads, writes)

    def stt(self, eng, out, in0, scalar, in1, op0, op1, reads, writes):
        self.op(eng, lambda e, o=out, a=in0, s=scalar, b=in1, p=op0, q=op1:
                e.scalar_tensor_tensor(out=o, in0=a, scalar=s, in1=b, op0=p, op1=q), reads, writes)

    def cp(self, eng, out, in_, reads, writes):
        if eng == "act":
            self.op("act", lambda e, o=out, i=in_: e.copy(out=o, in_=i), reads, writes)
        else:
            self.op(eng, lambda e, o=out, i=in_: e.tensor_copy(out=o, in_=i), reads, writes)

    def red(self, eng, out, in_, reads, writes):
        self.op(eng, lambda e, o=out, i=in_: e.reduce_sum(out=o, in_=i, axis=AX.X), reads, writes)

    def rcp(self, out, in_, reads, writes):
        self.op("dve", lambda e, o=out, i=in_: e.reciprocal(out=o, in_=i), reads, writes)

    def ms(self, eng, ap, val, writes):
        self.op(eng, lambda e, a=ap, v=val: e.memset(a, v), (), writes)


def _winc(ins, wh, wv, h, inc):
    ins.wait_op(wh, wv, "sem-ge")
    return ins.then_inc(h, inc)


def bc_rows(ap2d):
    return ap2d.partition_broadcast(128).rearrange("p a b -> p (a b)")


class Prog:
    def __init__(self, seqs, dbg=()):
        self.seqs = seqs
        self.dbg = set(dbg)
        nc = self.nc = bass.Bass("TRN2", target_bir_lowering=False)
        Tm = self.Tm = max(seqs)
        nb, nt = Tm // 512, Tm // 128
        ei = lambda n, s: nc.dram_tensor(n, s, F32, kind="ExternalInput")
        self.xin = [ei("x%d" % i, [T, D]) for i, T in enumerate(seqs)]
        self.yout = [nc.dram_tensor("y%d" % i, [T, D], F32, kind="ExternalOutput") for i, T in enumerate(seqs)]
        self.norm_g = ei("norm_g", [4, D])
        self.attn_w_in = ei("attn_w_in", [2, D, 8 * HL * 128])
        self.attn_lambda = ei("attn_lambda", [2, 512])
        self.attn_subln_g = ei("attn_subln_g", [2, 256])
        self.attn_w_out = ei("attn_w_out", [2, 2048, 2048])
        self.dn_w_in = ei("dn_w_in", [2, D, 2 * KHL * 128 + 2 * VHL * 128 + 4 * VHL])
        self.dn_conv_w = ei("dn_conv_w", [2, 5, 2 * KHL * 128 + VHL * 128])
        self.dn_a_log_fwd = ei("dn_a_log_fwd", [2, VHL])
        self.dn_dt_bias_fwd = ei("dn_dt_bias_fwd", [2, VHL])
        self.dn_a_log_bwd = ei("dn_a_log_bwd", [2, VHL])
        self.dn_dt_bias_bwd = ei("dn_dt_bias_bwd", [2, VHL])
        self.dn_norm_g = ei("dn_norm_g", [2, 128])
        self.dn_w_out = ei("dn_w_out", [2, 4096, 2048])
        self.final_norm_g = ei("final_norm_g", [1, D])
        self.cst = ei("cst", [128, C_END])

        def scr(n, s, dt):
            return nc.dram_tensor(n, s, dt, kind=("ExternalOutput" if n in self.dbg else "Internal"))
        self.xs = scr("xs", [Tm, D], F32)
        self.hT = scr("hT", [nb, 128, 16, 512], BF16)
        self.y = scr("ysc", [nt, 128, 2048], F32)
        self.qT = scr("qT", [2 * HL, nb, 128, 512], BF16)
        self.kT = scr("kT", [2 * HL, nb, 128, 512], BF16)
        self.va = scr("va", [nt, 128, HL, VA], BF16)
        self.sg = scr("sg", [nt, 128, HL * 256], F32)
        self.pcs = scr("pcs", [2 * KHL + VHL, 128, Tm], F32)
        self.qn = scr("qn", [nt, 128, KHL, 128], BF16)
        self.kn = scr("kn", [nt, 128, KHL, 128], BF16)
        self.ktok = scr("ktok", [nt, 128, KHL * 128], BF16)
        self.vtok = scr("vtok", [nt, 128, VHL * 128], BF16)
        self.sz = scr("sz", [nt, 128, VHL * 128], F32)
        self.gb = scr("gb", [nt, 128, 4 * VHL], F32)
        self.of = scr("of", [nt, 128, VHL * 128], F32)
        self.og_loc = {}
        self.og_all = {}
        self.og_bpc = {"a": max(1, 1024 // (128 * 2 * HL)), "g": max(1, 1024 // (128 * VHL))}
        for si, T in enumerate(seqs):
            nb_ = T // 512
            for kind, kl in (("a", 2 * HL), ("g", VHL)):
                bpc = self.og_bpc[kind]
                nch = -(-nb_ // bpc)
                szs = [min(bpc, nb_ - c * bpc) * 128 * kl for c in range(nch)]
                self.og_loc[si, kind] = [scr("ogl_%d%s%d" % (si, kind, c), [szs[c], 512], BF16) for c in range(nch)]
                self.og_all[si, kind] = [scr("oga_%d%s%d" % (si, kind, c), [NCORE * szs[c], 512], BF16) for c in range(nch)]

    def og_loc_blk(self, si, kind, kl):
        bpc = self.og_bpc[kind]

        def f(b):
            c, bl = divmod(b, bpc)
            return self.og_loc[si, kind][c].ap().rearrange("(b p k) t -> b p k t", p=128, k=kl)[bl]
        return f

    def og_all_blk(self, si, kind, kl):
        bpc = self.og_bpc[kind]

        def f(r, b):
            c, bl = divmod(b, bpc)
            return self.og_all[si, kind][c].ap().rearrange("(r b p k) t -> r b p k t", r=NCORE, p=128, k=kl)[r, bl]
        return f

    def load_const_bf16(self, tr, name, col, n=128):
        st = tr.sb(name + "_f", [128, n], F32)
        tr.dma("sp", st[:], self.cst[:, col:col + n], writes=[st])
        b = tr.sb(name, [128, n], BF16)
        tr.cp("dve", b[:], st[:], [st], [b])
        return b

    def load_const_f32(self, tr, name, col, n=128):
        st = tr.sb(name, [128, n], F32)
        tr.dma("sp", st[:], self.cst[:, col:col + n], writes=[st])
        return st

    def resnorm(self, tr, T, x_src, y_src, x_dst, g_row, final, out):
        nt = T // 128
        with tr.phase():
            gt = tr.sb("gt", [128, D], F32)
            tr.dma("sp", gt[:], bc_rows(g_row), writes=[gt])
            ident = self.load_const_bf16(tr, "ident", C_IDENT)
            xts = [tr.sb("xt%d" % i, [128, D], F32) for i in range(2)]
            yts = [tr.sb("yt%d" % i, [128, D], F32) for i in range(2)]
            sq = tr.sb("sq", [128, D], F32)
            ssq = tr.sb("ssq", [128, 2], F32)
            hbs = [tr.sb("hb%d" % i, [128, D], BF16) for i in range(2)]
            hfs = [tr.sb("hf%d" % i, [128, D], F32) for i in range(2)] if final else None
            hTt = [tr.sb("hTt%d" % i, [128, 16, 512], BF16) for i in range(2)]
            for tt in range(nt):
                xt = xts[tt % 2]
                tr.dma("sp", xt[:], x_src[tt * 128:(tt + 1) * 128, :], writes=[xt])
                if y_src is not None:
                    yt = yts[tt % 2]
                    tr.dma("sp", yt[:], y_src[tt], writes=[yt])
                    tr.tt("dve", xt[:], xt[:], yt[:], ALU.add, [xt, yt], [xt])
                if x_dst is not None:
                    tr.dma("pool", x_dst[tt * 128:(tt + 1) * 128, :], xt[:], reads=[xt])
                if RES_ACCUM:
                    tr.ms("pool", ssq[:, 0:1], 0.0, [ssq])
                    tr.op("act", lambda e, o=sq[:], i=xt[:], a=ssq[:, 0:1]: e.activation(out=o, in_=i, func=AF.Square, accum_out=a),
                          [xt, ssq], [sq, ssq])
                else:
                    tr.act(sq[:], xt[:], AF.Square, [xt], [sq])
                    tr.red("dve", ssq[:, 0:1], sq[:], [sq], [ssq])
                tr.act(ssq[:, 1:2], ssq[:, 0:1], AF.Sqrt, [ssq], [ssq], bias=RMS_EPS, scale=1.0 / D)
                tr.rcp(ssq[:, 1:2], ssq[:, 1:2], [ssq], [ssq])
                if final:
                    hf = hfs[tt % 2]
                    tr.stt("dve", hf[:], xt[:], ssq[:, 1:2], gt[:], ALU.mult, ALU.mult, [xt, ssq, gt], [hf])
                    tr.dma("pool", out[tt * 128:(tt + 1) * 128, :], hf[:], reads=[hf])
                    continue
                hb = hbs[tt % 2]
                tr.stt("dve", hb[:], xt[:], ssq[:, 1:2], gt[:], ALU.mult, ALU.mult, [xt, ssq, gt], [hb])
                b, s = tt // 4, tt % 4
                ht = hTt[b % 2]
                for half in range(2):
                    pb = tr.bank()
                    pv = pb[:].bitcast(BF16)
                    for k8 in range(8):
                        kc = half * 8 + k8
                        tr.tp(pv[:, k8 * 128:(k8 + 1) * 128], hb[:, kc * 128:(kc + 1) * 128], ident[:], [hb, ident], [pb])
                    tr.cp("act", ht[:, half * 8:(half + 1) * 8, s * 128:(s + 1) * 128],
                          pv.rearrange("p (a b) -> p a b", b=128), [pb], [ht])
                if s == 3:
                    tr.dma("pool", out[b], ht[:], reads=[ht])

    def proj(self, tr, T, src_loader, nk, w_ap, wc, mode, evac_factory):
        nb = T // 512
        with tr.phase():
            wt = tr.sb("wt", [128, nk, wc], BF16)
            stg = [tr.sb("wstg%d" % i, [128, wc], F32) for i in range(2)]
            for kc in range(nk):
                st = stg[kc % 2]
                tr.dma("sp", st[:], w_ap[kc * 128:(kc + 1) * 128, :], writes=[st])
                if kc % 2:
                    tr.cp("act", wt[:, kc, :], st[:], [st], [wt])
                else:
                    tr.cp("dve", wt[:, kc, :], st[:], [st], [wt])
            sbt = [tr.sb("psrc%d" % i, [128, nk, 512], BF16) for i in range(2)]
            evac = evac_factory(tr)
            for b in range(nb):
                s = sbt[b % 2]
                src_loader(tr, b, s)
                if mode == "feat":
                    for ct in range(wc // 128):
                        pb = tr.bank()
                        for kc in range(nk):
                            tr.mm(pb[:], wt[:, kc, ct * 128:(ct + 1) * 128], s[:, kc, :], kc == 0, kc == nk - 1, [wt, s], [pb])
                        evac(b, ct, pb)
                else:
                    n = min(wc, 512)
                    for sub in range(4):
                        for cg in range(max(1, wc // 512)):
                            pb = tr.bank()
                            for kc in range(nk):
                                tr.mm(pb[:, 0:n], s[:, kc, sub * 128:(sub + 1) * 128], wt[:, kc, cg * n:(cg + 1) * n],
                                      kc == 0, kc == nk - 1, [wt, s], [pb])
                            evac(b, sub, cg, pb)

    def src_hT(self):
        def ld(tr, b, s):
            tr.dma("sp", s[:], self.hT[b], writes=[s])
        return ld

    def src_gathered(self, si, kind, kl):
        v = self.og_all_blk(si, kind, kl)

        def ld(tr, b, s):
            for r in range(NCORE):
                tr.dma("sp", s[:, r * kl:(r + 1) * kl, :], v(r, b), writes=[s])
        return ld

    def ev_feat_bf16(self, dst, scale):
        def fac(tr):
            stg = [tr.sb("evq%d" % i, [128, 512], BF16) for i in range(3)]
            cnt = [0]

            def ev(b, ct, pb):
                st = stg[cnt[0] % 3]
                cnt[0] += 1
                if cnt[0] % 2:
                    tr.act(st[:], pb[:], AF.Copy, [pb], [st], scale=scale)
                else:
                    tr.ts("dve", st[:], pb[:], scale, None, ALU.mult, None, [pb], [st])
                tr.dma("pool", dst[ct][b], st[:], reads=[st])
            return ev
        return fac

    def ev_v(self):
        def fac(tr):
            stg = [tr.sb("evv%d" % i, [128, HL, VA], BF16) for i in range(2)]
            for st in stg:
                tr.ms("dve", st[:], 1.0, [st])

            def ev(b, sub, cg, pb):
                tt = b * 4 + sub
                st = stg[tt % 2]
                src = pb[:].rearrange("p (h e) -> p h e", e=256)
                tr.cp("act", st[:, 0:HL, 0:256], src, [pb], [st])
                tr.dma("pool", self.va[tt], st[:], reads=[st])
            return ev
        return fac

    def ev_tok_f32(self, dst, col0, ncg, func, wcg=512):
        def fac(tr):
            stg = [tr.sb("evt%d" % i, [128, ncg * wcg], F32) for i in range(2)]

            def ev(b, sub, cg, pb):
                tt = b * 4 + sub
                st = stg[tt % 2]
                if func is not None:
                    tr.act(st[:, cg * wcg:(cg + 1) * wcg], pb[:, 0:wcg], func, [pb], [st])
                elif cg % 2:
                    tr.cp("act", st[:, cg * wcg:(cg + 1) * wcg], pb[:, 0:wcg], [pb], [st])
                else:
                    tr.cp("dve", st[:, cg * wcg:(cg + 1) * wcg], pb[:, 0:wcg], [pb], [st])
                if cg == ncg - 1:
                    tr.dma("pool", dst[tt][:, col0:col0 + ncg * wcg], st[:], reads=[st])
            return ev
        return fac

    def ev_pc(self, ct0):
        def fac(tr):
            stg = [tr.sb("evp%d" % i, [128, 512], F32) for i in range(3)]
            cnt = [0]

            def ev(b, ct, pb):
                st = stg[cnt[0] % 3]
                cnt[0] += 1
                tr.cp("act" if cnt[0] % 2 else "dve", st[:], pb[:], [pb], [st])
                tr.dma("pool", self.pcs[ct0 + ct][:, b * 512:(b + 1) * 512], st[:], reads=[st])
            return ev
        return fac

    def ev_gates(self, j):
        H = VHL

        def fac(tr):
            dtb = tr.sb("dtb", [128, 2, H], F32)
            nA = tr.sb("nA", [128, 2, H], F32)
            tr.dma("sp", dtb[:, 0, :], bc_rows(self.dn_dt_bias_fwd[j:j + 1, :]), writes=[dtb])
            tr.dma("sp", dtb[:, 1, :], bc_rows(self.dn_dt_bias_bwd[j:j + 1, :]), writes=[dtb])
            tr.dma("sp", nA[:, 0, :], bc_rows(self.dn_a_log_fwd[j:j + 1, :]), writes=[nA])
            tr.dma("sp", nA[:, 1, :], bc_rows(self.dn_a_log_bwd[j:j + 1, :]), writes=[nA])
            tr.act(nA[:], nA[:], AF.Exp, [nA], [nA])
            tr.ts("dve", nA[:], nA[:], -1.0, None, ALU.mult, None, [nA], [nA])
            xa = tr.sb("g_xa", [128, 2, H], F32)
            ax = tr.sb("g_ax", [128, 2, H], F32)
            outs = [tr.sb("g_out%d" % i, [128, 4, H], F32) for i in range(2)]

            def ev(b, sub, cg, pb):
                tt = b * 4 + sub
                o = outs[tt % 2]
                pv = pb[:, 0:4 * H].rearrange("p (a b) -> p a b", b=H)
                for d in range(2):
                    tr.tt("dve", xa[:, d, :], pv[:, 2 * d, :], dtb[:, d, :], ALU.add, [pb, dtb], [xa])
                tr.act(ax[:], xa[:], AF.Abs, [xa], [ax])
                tr.act(ax[:], ax[:], AF.Exp, [ax], [ax], scale=-1.0)
                tr.act(ax[:], ax[:], AF.Ln, [ax], [ax], bias=1.0)
                tr.ts("dve", xa[:], xa[:], 0.0, None, ALU.max, None, [xa], [xa])
                tr.tt("dve", xa[:], xa[:], ax[:], ALU.add, [xa, ax], [xa])
                for d in range(2):
                    tr.tt("dve", o[:, 2 * d, :], xa[:, d, :], nA[:, d, :], ALU.mult, [xa, nA], [o])
                    tr.act(o[:, 2 * d + 1, :], pv[:, 2 * d + 1, :], AF.Sigmoid, [pb], [o])
                tr.dma("pool", self.gb[tt], o[:].rearrange("p a b -> p (a b)"), reads=[o])
            return ev
        return fac

    def attn_core(self, tr, T, j, lam_init, og_view):
        nt = T // 128
        nq = T // 256
        with tr.phase():
            ident = self.load_const_bf16(tr, "ident", C_IDENT)
            Jt = self.load_const_f32(tr, "Jt", C_J, 256)
            Ja = self.load_const_f32(tr, "Ja", C_JABS, 512)
            cb = self.load_const_f32(tr, "cb", C_CB, HL * NCB)
            slp = self.load_const_f32(tr, "slp", C_SLP, 4)
            lv = tr.sb("lv", [128, 512], F32)
            tr.dma("sp", lv[:], bc_rows(self.attn_lambda[j:j + 1, :]), writes=[lv])
            lp = tr.sb("lp", [128, 256], F32)
            ls = tr.sb("ls", [128, 4], F32)
            tr.tt("dve", lp[:, 0:128], lv[:, 0:128], lv[:, 128:256], ALU.mult, [lv], [lp])
            tr.tt("dve", lp[:, 128:256], lv[:, 256:384], lv[:, 384:512], ALU.mult, [lv], [lp])
            tr.red("dve", ls[:, 0:2], lp[:].rearrange("p (a b) -> p a b", b=128), [lp], [ls])
            tr.act(ls[:, 0:2], ls[:, 0:2], AF.Exp, [ls], [ls])
            tr.tt("dve", ls[:, 2:3], ls[:, 1:2], ls[:, 0:1], ALU.subtract, [ls], [ls])
            tr.ts("dve", ls[:, 3:4], ls[:, 2:3], -lam_init, None, ALU.add, None, [ls], [ls])
            sgn = tr.sb("sgn", [128, 256], F32)
            tr.dma("sp", sgn[:], bc_rows(self.attn_subln_g[j:j + 1, :]), writes=[sgn])
            tr.ts("dve", sgn[:], sgn[:], 1.0 - lam_init, None, ALU.mult, None, [sgn], [sgn])

            qts = [tr.sb("aq%d" % i, [128, 2, 256], BF16) for i in range(2)]
            sgts = [tr.sb("asg%d" % i, [128, 2, 256], F32) for i in range(2)]
            kts = [tr.sb("ak%d" % i, [128, 2, 512], BF16) for i in range(4)]
            vts = [tr.sb("av%d" % i, [128, 4, VA], BF16) for i in range(4)]
            tbs = [tr.sb("atb%d" % i, [128, 512], F32) for i in range(4)]
            pbs = [tr.sb("apb%d" % i, [128, 512], BF16) for i in range(4)]
            om = tr.sb("aom", [128, 2, 2, 256], F32)
            rden = tr.sb("arden", [128, 4], F32)
            oc = tr.sb("aoc", [128, 2, 256], F32)
            osq = tr.sb("aosq", [128, 2, 256], F32)
            ost = tr.sb("aost", [128, 4], F32)
            ogb = tr.sb("aogb", [128, 2, 256], BF16)
            ogTt = [tr.sb("aogT%d" % i, [128, 2, 256], BF16) for i in range(2)]
            acc = tr.P[0:4]
            sc = tr.P[4:8]
            tpb = tr.P[4:5]
            nblk = 0
            it = 0
            for h in range(HL):
                W = ATT_W[h]
                mneg = slp[:, 2 * h:2 * h + 1]
                mpos = slp[:, 2 * h + 1:2 * h + 2]
                for qt in range(nq):
                    q0 = qt * 256
                    qtile = qts[it % 2]
                    sgt = sgts[it % 2]
                    ogt = ogTt[it % 2]
                    it += 1
                    qb, qo = qt // 2, (qt % 2) * 256
                    for mp in range(2):
                        tr.dma("sp", qtile[:, mp, :], self.qT[2 * h + mp][qb][:, qo:qo + 256], writes=[qtile])
                    tr.dma("sp", sgt[:], self.sg[qt * 2:qt * 2 + 2, :, h * 256:(h + 1) * 256].rearrange("s p e -> p s e"),
                           writes=[sgt])
                    kt_lo = max(0, (q0 - W) // 128)
                    kt_hi = min(nt, -((-(q0 + 256 + W)) // 128))
                    kts_list = list(range(kt_lo, kt_hi))
                    nk_ = len(kts_list)
                    info = {}
                    cur = {"kb": -1, "kt": None, "vt": None}

                    def issue_qk(idx):
                        nonlocal nblk
                        kt = kts_list[idx]
                        kb, ks = kt // 4, kt % 4
                        if kb != cur["kb"]:
                            cur["kb"] = kb
                            cur["kt"] = kts[nblk % 4]
                            cur["vt"] = vts[nblk % 4]
                            nblk += 1
                            for mp in range(2):
                                tr.dma("sp", cur["kt"][:, mp, :], self.kT[2 * h + mp][kb], writes=[cur["kt"]])
                            tr.dma("sp", cur["vt"][:], self.va[kb * 4:kb * 4 + 4, :, h, :].rearrange("s p e -> p s e"),
                                   writes=[cur["vt"]])
                        ktile, vtile = cur["kt"], cur["vt"]
                        sb_ = sc[idx % 4]
                        for mp in range(2):
                            tr.mm(sb_[:, mp * 256:(mp + 1) * 256], ktile[:, mp, ks * 128:(ks + 1) * 128], qtile[:, mp, :],
                                  True, True, [ktile, qtile], [sb_])
                        info[idx] = (vtile, ks, kt * 128, sb_)

                    def issue_soft(idx):
                        vtile, ks, k0, sb_ = info[idx]
                        tb = tbs[idx % 4]
                        pb = pbs[idx % 4]
                        sv = sb_[:].rearrange("p (a b) -> p a b", b=256)
                        tv = tb[:].rearrange("p (a b) -> p a b", b=256)
                        if k0 + 127 < q0:
                            jv = Jt[:].unsqueeze(1).to_broadcast([128, 2, 256])
                            tr.stt("dve", tv, jv, mneg, sv, ALU.mult, ALU.add, [Jt, sb_, slp], [tb])
                            ci = (q0 - k0) // 128
                        elif k0 > q0 + 255:
                            jv = Jt[:].unsqueeze(1).to_broadcast([128, 2, 256])
                            tr.stt("dve", tv, jv, mpos, sv, ALU.mult, ALU.add, [Jt, sb_, slp], [tb])
                            ci = (k0 - q0) // 128
                        else:
                            i_ = (k0 - q0) // 128
                            jv = Ja[:, i_ * 256:(i_ + 1) * 256].unsqueeze(1).to_broadcast([128, 2, 256])
                            tr.stt("dve", tv, jv, mneg, sv, ALU.mult, ALU.add, [Ja, sb_, slp], [tb])
                            ci = 0
                        tr.act(pb[:], tb[:], AF.Exp, [tb, cb], [pb], bias=cb[:, h * NCB + ci:h * NCB + ci + 1])

                    def issue_av(idx):
                        vtile, ks, k0, sb_ = info.pop(idx)
                        pb = pbs[idx % 4]
                        first, last = idx == 0, idx == nk_ - 1
                        for mp in range(2):
                            for qs in range(2):
                                a = acc[mp * 2 + qs]
                                tr.mm(a[:, 0:257], pb[:, mp * 256 + qs * 128:mp * 256 + (qs + 1) * 128], vtile[:, ks, 0:257],
                                      first, last, [pb, vtile], [a])

                    LOOK = 3
                    for idx in range(min(LOOK, nk_)):
                        issue_qk(idx)
                    for idx in range(nk_):
                        issue_soft(idx)
                        if idx + LOOK < nk_:
                            issue_qk(idx + LOOK)
                        issue_av(idx)
                    for mp in range(2):
                        for qs in range(2):
                            a = acc[mp * 2 + qs]
                            c_ = mp * 2 + qs
                            tr.rcp(rden[:, c_:c_ + 1], a[:, 256:257], [a], [rden])
                            if qs:
                                tr.act(om[:, mp, qs, :], a[:, 0:256], AF.Copy, [a, rden], [om], scale=rden[:, c_:c_ + 1])
                            else:
                                tr.ts("dve", om[:, mp, qs, :], a[:, 0:256], rden[:, c_:c_ + 1], None, ALU.mult, None, [a, rden], [om])
                    tr.stt("dve", oc[:], om[:, 1, :, :], ls[:, 3:4], om[:, 0, :, :], ALU.mult, ALU.add, [om, ls], [oc])
                    tr.act(osq[:], oc[:], AF.Square, [oc], [osq])
                    tr.red("dve", ost[:, 0:2], osq[:], [osq], [ost])
                    tr.act(ost[:, 2:4], ost[:, 0:2], AF.Sqrt, [ost], [ost], bias=RMS_EPS, scale=1.0 / 256)
                    tr.rcp(ost[:, 2:4], ost[:, 2:4], [ost], [ost])
                    tr.tt("dve", oc[:], oc[:], ost[:, 2:4].unsqueeze(2).to_broadcast([128, 2, 256]), ALU.mult, [oc, ost], [oc])
                    tr.tt("pool", oc[:], oc[:], sgn[:].unsqueeze(1).to_broadcast([128, 2, 256]), ALU.mult, [oc, sgn], [oc])
                    tr.tt("dve", ogb[:], oc[:], sgt[:], ALU.mult, [oc, sgt], [ogb])
                    tb_ = tpb[0]
                    tv_ = tb_[:].bitcast(BF16)
                    for ec in range(2):
                        for qs in range(2):
                            c_ = (ec * 2 + qs) * 128
                            tr.tp(tv_[:, c_:c_ + 128], ogb[:, qs, ec * 128:(ec + 1) * 128], ident[:], [ogb, ident], [tb_])
                    tr.cp("act", ogt[:], tv_[:, 0:512].rearrange("p (a b) -> p a b", b=256), [tb_], [ogt])
                    tr.dma("pool", og_view(qb)[:, 2 * h:2 * h + 2, qo:qo + 256], ogt[:], reads=[ogt])

    def gdn_conv(self, tr, T, j):
        nb = T // 512
        NCT = 2 * KHL + VHL
        with tr.phase():
            ident = self.load_const_bf16(tr, "ident", C_IDENT)
            identf = self.load_const_f32(tr, "identf", C_IDENT)
            onesf = self.load_const_f32(tr, "onesf", C_ONES)
            cwr = tr.sb("cwr", [5, NCT * 128], F32)
            tr.dma("sp", cwr[:], self.dn_conv_w[j], writes=[cwr])
            cw = tr.sb("cw", [128, NCT, 5], F32)
            for ct in range(NCT):
                pb = tr.bank()
                tr.tp(pb[:, 0:5], cwr[0:5, ct * 128:(ct + 1) * 128], identf[0:5, 0:5], [cwr, identf], [pb])
                tr.cp("dve", cw[:, ct, :], pb[:, 0:5], [pb], [cw])
            accs = [tr.sb("ca%d" % i, [128, 512], F32) for i in range(2)]
            xins = [tr.sb("cx%d" % i, [128, 516], F32) for i in range(3)]
            sxs = [tr.sb("cs%d" % i, [128, 512], F32) for i in range(2)]
            sqs = [tr.sb("cq%d" % i, [128, 512], F32) for i in range(2)]
            rns = [tr.sb("cr%d" % i, [128, 512], F32) for i in range(2)]
            xnbs = [tr.sb("cn%d" % i, [128, 512], BF16) for i in range(2)]
            tks = [tr.sb("ct%d" % i, [128, 4, 128], BF16) for i in range(2)]
            it = 0
            for ct in range(NCT):
                for b in range(nb):
                    xin = xins[it % 3]
                    sx = sxs[it % 2]
                    sq = sqs[it % 2]
                    rn = rns[it % 2]
                    xnb = xnbs[it % 2]
                    tk = tks[it % 2]
                    it += 1
                    lo = b * 512 - 2
                    hi = b * 512 + 514
                    c0, c1 = 0, 516
                    if b == 0:
                        tr.ms("dve", xin[:, 0:2], 0.0, [xin])
                        lo, c0 = 0, 2
                    if b == nb - 1:
                        tr.ms("dve", xin[:, 514:516], 0.0, [xin])
                        hi, c1 = T, 514
                    tr.dma("sp", xin[:, c0:c1], self.pcs[ct][:, lo:hi], writes=[xin])
                    acc = accs[it % 2]
                    tr.ts("dve", acc[:], xin[:, 0:512], cw[:, ct, 0:1], None, ALU.mult, None, [xin, cw], [acc])
                    for tap in range(1, 5):
                        tr.stt("dve", acc[:], xin[:, tap:tap + 512], cw[:, ct, tap:tap + 1], acc[:], ALU.mult, ALU.add,
                               [xin, cw, acc], [acc])
                    tr.act(sx[:], acc[:], AF.Silu, [acc], [sx])
                    if ct < 2 * KHL:
                        kh = ct % KHL
                        tr.act(sq[:], sx[:], AF.Square, [sx], [sq])
                        pb = tr.bank()
                        tr.mm(pb[:], onesf[:], sq[:], True, True, [onesf, sq], [pb])
                        tr.act(rn[:], pb[:], AF.Sqrt, [pb], [rn], bias=L2_EPS)
                        tr.rcp(rn[:], rn[:], [rn], [rn])
                        tr.stt("dve", xnb[:], sx[:], (128 ** -0.5 if ct < KHL else 1.0), rn[:], ALU.mult, ALU.mult, [sx, rn], [xnb])
                        dst = self.qn if ct < KHL else self.kn
                        tr.dma("pool", dst[b * 4:b * 4 + 4, :, kh, :].rearrange("s p t -> p s t"),
                               xnb[:].rearrange("p (s t) -> p s t", t=128), reads=[xnb])
                        if ct < KHL:
                            continue
                    else:
                        tr.cp("act", xnb[:], sx[:], [sx], [xnb])
                    pb = tr.bank()
                    pv = pb[:].bitcast(BF16)
                    for s in range(4):
                        tr.tp(pv[:, s * 128:(s + 1) * 128], xnb[:, s * 128:(s + 1) * 128], ident[:], [xnb, ident], [pb])
                    tr.cp("act", tk[:], pv[:, 0:512].rearrange("p (s d) -> p s d", d=128), [pb], [tk])
                    if ct < 2 * KHL:
                        kh = ct - KHL
                        tr.dma("pool", self.ktok[b * 4:b * 4 + 4, :, kh * 128:(kh + 1) * 128].rearrange("s p d -> p s d"),
                               tk[:], reads=[tk])
                    else:
                        vh = ct - 2 * KHL
                        tr.dma("pool", self.vtok[b * 4:b * 4 + 4, :, vh * 128:(vh + 1) * 128].rearrange("s p d -> p s d"),
                               tk[:], reads=[tk])

    def gdn_scan(self, tr, T, j, d, og_view):
        nt = T // 128
        bwd = d == 1
        H = VHL
        with tr.phase():
            ident = self.load_const_bf16(tr, "ident", C_IDENT)
            identf = self.load_const_f32(tr, "identf", C_IDENT)
            onesf = self.load_const_f32(tr, "onesf", C_ONES)
            tri = self.load_const_f32(tr, "tri", C_TRIB if bwd else C_TRIF)
            mb4 = tr.sb("mb4", [128, 4, 128], F32)
            nmbs4 = tr.sb("nmbs4", [128, 4, 128], F32)
            for hh in range(4):
                cm = C_MBB if bwd else C_MBF
                cms = C_MBSB if bwd else C_MBSF
                tr.dma("sp", mb4[:, hh, :], self.cst[:, cm:cm + 128], writes=[mb4])
                tr.dma("sp", nmbs4[:, hh, :], self.cst[:, cms:cms + 128], writes=[nmbs4])
            tr.ts("dve", nmbs4[:], nmbs4[:], -1.0, None, ALU.mult, None, [nmbs4], [nmbs4])
            gn = tr.sb("gn", [128, 128], F32)
            tr.dma("sp", gn[:], bc_rows(self.dn_norm_g[j:j + 1, :]), writes=[gn])
            S32 = tr.sb("S32", [128, H, 128], F32)
            Sb = tr.sb("Sb", [128, H, 128], BF16)
            tr.ms("dve", S32[:], 0.0, [S32])
            tr.ms("dve", Sb[:], 0.0, [Sb])
            NB = 3
            qchs = [tr.sb("qch%d" % i, [128, KHL, 128], BF16) for i in range(NB)]
            kchs = [tr.sb("kch%d" % i, [128, KHL, 128], BF16) for i in range(NB)]
            ktks = [tr.sb("ktk%d" % i, [128, KHL, 128], BF16) for i in range(NB)]
            vtks = [tr.sb("vtk%d" % i, [128, H, 128], BF16) for i in range(NB)]
            gbts = [tr.sb("gbt%d" % i, [128, 4, H], F32) for i in range(NB)]
            if bwd:
                ofts = [tr.sb("oft%d" % i, [128, H, 128], F32) for i in range(NB)]
                szts = [tr.sb("szt%d" % i, [128, H, 128], F32) for i in range(NB)]
                ogTts = [tr.sb("sogT%d" % i, [128, H, 128], BF16) for i in range(NB)]
            else:
                osts = [tr.sb("ost%d" % i, [128, H, 128], F32) for i in range(NB)]
            gcs = [tr.sb("gc%d" % i, [128, 2 * H], F32) for i in range(NB)]
            egcs = [tr.sb("egc%d" % i, [128, 2 * H], F32) for i in range(NB)]
            dkfs = [tr.sb("dkf%d" % i, [128, H], F32) for i in range(NB)]
            bks = [tr.sb("bk%d" % i, [128, H], F32) for i in range(NB)]
            ngcs = [tr.sb("ngc%d" % i, [128, H], F32) for i in range(NB)]
            NSETS = 4
            G = []
            for g_ in range(NSETS):
                t_ = {}
                for nm, dt_ in (("gbc", F32), ("E", BF16), ("ETs", F32), ("Lb", F32), ("Ub", F32), ("Aq", BF16),
                                ("PA", F32), ("QA", F32), ("N32", F32), ("Nb", BF16), ("bv", BF16), ("bke", BF16),
                                ("kd", BF16), ("u32", F32), ("wTb", BF16), ("vnew", BF16), ("o1", F32),
                                ("ogb", BF16), ("stmp", F32)):
                    t_[nm] = tr.sb("%s_%d" % (nm, g_), [128, 4, 128], dt_)
                t_["ost"] = tr.sb("gost_%d" % g_, [128, 8], F32)
                G.append(t_)

            def b4(ap2):
                return ap2.unsqueeze(2).to_broadcast([128, 4, 128])

            def fl(buf):
                return buf[:].rearrange("p a b -> p (a b)")

            order = range(nt - 1, -1, -1) if bwd else range(nt)
            live = []
            setno = [0]

            def finish(cx):
                cx["remaining"] -= 1
                if cx["remaining"]:
                    return
                n_ = cx["n"]
                if bwd:
                    tr.dma("pool", og_view(n_ // 4)[:, :, (n_ % 4) * 128:(n_ % 4 + 1) * 128], cx["ogTt"][:], reads=[cx["ogTt"]])
                else:
                    tr.dma("pool", self.of[n_], fl(cx["ostg"]), reads=[cx["ostg"]])

            for ci, n in enumerate(order):
                qch, kch, ktk, vtk, gbt = qchs[ci % NB], kchs[ci % NB], ktks[ci % NB], vtks[ci % NB], gbts[ci % NB]
                gc, egc, dkf, bk, ngc = gcs[ci % NB], egcs[ci % NB], dkfs[ci % NB], bks[ci % NB], ngcs[ci % NB]
                cx = {"n": n, "remaining": H // 4}
                tr.dma("sp", qch[:], self.qn[n], writes=[qch])
                tr.dma("sp", kch[:], self.kn[n], writes=[kch])
                tr.dma("sp", fl(ktk), self.ktok[n], writes=[ktk])
                tr.dma("sp", fl(vtk), self.vtok[n], writes=[vtk])
                tr.dma("sp", fl(gbt), self.gb[n], writes=[gbt])
                if bwd:
                    oft, szt, ogTt = ofts[ci % NB], szts[ci % NB], ogTts[ci % NB]
                    cx["ogTt"] = ogTt
                    tr.dma("sp", fl(oft), self.of[n], writes=[oft])
                    tr.dma("sp", fl(szt), self.sz[n], writes=[szt])
                else:
                    ostg = osts[ci % NB]
                    cx["ostg"] = ostg
                graw = gbt[:, 2 * d, :]
                beta = gbt[:, 2 * d + 1, :]
                pb = tr.bank()
                tr.mm(pb[:, 0:H], tri[:], graw, True, True, [tri, gbt], [pb])
                tr.mm(pb[:, H:2 * H], onesf[:], graw, True, True, [onesf, gbt], [pb])
                tr.cp("dve", gc[:], pb[:, 0:2 * H], [pb], [gc])
                tr.act(egc[:], gc[:], AF.Exp, [gc], [egc])
                tr.ts("dve", ngc[:], gc[:, 0:H], -1.0, None, ALU.mult, None, [gc], [ngc])
                tr.tt("dve", dkf[:], gc[:, H:2 * H], gc[:, 0:H], ALU.subtract, [gc], [dkf])
                tr.act(dkf[:], dkf[:], AF.Exp, [dkf], [dkf])
                tr.tt("dve", bk[:], beta, egc[:, 0:H], ALU.mult, [gbt, egc], [bk])

                def group(gq, t_, cx=cx, qch=qch, kch=kch, ktk=ktk, vtk=vtk, gbt=gbt, gc=gc, egc=egc, dkf=dkf, bk=bk, ngc=ngc,
                          graw=graw, beta=beta, oft=(oft if bwd else None), szt=(szt if bwd else None),
                          ogTt=(ogTt if bwd else None), ostg=(None if bwd else ostg)):
                    gbc, E, ETs, Lb, Ub, Aq = t_["gbc"], t_["E"], t_["ETs"], t_["Lb"], t_["Ub"], t_["Aq"]
                    N32, Nb, bv, bke, kd = t_["N32"], t_["Nb"], t_["bv"], t_["bke"], t_["kd"]
                    u32, wTb, vnew, o1, ogb, stmp, ost = (t_["u32"], t_["wTb"], t_["vnew"], t_["o1"], t_["ogb"], t_["stmp"],
                                                          t_["ost"])
                    ot = o1
                    osq = stmp
                    h0 = 4 * gq
                    kh0 = 2 * gq
                    tr.tt("dve", gbc[:], identf[:].unsqueeze(1).to_broadcast([128, 4, 128]), b4(gc[:, h0:h0 + 4]), ALU.mult,
                          [identf, gc], [gbc])
                    pA = tr.bank()
                    pB = tr.bank()
                    tr.mm(pA[:], onesf[:], fl(gbc), True, False, [onesf, gbc], [pA])
                    tr.mm(pA[:], identf[:], fl(mb4), False, True, [identf, mb4], [pA])
                    tr.mm(pB[:], onesf[:], fl(gbc), True, False, [onesf, gbc], [pB])
                    tr.mm(pB[:], identf[:], fl(nmbs4), False, True, [identf, nmbs4], [pB])
                    for hh in range(4):
                        cs = slice(hh * 128, (hh + 1) * 128)
                        tr.act(E[:, hh, :], pA[:, cs], AF.Exp, [pA, ngc], [E], bias=ngc[:, h0 + hh:h0 + hh + 1])
                        tr.act(ETs[:, hh, :], pB[:, cs], AF.Exp, [pB, gc], [ETs], bias=gc[:, h0 + hh:h0 + hh + 1], scale=-1.0)
                    pC = tr.bank()
                    for i_ in range(2):
                        tr.mm(pC[:, i_ * 128:(i_ + 1) * 128], kch[:, kh0 + i_, :], kch[:, kh0 + i_, :], True, True, [kch], [pC])
                        tr.mm(pC[:, (2 + i_) * 128:(3 + i_) * 128], kch[:, kh0 + i_, :], qch[:, kh0 + i_, :], True, True,
                              [kch, qch], [pC])
                    for hh in range(4):
                        i_ = hh // 2
                        tr.stt("dve", Lb[:, hh, :], pC[:, i_ * 128:(i_ + 1) * 128], beta[:, h0 + hh:h0 + hh + 1], ETs[:, hh, :],
                               ALU.mult, ALU.mult, [pC, gbt, ETs], [Lb])
                    for i_ in range(2):
                        tr.tt("dve", Aq[:, 2 * i_:2 * i_ + 2, :],
                              pC[:, (2 + i_) * 128:(3 + i_) * 128].unsqueeze(1).to_broadcast([128, 2, 128]),
                              E[:, 2 * i_:2 * i_ + 2, :], ALU.mult, [pC, E], [Aq])
                    yield
                    pT = tr.bank()
                    for hh in range(4):
                        tr.tp(pT[:, hh * 128:(hh + 1) * 128], Lb[:, hh, :], identf[:], [Lb, identf], [pT])
                    tr.cp("act", fl(Ub), pT[:], [pT], [Ub])
                    tr.tt("dve", N32[:], identf[:].unsqueeze(1).to_broadcast([128, 4, 128]), Ub[:], ALU.subtract,
                          [identf, Ub], [N32])
                    yield
                    Pc, Qc = Ub, Lb
                    for k in range(1, 7):
                        if k % 2:
                            Qn, Pn = t_["QA"], t_["PA"]
                        else:
                            Qn, Pn = Lb, Ub
                        pX = tr.bank()
                        for hh in range(4):
                            tr.mm(pX[:, hh * 128:(hh + 1) * 128], Pc[:, hh, :], Qc[:, hh, :], True, True, [Pc, Qc], [pX])
                        if k < 6:
                            pY = tr.bank()
                            for hh in range(4):
                                tr.mm(pY[:, hh * 128:(hh + 1) * 128], Qc[:, hh, :], Pc[:, hh, :], True, True, [Pc, Qc], [pY])
                        tr.cp("act", fl(Qn), pX[:], [pX], [Qn])
                        if k < 6:
                            tr.cp("dve", fl(Pn), pY[:], [pY], [Pn])
                        yield
                        pZ = tr.bank()
                        for hh in range(4):
                            tr.mm(pZ[:, hh * 128:(hh + 1) * 128], Qn[:, hh, :], N32[:, hh, :], True, True, [Qn, N32], [pZ])
                        tr.tt("dve", fl(N32), fl(N32), pZ[:], ALU.add, [N32, pZ], [N32])
                        Pc, Qc = Pn, Qn
                        yield
                    tr.cp("act", Nb[:], N32[:], [N32], [Nb])
                    tr.tt("dve", bv[:], vtk[:, h0:h0 + 4, :], b4(beta[:, h0:h0 + 4]), ALU.mult, [vtk, gbt], [bv])
                    for i_ in range(2):
                        kx = ktk[:, kh0 + i_, :].unsqueeze(1).to_broadcast([128, 2, 128])
                        hs = slice(h0 + 2 * i_, h0 + 2 * i_ + 2)
                        tr.tt("pool", bke[:, 2 * i_:2 * i_ + 2, :], kx, bk[:, hs].unsqueeze(2).to_broadcast([128, 2, 128]),
                              ALU.mult, [ktk, bk], [bke])
                        tr.tt("pool", kd[:, 2 * i_:2 * i_ + 2, :], kx, dkf[:, hs].unsqueeze(2).to_broadcast([128, 2, 128]),
                              ALU.mult, [ktk, dkf], [kd])
                    yield
                    pU = tr.bank()
                    pW = tr.bank()
                    for hh in range(4):
                        cs = slice(hh * 128, (hh + 1) * 128)
                        tr.mm(pU[:, cs], Nb[:, hh, :], bv[:, hh, :], True, True, [Nb, bv], [pU])
                        tr.mm(pW[:, cs], bke[:, hh, :], Nb[:, hh, :], True, True, [Nb, bke], [pW])
                    tr.cp("act", fl(u32), pU[:], [pU], [u32])
                    tr.cp("dve", fl(wTb), pW[:], [pW], [wTb])
                    yield
                    pA2 = tr.bank()
                    for hh in range(4):
                        tr.mm(pA2[:, hh * 128:(hh + 1) * 128], wTb[:, hh, :], Sb[:, h0 + hh, :], True, True, [wTb, Sb], [pA2])
                    tr.stt("dve", fl(vnew), pA2[:], -1.0, fl(u32), ALU.mult, ALU.add, [pA2, u32], [vnew])
                    yield
                    pB1 = tr.bank()
                    pB2 = tr.bank()
                    pC2 = tr.bank()
                    for hh in range(4):
                        cs = slice(hh * 128, (hh + 1) * 128)
                        tr.mm(pB1[:, cs], qch[:, kh0 + hh // 2, :], Sb[:, h0 + hh, :], True, True, [qch, Sb], [pB1])
                        tr.mm(pB2[:, cs], Aq[:, hh, :], vnew[:, hh, :], True, True, [Aq, vnew], [pB2])
                        tr.mm(pC2[:, cs], kd[:, hh, :], vnew[:, hh, :], True, True, [kd, vnew], [pC2])
                    tr.tt("dve", o1[:], pB1[:].rearrange("p (a b) -> p a b", b=128), b4(egc[:, h0:h0 + 4]), ALU.mult,
                          [pB1, egc], [o1])
                    tr.tt("pool", stmp[:], S32[:, h0:h0 + 4, :], b4(egc[:, H + h0:H + h0 + 4]), ALU.mult, [S32, egc], [stmp])
                    tr.tt("dve", S32[:, h0:h0 + 4, :], stmp[:], pC2[:].rearrange("p (a b) -> p a b", b=128), ALU.add,
                          [stmp, pC2], [S32])
                    tr.cp("act", Sb[:, h0:h0 + 4, :], S32[:, h0:h0 + 4, :], [S32], [Sb])
                    if not bwd:
                        tr.tt("dve", ostg[:, h0:h0 + 4, :], o1[:], pB2[:].rearrange("p (a b) -> p a b", b=128), ALU.add,
                              [o1, pB2], [ostg])
                        finish(cx)
                        return
                    tr.tt("dve", ot[:], o1[:], pB2[:].rearrange("p (a b) -> p a b", b=128), ALU.add, [o1, pB2], [ot])
                    yield
                    tr.tt("pool", ot[:], ot[:], oft[:, h0:h0 + 4, :], ALU.add, [ot, oft], [ot])
                    tr.act(osq[:], ot[:], AF.Square, [ot], [osq])
                    tr.red("dve", ost[:, 0:4], osq[:], [osq], [ost])
                    tr.act(ost[:, 4:8], ost[:, 0:4], AF.Sqrt, [ost], [ost], bias=RMS_EPS, scale=1.0 / 128)
                    tr.rcp(ost[:, 4:8], ost[:, 4:8], [ost], [ost])
                    tr.tt("dve", ot[:], ot[:], b4(ost[:, 4:8]), ALU.mult, [ot, ost], [ot])
                    tr.tt("pool", ot[:], ot[:], gn[:].unsqueeze(1).to_broadcast([128, 4, 128]), ALU.mult, [ot, gn], [ot])
                    tr.tt("dve", ogb[:], ot[:], szt[:, h0:h0 + 4, :], ALU.mult, [ot, szt], [ogb])
                    yield
                    pT2 = tr.bank()
                    pT2v = pT2[:].bitcast(BF16)
                    for hh in range(4):
                        tr.tp(pT2v[:, hh * 128:(hh + 1) * 128], ogb[:, hh, :], ident[:], [ogb, ident], [pT2])
                    tr.cp("act", ogTt[:, h0:h0 + 4, :].rearrange("p a b -> p (a b)"), pT2v[:, 0:512], [pT2], [ogTt])
                    finish(cx)

                for gq in range(H // 4):
                    live.append(group(gq, G[setno[0] % NSETS]))
                    setno[0] += 1
                def step_all():
                    nonlocal live
                    nxt = []
                    for g_ in live:
                        try:
                            next(g_)
                            nxt.append(g_)
                        except StopIteration:
                            pass
                    live = nxt

                if ci == 0:
                    for _ in range(9):
                        step_all()
                while len(live) > (H // 4 if ci < nt - 1 else 0):
                    step_all()

    def build(self, depth=4):
        nc = self.nc
        with contextlib.ExitStack() as es:
            tr = self.tr = TR(nc, es)
            AQ = HL * 256
            KW = KHL * 128
            VW = VHL * 128
            for si, T in enumerate(self.seqs):
                self.resnorm(tr, T, self.xin[si], None, self.xs, self.norm_g[0:1, :], False, self.hT)
                for i in range(depth):
                    j = i // 2
                    if i % 2 == 0:
                        w = self.attn_w_in[j]
                        ogv = self.og_loc_blk(si, "a", 2 * HL)
                        self.proj(tr, T, self.src_hT(), 16, w[:, 0:AQ], AQ, "feat", self.ev_feat_bf16(self.qT, 128 ** -0.5))
                        self.proj(tr, T, self.src_hT(), 16, w[:, AQ:2 * AQ], AQ, "feat", self.ev_feat_bf16(self.kT, 1.0))
                        self.proj(tr, T, self.src_hT(), 16, w[:, 2 * AQ:3 * AQ], AQ, "tok", self.ev_v())
                        self.proj(tr, T, self.src_hT(), 16, w[:, 3 * AQ:4 * AQ], AQ, "tok", self.ev_tok_f32(self.sg, 0, 1, AF.Silu))
                        self.attn_core(tr, T, j, 0.8 - 0.6 * math.exp(-0.3 * i), ogv)
                        tr.allgather(list(zip(self.og_loc[si, "a"], self.og_all[si, "a"])))
                        self.proj(tr, T, self.src_gathered(si, "a", 2 * HL), 16, self.attn_w_out[j], 2048, "tok",
                                  self.ev_tok_f32(self.y, 0, 4, None))
                    else:
                        w = self.dn_w_in[j]
                        ogv = self.og_loc_blk(si, "g", VHL)
                        self.proj(tr, T, self.src_hT(), 16, w[:, 0:KW], KW, "feat", self.ev_pc(0))
                        self.proj(tr, T, self.src_hT(), 16, w[:, KW:2 * KW], KW, "feat", self.ev_pc(KHL))
                        self.proj(tr, T, self.src_hT(), 16, w[:, 2 * KW:2 * KW + VW], VW, "feat", self.ev_pc(2 * KHL))
                        self.proj(tr, T, self.src_hT(), 16, w[:, 2 * KW + VW:2 * KW + 2 * VW], VW, "tok",
                                  self.ev_tok_f32(self.sz, 0, VW // 512, AF.Silu))
                        self.proj(tr, T, self.src_hT(), 16, w[:, 2 * KW + 2 * VW:2 * KW + 2 * VW + 4 * VHL], 4 * VHL, "tok",
                                  self.ev_gates(j))
                        import os
                        skip = os.environ.get("DBG_SKIP", "")
                        if "conv" not in skip:
                            self.gdn_conv(tr, T, j)
                        if "scanf" not in skip:
                            self.gdn_scan(tr, T, j, 0, ogv)
                        if "scanb" not in skip:
                            self.gdn_scan(tr, T, j, 1, ogv)
                        tr.allgather(list(zip(self.og_loc[si, "g"], self.og_all[si, "g"])))
                        for c2 in range(2):
                            self.proj(tr, T, self.src_gathered(si, "g", VHL), 32, self.dn_w_out[j][:, c2 * 1024:(c2 + 1) * 1024],
                                      1024, "tok", self.ev_tok_f32(self.y, c2 * 1024, 2, None))
                    last = i == depth - 1
                    if last:
                        self.resnorm(tr, T, self.xs, self.y, None, self.final_norm_g[0:1, :], True, self.yout[si])
                    else:
                        self.resnorm(tr, T, self.xs, self.y, self.xs, self.norm_g[i + 1:i + 2, :], False, self.hT)
        return nc


def core_weights(w, core):
    f = lambda a: np.ascontiguousarray(a, np.float32)
    heads = core_heads(core)
    o = {}
    awi = w["attn_w_in"]
    cols = []
    for base in (0, 2048, 4096, 6144):
        for h in heads:
            cols.append(np.arange(base + h * 256, base + (h + 1) * 256))
    o["attn_w_in"] = f(awi[:, :, np.concatenate(cols)])
    perm = np.concatenate([np.arange(h * 256, (h + 1) * 256) for r in range(NCORE) for h in core_heads(r)])
    o["attn_w_out"] = f(w["attn_w_out"][:, perm, :])
    dwi = w["dn_w_in"]
    kh = np.arange(core * KHL * 128, (core + 1) * KHL * 128)
    vh = np.arange(core * VHL * 128, (core + 1) * VHL * 128)
    ab = np.concatenate([12288 + t * 32 + np.arange(core * VHL, (core + 1) * VHL) for t in range(4)])
    o["dn_w_in"] = f(dwi[:, :, np.concatenate([kh, 2048 + kh, 4096 + vh, 8192 + vh, ab])])
    o["dn_conv_w"] = f(w["dn_conv_w"][:, :, np.concatenate([kh, 2048 + kh, 4096 + vh])])
    for k in ("dn_a_log_fwd", "dn_dt_bias_fwd", "dn_a_log_bwd", "dn_dt_bias_bwd"):
        o[k] = f(w[k][:, core * VHL:(core + 1) * VHL])
    o["dn_w_out"] = f(w["dn_w_out"])
    o["norm_g"] = f(w["norm_g"])
    o["attn_lambda"] = f(w["attn_lambda"]).reshape(2, 512)
    o["attn_subln_g"] = f(w["attn_subln_g"])
    o["dn_norm_g"] = f(w["dn_norm_g"])
    o["final_norm_g"] = f(w["final_norm_g"]).reshape(1, D)
    o["cst"] = make_consts(core)
    return o


def make_in_maps(xs, w):
    maps = []
    for c in range(NCORE):
        m = core_weights(w, c)
        for i, x in enumerate(xs):
            m["x%d" % i] = np.ascontiguousarray(x, np.float32)
        maps.append(m)
    return maps


def kernel(x_prompt, x_sample, **w):
    prog = Prog(SEQS)
    nc = prog.build()
    in_maps = make_in_maps([x_prompt[0], x_sample[0], x_sample[1]], w)
    res = run_bass_kernel_spmd(nc, in_maps, core_ids=list(range(NCORE)))
    y_prompt = np.asarray(res.results[0]["y0"], np.float32)[None]
    y_sample = np.stack([np.asarray(res.results[0]["y1"], np.float32), np.asarray(res.results[0]["y2"], np.float32)], axis=0)
    return (y_prompt, y_sample)
```

```python
import contextlib
import math
import numpy as np
import concourse.bass as bass
import concourse.mybir as mybir
from concourse.bass_utils import run_bass_kernel_spmd

F32 = mybir.dt.float32
BF16 = mybir.dt.bfloat16
AF = mybir.ActivationFunctionType
ALU = mybir.AluOpType
AX = mybir.AxisListType

D = 2048
SEQS = (16384, 4096, 4096)
NCORE = 4
HL = 2
KHL = 4
VHL = 8
ATT_W = (1280, 1 << 30)
NSLOT = 8
SAME_ENGINE_SYNC = True
INV_F32 = True
RES_ACCUM = True
ATT_THRESH = 80.0
VA = 264
RMS_EPS = 1e-6
L2_EPS = 1e-6
NEG = -30000.0

C_IDENT, C_ONES, C_TRIF, C_TRIB, C_NTRIF, C_NTRIB, C_MBF, C_MBB, C_MBSF, C_MBSB = [i * 128 for i in range(10)]
C_J = 1280
C_JABS = C_J + 256
C_CB = C_JABS + 512
NCB = 132
C_SLP = C_CB + HL * NCB
C_END = C_SLP + 4


def core_heads(core):
    return (core, 7 - core)


def make_consts(core):
    c = np.zeros((128, C_END), np.float32)
    p = np.arange(128)[:, None].astype(np.float64)
    i = np.arange(128)[None, :].astype(np.float64)
    c[:, C_IDENT:C_IDENT + 128] = (p == i)
    c[:, C_ONES:C_ONES + 128] = 1.0
    c[:, C_TRIF:C_TRIF + 128] = (p <= i)
    c[:, C_TRIB:C_TRIB + 128] = (p >= i)
    c[:, C_NTRIF:C_NTRIF + 128] = -1.0 * (p <= i)
    c[:, C_NTRIB:C_NTRIB + 128] = -1.0 * (p >= i)
    c[:, C_MBF:C_MBF + 128] = np.where(i >= p, 0.0, NEG)
    c[:, C_MBB:C_MBB + 128] = np.where(i <= p, 0.0, NEG)
    c[:, C_MBSF:C_MBSF + 128] = np.where(p > i, 0.0, NEG)
    c[:, C_MBSB:C_MBSB + 128] = np.where(p < i, 0.0, NEG)
    j = np.arange(256)[None, :].astype(np.float64)
    c[:, C_J:C_J + 256] = j - p
    c[:, C_JABS:C_JABS + 256] = np.abs(j - p)
    c[:, C_JABS + 256:C_JABS + 512] = np.abs(j - p - 128)
    for sl, h in enumerate(core_heads(core)):
        m = 2.0 ** (-(h + 1))
        c[:, C_CB + sl * NCB:C_CB + (sl + 1) * NCB] = -m * 128.0 * np.arange(NCB)[None, :]
        c[:, C_SLP + 2 * sl] = -m
        c[:, C_SLP + 2 * sl + 1] = m
    return c


class Sem:
    __slots__ = ("h", "v")

    def __init__(self, nc, es, name):
        self.h = es.enter_context(nc.semaphore(name))
        self.v = 0


class Buf:
    __slots__ = ("w", "r", "t")

    def __init__(self, t):
        self.w = None
        self.r = {}
        self.t = t

    def __getitem__(self, idx):
        return self.t[idx]


class TR:
    def __init__(self, nc, es):
        self.nc = nc
        self.es = es
        self.pes = None
        self.engs = ("pe", "act", "dve", "pool", "sp")
        self.sem = {e: Sem(nc, es, "s_" + e) for e in ("pe", "act", "dve", "pool")}
        self.slots = {q: [Sem(nc, es, "d_%s%d" % (q, i)) for i in range(NSLOT)] for q in ("sp", "act", "pool")}
        self.slot_i = {q: 0 for q in self.slots}
        self.known = {e: {} for e in self.engs}
        self.q = {e: [] for e in self.engs}
        self.nins = 0
        self.pend = []
        self.ccsem = None
        self.P = [Buf(es.enter_context(nc.psum_tensor("P%d" % i, [128, 512], F32))) for i in range(8)]
        self.pi = 0

    def bank(self):
        b = self.P[self.pi]
        self.pi = (self.pi + 1) % 8
        return b

    @contextlib.contextmanager
    def phase(self):
        with contextlib.ExitStack() as pes:
            self.pes = pes
            yield
            self.barrier()
            self.emit()
            self.pes = None

    def emit(self):
        with self.nc.Block() as block:
            for e, sect in (("sp", block.sync), ("pe", block.tensor), ("act", block.scalar),
                            ("dve", block.vector), ("pool", block.gpsimd)):
                lst = self.q[e]
                if not lst:
                    continue

                def body(eng, lst=lst):
                    for f in lst:
                        f(eng)
                sect(body)
                self.nins += len(lst)
        self.q = {e: [] for e in self.engs}

    def sb(self, name, shape, dt):
        es = self.pes if self.pes is not None else self.es
        self.uid = getattr(self, "uid", 0) + 1
        return Buf(es.enter_context(self.nc.sbuf_tensor("%s_%d" % (name, self.uid), shape, dt)))

    def wait(self, e, so, val):
        k = self.known[e]
        if k.get(so, 0) >= val:
            return
        self.pend.append((so.h, val))
        k[so] = val

    def flush(self, e, keep_last):
        p = self.pend
        self.pend = []
        last = None
        if keep_last and p:
            last = p.pop()
        for h, val in p:
            self.q[e].append(lambda eng, h=h, val=val: eng.wait_ge(h, val))
        return last

    def _deps(self, e, reads, writes, own):
        same = SAME_ENGINE_SYNC and e != "pe"
        for b in reads:
            if b.w is not None:
                so, v = b.w
                if so is own and not same:
                    continue
                self.wait(e, so, v)
        for b in writes:
            if b.w is not None:
                so, v = b.w
                if not (so is own and not same):
                    self.wait(e, so, v)
            for so, v in b.r.items():
                if so is own:
                    continue
                self.wait(e, so, v)

    def op(self, e, fn, reads=(), writes=()):
        own = self.sem[e]
        self._deps(e, reads, writes, own)
        lw = self.flush(e, True)
        own.v += 1
        h = own.h
        if lw is None:
            self.q[e].append(lambda eng, fn=fn, h=h: fn(eng).then_inc(h, 1))
        else:
            self.q[e].append(lambda eng, fn=fn, h=h, wh=lw[0], wv=lw[1]: _winc(fn(eng), wh, wv, h, 1))
        for b in reads:
            b.r[own] = own.v
        for b in writes:
            b.w = (own, own.v)
            b.r = {}

    def dma(self, q, out, in_, reads=(), writes=()):
        sl = self.slots[q]
        i = self.slot_i[q]
        self.slot_i[q] = (i + 1) % NSLOT
        so = sl[i]
        self._deps(q, reads, writes, None)
        self.wait(q, so, so.v)
        lw = self.flush(q, True)
        so.v += 16
        h = so.h
        if lw is None:
            self.q[q].append(lambda eng, o=out, i_=in_, h=h: eng.dma_start(out=o, in_=i_).then_inc(h, 16))
        else:
            self.q[q].append(lambda eng, o=out, i_=in_, h=h, wh=lw[0], wv=lw[1]:
                             _winc(eng.dma_start(out=o, in_=i_), wh, wv, h, 16))
        for b in reads:
            b.r[so] = so.v
        for b in writes:
            b.w = (so, so.v)
            b.r = {}

    def barrier(self):
        allsems = list(self.sem.values()) + [s for sl in self.slots.values() for s in sl]
        for e in self.engs:
            for so in allsems:
                if so.v > 0:
                    self.wait(e, so, so.v)
            self.flush(e, False)

    def allgather(self, pairs):
        if self.ccsem is None:
            self.ccsem = Sem(self.nc, self.es, "ccsem")
        so = self.ccsem
        h = so.h
        for src, dst in pairs:
            so.v += 1
            self.q["pool"].append(lambda eng, h=h, s_=src, d_=dst: eng.collective_compute(
                "AllGather", ALU.bypass, replica_groups=[list(range(NCORE))],
                ins=[s_.ap().opt()], outs=[d_.ap().opt()]).then_inc(h, 1))
        for e in self.engs:
            self.wait(e, so, so.v)
            self.flush(e, False)
        self.emit()

    def mm(self, out, lhsT, rhs, start, stop, reads, writes):
        self.op("pe", lambda e, o=out, l=lhsT, r=rhs, s=start, t=stop: e.matmul(o, lhsT=l, rhs=r, start=s, stop=t),
                reads, writes)

    def tp(self, out, in_, ident, reads, writes):
        self.op("pe", lambda e, o=out, i=in_, d=ident: e.transpose(o, i, d), reads, writes)

    def act(self, out, in_, func, reads, writes, bias=0.0, scale=1.0):
        self.op("act", lambda e, o=out, i=in_, f=func, b=bias, s=scale: e.activation(out=o, in_=i, func=f, bias=b, scale=s),
                reads, writes)

    def tt(self, eng, out, in0, in1, op, reads, writes):
        self.op(eng, lambda e, o=out, a=in0, b=in1, p=op: e.tensor_tensor(out=o, in0=a, in1=b, op=p), reads, writes)

    def ts(self, eng, out, in0, s1, s2, op0, op1, reads, writes):
        if s2 is None:
            self.op(eng, lambda e, o=out, a=in0, s=s1, p=op0: e.tensor_scalar(out=o, in0=a, scalar1=s, scalar2=None, op0=p),
                    reads, writes)
        else:
            self.op(eng, lambda e, o=out, a=in0, s=s1, u=s2, p=op0, q=op1:
                    e.tensor_scalar(out=o, in0=a, scalar1=s, scalar2=u, op0=p, op1=q), reads, writes)

    def stt(self, eng, out, in0, scalar, in1, op0, op1, reads, writes):
        self.op(eng, lambda e, o=out, a=in0, s=scalar, b=in1, p=op0, q=op1:
                e.scalar_tensor_tensor(out=o, in0=a, scalar=s, in1=b, op0=p, op1=q), reads, writes)

    def cp(self, eng, out, in_, reads, writes):
        if eng == "act":
            self.op("act", lambda e, o=out, i=in_: e.copy(out=o, in_=i), reads, writes)
        else:
            self.op(eng, lambda e, o=out, i=in_: e.tensor_copy(out=o, in_=i), reads, writes)

    def red(self, eng, out, in_, reads, writes):
        self.op(eng, lambda e, o=out, i=in_: e.reduce_sum(out=o, in_=i, axis=AX.X), reads, writes)

    def rcp(self, out, in_, reads, writes):
        self.op("dve", lambda e, o=out, i=in_: e.reciprocal(out=o, in_=i), reads, writes)

    def ms(self, eng, ap, val, writes):
        self.op(eng, lambda e, a=ap, v=val: e.memset(a, v), (), writes)


def _winc(ins, wh, wv, h, inc):
    ins.wait_op(wh, wv, "sem-ge")
    return ins.then_inc(h, inc)


def bc_rows(ap2d):
    return ap2d.partition_broadcast(128).rearrange("p a b -> p (a b)")


class Prog:
    def __init__(self, seqs, dbg=()):
        self.seqs = seqs
        self.dbg = set(dbg)
        nc = self.nc = bass.Bass("TRN2", target_bir_lowering=False)
        Tm = self.Tm = max(seqs)
        nb, nt = Tm // 512, Tm // 128
        ei = lambda n, s: nc.dram_tensor(n, s, F32, kind="ExternalInput")
        self.xin = [ei("x%d" % i, [T, D]) for i, T in enumerate(seqs)]
        self.yout = [nc.dram_tensor("y%d" % i, [T, D], F32, kind="ExternalOutput") for i, T in enumerate(seqs)]
        self.norm_g = ei("norm_g", [4, D])
        self.attn_w_in = ei("attn_w_in", [2, D, 8 * HL * 128])
        self.attn_lambda = ei("attn_lambda", [2, 512])
        self.attn_subln_g = ei("attn_subln_g", [2, 256])
        self.attn_w_out = ei("attn_w_out", [2, 2048, 2048])
        self.dn_w_in = ei("dn_w_in", [2, D, 2 * KHL * 128 + 2 * VHL * 128 + 4 * VHL])
        self.dn_conv_w = ei("dn_conv_w", [2, 5, 2 * KHL * 128 + VHL * 128])
        self.dn_a_log_fwd = ei("dn_a_log_fwd", [2, VHL])
        self.dn_dt_bias_fwd = ei("dn_dt_bias_fwd", [2, VHL])
        self.dn_a_log_bwd = ei("dn_a_log_bwd", [2, VHL])
        self.dn_dt_bias_bwd = ei("dn_dt_bias_bwd", [2, VHL])
        self.dn_norm_g = ei("dn_norm_g", [2, 128])
        self.dn_w_out = ei("dn_w_out", [2, 4096, 2048])
        self.final_norm_g = ei("final_norm_g", [1, D])
        self.cst = ei("cst", [128, C_END])

        def scr(n, s, dt):
            return nc.dram_tensor(n, s, dt, kind=("ExternalOutput" if n in self.dbg else "Internal"))
        self.xs = scr("xs", [Tm, D], F32)
        self.hT = scr("hT", [nb, 128, 16, 512], BF16)
        self.y = scr("ysc", [nt, 128, 2048], F32)
        self.qT = scr("qT", [2 * HL, nb, 128, 512], BF16)
        self.kT = scr("kT", [2 * HL, nb, 128, 512], BF16)
        self.va = scr("va", [nt, 128, HL, VA], BF16)
        self.sg = scr("sg", [nt, 128, HL * 256], F32)
        self.pcs = scr("pcs", [2 * KHL + VHL, 128, Tm], F32)
        self.qn = scr("qn", [nt, 128, KHL, 128], BF16)
        self.kn = scr("kn", [nt, 128, KHL, 128], BF16)
        self.ktok = scr("ktok", [nt, 128, KHL * 128], BF16)
        self.vtok = scr("vtok", [nt, 128, VHL * 128], BF16)
        self.sz = scr("sz", [nt, 128, VHL * 128], F32)
        self.gb = scr("gb", [nt, 128, 4 * VHL], F32)
        self.of = scr("of", [nt, 128, VHL * 128], F32)
        self.og_loc = {}
        self.og_all = {}
        self.og_bpc = {"a": max(1, 1024 // (128 * 2 * HL)), "g": max(1, 1024 // (128 * VHL))}
        for si, T in enumerate(seqs):
            nb_ = T // 512
            for kind, kl in (("a", 2 * HL), ("g", VHL)):
                bpc = self.og_bpc[kind]
                nch = -(-nb_ // bpc)
                szs = [min(bpc, nb_ - c * bpc) * 128 * kl for c in range(nch)]
                self.og_loc[si, kind] = [scr("ogl_%d%s%d" % (si, kind, c), [szs[c], 512], BF16) for c in range(nch)]
                self.og_all[si, kind] = [scr("oga_%d%s%d" % (si, kind, c), [NCORE * szs[c], 512], BF16) for c in range(nch)]

    def og_loc_blk(self, si, kind, kl):
        bpc = self.og_bpc[kind]

        def f(b):
            c, bl = divmod(b, bpc)
            return self.og_loc[si, kind][c].ap().rearrange("(b p k) t -> b p k t", p=128, k=kl)[bl]
        return f

    def og_all_blk(self, si, kind, kl):
        bpc = self.og_bpc[kind]

        def f(r, b):
            c, bl = divmod(b, bpc)
            return self.og_all[si, kind][c].ap().rearrange("(r b p k) t -> r b p k t", r=NCORE, p=128, k=kl)[r, bl]
        return f

    def load_const_bf16(self, tr, name, col, n=128):
        st = tr.sb(name + "_f", [128, n], F32)
        tr.dma("sp", st[:], self.cst[:, col:col + n], writes=[st])
        b = tr.sb(name, [128, n], BF16)
        tr.cp("dve", b[:], st[:], [st], [b])
        return b

    def load_const_f32(self, tr, name, col, n=128):
        st = tr.sb(name, [128, n], F32)
        tr.dma("sp", st[:], self.cst[:, col:col + n], writes=[st])
        return st

    def resnorm(self, tr, T, x_src, y_src, x_dst, g_row, final, out):
        nt = T // 128
        with tr.phase():
            gt = tr.sb("gt", [128, D], F32)
            tr.dma("sp", gt[:], bc_rows(g_row), writes=[gt])
            ident = self.load_const_bf16(tr, "ident", C_IDENT)
            xts = [tr.sb("xt%d" % i, [128, D], F32) for i in range(2)]
            yts = [tr.sb("yt%d" % i, [128, D], F32) for i in range(2)]
            sq = tr.sb("sq", [128, D], F32)
            ssq = tr.sb("ssq", [128, 2], F32)
            hbs = [tr.sb("hb%d" % i, [128, D], BF16) for i in range(2)]
            hfs = [tr.sb("hf%d" % i, [128, D], F32) for i in range(2)] if final else None
            hTt = [tr.sb("hTt%d" % i, [128, 16, 512], BF16) for i in range(2)]
            for tt in range(nt):
                xt = xts[tt % 2]
                tr.dma("sp", xt[:], x_src[tt * 128:(tt + 1) * 128, :], writes=[xt])
                if y_src is not None:
                    yt = yts[tt % 2]
                    tr.dma("sp", yt[:], y_src[tt], writes=[yt])
                    tr.tt("dve", xt[:], xt[:], yt[:], ALU.add, [xt, yt], [xt])
                if x_dst is not None:
                    tr.dma("pool", x_dst[tt * 128:(tt + 1) * 128, :], xt[:], reads=[xt])
                if RES_ACCUM:
                    tr.ms("pool", ssq[:, 0:1], 0.0, [ssq])
                    tr.op("act", lambda e, o=sq[:], i=xt[:], a=ssq[:, 0:1]: e.activation(out=o, in_=i, func=AF.Square, accum_out=a),
                          [xt, ssq], [sq, ssq])
                else:
                    tr.act(sq[:], xt[:], AF.Square, [xt], [sq])
                    tr.red("dve", ssq[:, 0:1], sq[:], [sq], [ssq])
                tr.act(ssq[:, 1:2], ssq[:, 0:1], AF.Sqrt, [ssq], [ssq], bias=RMS_EPS, scale=1.0 / D)
                tr.rcp(ssq[:, 1:2], ssq[:, 1:2], [ssq], [ssq])
                if final:
                    hf = hfs[tt % 2]
                    tr.stt("dve", hf[:], xt[:], ssq[:, 1:2], gt[:], ALU.mult, ALU.mult, [xt, ssq, gt], [hf])
                    tr.dma("pool", out[tt * 128:(tt + 1) * 128, :], hf[:], reads=[hf])
                    continue
                hb = hbs[tt % 2]
                tr.stt("dve", hb[:], xt[:], ssq[:, 1:2], gt[:], ALU.mult, ALU.mult, [xt, ssq, gt], [hb])
                b, s = tt // 4, tt % 4
                ht = hTt[b % 2]
                for half in range(2):
                    pb = tr.bank()
                    pv = pb[:].bitcast(BF16)
                    for k8 in range(8):
                        kc = half * 8 + k8
                        tr.tp(pv[:, k8 * 128:(k8 + 1) * 128], hb[:, kc * 128:(kc + 1) * 128], ident[:], [hb, ident], [pb])
                    tr.cp("act", ht[:, half * 8:(half + 1) * 8, s * 128:(s + 1) * 128],
                          pv.rearrange("p (a b) -> p a b", b=128), [pb], [ht])
                if s == 3:
                    tr.dma("pool", out[b], ht[:], reads=[ht])

    def proj(self, tr, T, src_loader, nk, w_ap, wc, mode, evac_factory):
        nb = T // 512
        with tr.phase():
            wt = tr.sb("wt", [128, nk, wc], BF16)
            stg = [tr.sb("wstg%d" % i, [128, wc], F32) for i in range(2)]
            for kc in range(nk):
                st = stg[kc % 2]
                tr.dma("sp", st[:], w_ap[kc * 128:(kc + 1) * 128, :], writes=[st])
                if kc % 2:
                    tr.cp("act", wt[:, kc, :], st[:], [st], [wt])
                else:
                    tr.cp("dve", wt[:, kc, :], st[:], [st], [wt])
            sbt = [tr.sb("psrc%d" % i, [128, nk, 512], BF16) for i in range(2)]
            evac = evac_factory(tr)
            for b in range(nb):
                s = sbt[b % 2]
                src_loader(tr, b, s)
                if mode == "feat":
                    for ct in range(wc // 128):
                        pb = tr.bank()
                        for kc in range(nk):
                            tr.mm(pb[:], wt[:, kc, ct * 128:(ct + 1) * 128], s[:, kc, :], kc == 0, kc == nk - 1, [wt, s], [pb])
                        evac(b, ct, pb)
                else:
                    n = min(wc, 512)
                    for sub in range(4):
                        for cg in range(max(1, wc // 512)):
                            pb = tr.bank()
                            for kc in range(nk):
                                tr.mm(pb[:, 0:n], s[:, kc, sub * 128:(sub + 1) * 128], wt[:, kc, cg * n:(cg + 1) * n],
                                      kc == 0, kc == nk - 1, [wt, s], [pb])
                            evac(b, sub, cg, pb)

    def src_hT(self):
        def ld(tr, b, s):
            tr.dma("sp", s[:], self.hT[b], writes=[s])
        return ld

    def src_gathered(self, si, kind, kl):
        v = self.og_all_blk(si, kind, kl)

        def ld(tr, b, s):
            for r in range(NCORE):
                tr.dma("sp", s[:, r * kl:(r + 1) * kl, :], v(r, b), writes=[s])
        return ld

    def ev_feat_bf16(self, dst, scale):
        def fac(tr):
            stg = [tr.sb("evq%d" % i, [128, 512], BF16) for i in range(3)]
            cnt = [0]

            def ev(b, ct, pb):
                st = stg[cnt[0] % 3]
                cnt[0] += 1
                if cnt[0] % 2:
                    tr.act(st[:], pb[:], AF.Copy, [pb], [st], scale=scale)
                else:
                    tr.ts("dve", st[:], pb[:], scale, None, ALU.mult, None, [pb], [st])
                tr.dma("pool", dst[ct][b], st[:], reads=[st])
            return ev
        return fac

    def ev_v(self):
        def fac(tr):
            stg = [tr.sb("evv%d" % i, [128, HL, VA], BF16) for i in range(2)]
            for st in stg:
                tr.ms("dve", st[:], 1.0, [st])

            def ev(b, sub, cg, pb):
                tt = b * 4 + sub
                st = stg[tt % 2]
                src = pb[:].rearrange("p (h e) -> p h e", e=256)
                tr.cp("act", st[:, 0:HL, 0:256], src, [pb], [st])
                tr.dma("pool", self.va[tt], st[:], reads=[st])
            return ev
        return fac

    def ev_tok_f32(self, dst, col0, ncg, func, wcg=512):
        def fac(tr):
            stg = [tr.sb("evt%d" % i, [128, ncg * wcg], F32) for i in range(2)]

            def ev(b, sub, cg, pb):
                tt = b * 4 + sub
                st = stg[tt % 2]
                if func is not None:
                    tr.act(st[:, cg * wcg:(cg + 1) * wcg], pb[:, 0:wcg], func, [pb], [st])
                elif cg % 2:
                    tr.cp("act", st[:, cg * wcg:(cg + 1) * wcg], pb[:, 0:wcg], [pb], [st])
                else:
                    tr.cp("dve", st[:, cg * wcg:(cg + 1) * wcg], pb[:, 0:wcg], [pb], [st])
                if cg == ncg - 1:
                    tr.dma("pool", dst[tt][:, col0:col0 + ncg * wcg], st[:], reads=[st])
            return ev
        return fac

    def ev_pc(self, ct0):
        def fac(tr):
            stg = [tr.sb("evp%d" % i, [128, 512], F32) for i in range(3)]
            cnt = [0]

            def ev(b, ct, pb):
                st = stg[cnt[0] % 3]
                cnt[0] += 1
                tr.cp("act" if cnt[0] % 2 else "dve", st[:], pb[:], [pb], [st])
                tr.dma("pool", self.pcs[ct0 + ct][:, b * 512:(b + 1) * 512], st[:], reads=[st])
            return ev
        return fac

    def ev_gates(self, j):
        H = VHL

        def fac(tr):
            dtb = tr.sb("dtb", [128, 2, H], F32)
            nA = tr.sb("nA", [128, 2, H], F32)
            tr.dma("sp", dtb[:, 0, :], bc_rows(self.dn_dt_bias_fwd[j:j + 1, :]), writes=[dtb])
            tr.dma("sp", dtb[:, 1, :], bc_rows(self.dn_dt_bias_bwd[j:j + 1, :]), writes=[dtb])
            tr.dma("sp", nA[:, 0, :], bc_rows(self.dn_a_log_fwd[j:j + 1, :]), writes=[nA])
            tr.dma("sp", nA[:, 1, :], bc_rows(self.dn_a_log_bwd[j:j + 1, :]), writes=[nA])
            tr.act(nA[:], nA[:], AF.Exp, [nA], [nA])
            tr.ts("dve", nA[:], nA[:], -1.0, None, ALU.mult, None, [nA], [nA])
            xa = tr.sb("g_xa", [128, 2, H], F32)
            ax = tr.sb("g_ax", [128, 2, H], F32)
            outs = [tr.sb("g_out%d" % i, [128, 4, H], F32) for i in range(2)]

            def ev(b, sub, cg, pb):
                tt = b * 4 + sub
                o = outs[tt % 2]
                pv = pb[:, 0:4 * H].rearrange("p (a b) -> p a b", b=H)
                for d in range(2):
                    tr.tt("dve", xa[:, d, :], pv[:, 2 * d, :], dtb[:, d, :], ALU.add, [pb, dtb], [xa])
                tr.act(ax[:], xa[:], AF.Abs, [xa], [ax])
                tr.act(ax[:], ax[:], AF.Exp, [ax], [ax], scale=-1.0)
                tr.act(ax[:], ax[:], AF.Ln, [ax], [ax], bias=1.0)
                tr.ts("dve", xa[:], xa[:], 0.0, None, ALU.max, None, [xa], [xa])
                tr.tt("dve", xa[:], xa[:], ax[:], ALU.add, [xa, ax], [xa])
                for d in range(2):
                    tr.tt("dve", o[:, 2 * d, :], xa[:, d, :], nA[:, d, :], ALU.mult, [xa, nA], [o])
                    tr.act(o[:, 2 * d + 1, :], pv[:, 2 * d + 1, :], AF.Sigmoid, [pb], [o])
                tr.dma("pool", self.gb[tt], o[:].rearrange("p a b -> p (a b)"), reads=[o])
            return ev
        return fac

    def attn_core(self, tr, T, j, lam_init, og_view):
        nt = T // 128
        nq = T // 256
        with tr.phase():
            ident = self.load_const_bf16(tr, "ident", C_IDENT)
            Jt = self.load_const_f32(tr, "Jt", C_J, 256)
            Ja = self.load_const_f32(tr, "Ja", C_JABS, 512)
            cb = self.load_const_f32(tr, "cb", C_CB, HL * NCB)
            slp = self.load_const_f32(tr, "slp", C_SLP, 4)
            lv = tr.sb("lv", [128, 512], F32)
            tr.dma("sp", lv[:], bc_rows(self.attn_lambda[j:j + 1, :]), writes=[lv])
            lp = tr.sb("lp", [128, 256], F32)
            ls = tr.sb("ls", [128, 4], F32)
            tr.tt("dve", lp[:, 0:128], lv[:, 0:128], lv[:, 128:256], ALU.mult, [lv], [lp])
            tr.tt("dve", lp[:, 128:256], lv[:, 256:384], lv[:, 384:512], ALU.mult, [lv], [lp])
            tr.red("dve", ls[:, 0:2], lp[:].rearrange("p (a b) -> p a b", b=128), [lp], [ls])
            tr.act(ls[:, 0:2], ls[:, 0:2], AF.Exp, [ls], [ls])
            tr.tt("dve", ls[:, 2:3], ls[:, 1:2], ls[:, 0:1], ALU.subtract, [ls], [ls])
            tr.ts("dve", ls[:, 3:4], ls[:, 2:3], -lam_init, None, ALU.add, None, [ls], [ls])
            sgn = tr.sb("sgn", [128, 256], F32)
            tr.dma("sp", sgn[:], bc_rows(self.attn_subln_g[j:j + 1, :]), writes=[sgn])
            tr.ts("dve", sgn[:], sgn[:], 1.0 - lam_init, None, ALU.mult, None, [sgn], [sgn])

            qts = [tr.sb("aq%d" % i, [128, 2, 256], BF16) for i in range(2)]
            sgts = [tr.sb("asg%d" % i, [128, 2, 256], F32) for i in range(2)]
            kts = [tr.sb("ak%d" % i, [128, 2, 512], BF16) for i in range(4)]
            vts = [tr.sb("av%d" % i, [128, 4, VA], BF16) for i in range(4)]
            tbs = [tr.sb("atb%d" % i, [128, 512], F32) for i in range(4)]
            pbs = [tr.sb("apb%d" % i, [128, 512], BF16) for i in range(4)]
            om = tr.sb("aom", [128, 2, 2, 256], F32)
            rden = tr.sb("arden", [128, 4], F32)
            oc = tr.sb("aoc", [128, 2, 256], F32)
            osq = tr.sb("aosq", [128, 2, 256], F32)
            ost = tr.sb("aost", [128, 4], F32)
            ogb = tr.sb("aogb", [128, 2, 256], BF16)
            ogTt = [tr.sb("aogT%d" % i, [128, 2, 256], BF16) for i in range(2)]
            acc = tr.P[0:4]
            sc = tr.P[4:8]
            tpb = tr.P[4:5]
            nblk = 0
            it = 0
            for h in range(HL):
                W = ATT_W[h]
                mneg = slp[:, 2 * h:2 * h + 1]
                mpos = slp[:, 2 * h + 1:2 * h + 2]
                for qt in range(nq):
                    q0 = qt * 256
                    qtile = qts[it % 2]
                    sgt = sgts[it % 2]
                    ogt = ogTt[it % 2]
                    it += 1
                    qb, qo = qt // 2, (qt % 2) * 256
                    for mp in range(2):
                        tr.dma("sp", qtile[:, mp, :], self.qT[2 * h + mp][qb][:, qo:qo + 256], writes=[qtile])
                    tr.dma("sp", sgt[:], self.sg[qt * 2:qt * 2 + 2, :, h * 256:(h + 1) * 256].rearrange("s p e -> p s e"),
                           writes=[sgt])
                    kt_lo = max(0, (q0 - W) // 128)
                    kt_hi = min(nt, -((-(q0 + 256 + W)) // 128))
                    kts_list = list(range(kt_lo, kt_hi))
                    nk_ = len(kts_list)
                    info = {}
                    cur = {"kb": -1, "kt": None, "vt": None}

                    def issue_qk(idx):
                        nonlocal nblk
                        kt = kts_list[idx]
                        kb, ks = kt // 4, kt % 4
                        if kb != cur["kb"]:
                            cur["kb"] = kb
                            cur["kt"] = kts[nblk % 4]
                            cur["vt"] = vts[nblk % 4]
                            nblk += 1
                            for mp in range(2):
                                tr.dma("sp", cur["kt"][:, mp, :], self.kT[2 * h + mp][kb], writes=[cur["kt"]])
                            tr.dma("sp", cur["vt"][:], self.va[kb * 4:kb * 4 + 4, :, h, :].rearrange("s p e -> p s e"),
                                   writes=[cur["vt"]])
                        ktile, vtile = cur["kt"], cur["vt"]
                        sb_ = sc[idx % 4]
                        for mp in range(2):
                            tr.mm(sb_[:, mp * 256:(mp + 1) * 256], ktile[:, mp, ks * 128:(ks + 1) * 128], qtile[:, mp, :],
                                  True, True, [ktile, qtile], [sb_])
                        info[idx] = (vtile, ks, kt * 128, sb_)

                    def issue_soft(idx):
                        vtile, ks, k0, sb_ = info[idx]
                        tb = tbs[idx % 4]
                        pb = pbs[idx % 4]
                        sv = sb_[:].rearrange("p (a b) -> p a b", b=256)
                        tv = tb[:].rearrange("p (a b) -> p a b", b=256)
                        if k0 + 127 < q0:
                            jv = Jt[:].unsqueeze(1).to_broadcast([128, 2, 256])
                            tr.stt("dve", tv, jv, mneg, sv, ALU.mult, ALU.add, [Jt, sb_, slp], [tb])
                            ci = (q0 - k0) // 128
                        elif k0 > q0 + 255:
                            jv = Jt[:].unsqueeze(1).to_broadcast([128, 2, 256])
                            tr.stt("dve", tv, jv, mpos, sv, ALU.mult, ALU.add, [Jt, sb_, slp], [tb])
                            ci = (k0 - q0) // 128
                        else:
                            i_ = (k0 - q0) // 128
                            jv = Ja[:, i_ * 256:(i_ + 1) * 256].unsqueeze(1).to_broadcast([128, 2, 256])
                            tr.stt("dve", tv, jv, mneg, sv, ALU.mult, ALU.add, [Ja, sb_, slp], [tb])
                            ci = 0
                        tr.act(pb[:], tb[:], AF.Exp, [tb, cb], [pb], bias=cb[:, h * NCB + ci:h * NCB + ci + 1])

                    def issue_av(idx):
                        vtile, ks, k0, sb_ = info.pop(idx)
                        pb = pbs[idx % 4]
                        first, last = idx == 0, idx == nk_ - 1
                        for mp in range(2):
                            for qs in range(2):
                                a = acc[mp * 2 + qs]
                                tr.mm(a[:, 0:257], pb[:, mp * 256 + qs * 128:mp * 256 + (qs + 1) * 128], vtile[:, ks, 0:257],
                                      first, last, [pb, vtile], [a])

                    LOOK = 3
                    for idx in range(min(LOOK, nk_)):
                        issue_qk(idx)
                    for idx in range(nk_):
                        issue_soft(idx)
                        if idx + LOOK < nk_:
                            issue_qk(idx + LOOK)
                        issue_av(idx)
                    for mp in range(2):
                        for qs in range(2):
                            a = acc[mp * 2 + qs]
                            c_ = mp * 2 + qs
                            tr.rcp(rden[:, c_:c_ + 1], a[:, 256:257], [a], [rden])
                            if qs:
                                tr.act(om[:, mp, qs, :], a[:, 0:256], AF.Copy, [a, rden], [om], scale=rden[:, c_:c_ + 1])
                            else:
                                tr.ts("dve", om[:, mp, qs, :], a[:, 0:256], rden[:, c_:c_ + 1], None, ALU.mult, None, [a, rden], [om])
                    tr.stt("dve", oc[:], om[:, 1, :, :], ls[:, 3:4], om[:, 0, :, :], ALU.mult, ALU.add, [om, ls], [oc])
                    tr.act(osq[:], oc[:], AF.Square, [oc], [osq])
                    tr.red("dve", ost[:, 0:2], osq[:], [osq], [ost])
                    tr.act(ost[:, 2:4], ost[:, 0:2], AF.Sqrt, [ost], [ost], bias=RMS_EPS, scale=1.0 / 256)
                    tr.rcp(ost[:, 2:4], ost[:, 2:4], [ost], [ost])
                    tr.tt("dve", oc[:], oc[:], ost[:, 2:4].unsqueeze(2).to_broadcast([128, 2, 256]), ALU.mult, [oc, ost], [oc])
                    tr.tt("pool", oc[:], oc[:], sgn[:].unsqueeze(1).to_broadcast([128, 2, 256]), ALU.mult, [oc, sgn], [oc])
                    tr.tt("dve", ogb[:], oc[:], sgt[:], ALU.mult, [oc, sgt], [ogb])
                    tb_ = tpb[0]
                    tv_ = tb_[:].bitcast(BF16)
                    for ec in range(2):
                        for qs in range(2):
                            c_ = (ec * 2 + qs) * 128
                            tr.tp(tv_[:, c_:c_ + 128], ogb[:, qs, ec * 128:(ec + 1) * 128], ident[:], [ogb, ident], [tb_])
                    tr.cp("act", ogt[:], tv_[:, 0:512].rearrange("p (a b) -> p a b", b=256), [tb_], [ogt])
                    tr.dma("pool", og_view(qb)[:, 2 * h:2 * h + 2, qo:qo + 256], ogt[:], reads=[ogt])

    def gdn_conv(self, tr, T, j):
        nb = T // 512
        NCT = 2 * KHL + VHL
        with tr.phase():
            ident = self.load_const_bf16(tr, "ident", C_IDENT)
            identf = self.load_const_f32(tr, "identf", C_IDENT)
            onesf = self.load_const_f32(tr, "onesf", C_ONES)
            cwr = tr.sb("cwr", [5, NCT * 128], F32)
            tr.dma("sp", cwr[:], self.dn_conv_w[j], writes=[cwr])
            cw = tr.sb("cw", [128, NCT, 5], F32)
            for ct in range(NCT):
                pb = tr.bank()
                tr.tp(pb[:, 0:5], cwr[0:5, ct * 128:(ct + 1) * 128], identf[0:5, 0:5], [cwr, identf], [pb])
                tr.cp("dve", cw[:, ct, :], pb[:, 0:5], [pb], [cw])
            accs = [tr.sb("ca%d" % i, [128, 512], F32) for i in range(2)]
            xins = [tr.sb("cx%d" % i, [128, 516], F32) for i in range(3)]
            sxs = [tr.sb("cs%d" % i, [128, 512], F32) for i in range(2)]
            sqs = [tr.sb("cq%d" % i, [128, 512], F32) for i in range(2)]
            rns = [tr.sb("cr%d" % i, [128, 512], F32) for i in range(2)]
            xnbs = [tr.sb("cn%d" % i, [128, 512], BF16) for i in range(2)]
            tks = [tr.sb("ct%d" % i, [128, 4, 128], BF16) for i in range(2)]
            it = 0
            for ct in range(NCT):
                for b in range(nb):
                    xin = xins[it % 3]
                    sx = sxs[it % 2]
                    sq = sqs[it % 2]
                    rn = rns[it % 2]
                    xnb = xnbs[it % 2]
                    tk = tks[it % 2]
                    it += 1
                    lo = b * 512 - 2
                    hi = b * 512 + 514
                    c0, c1 = 0, 516
                    if b == 0:
                        tr.ms("dve", xin[:, 0:2], 0.0, [xin])
                        lo, c0 = 0, 2
                    if b == nb - 1:
                        tr.ms("dve", xin[:, 514:516], 0.0, [xin])
                        hi, c1 = T, 514
                    tr.dma("sp", xin[:, c0:c1], self.pcs[ct][:, lo:hi], writes=[xin])
                    acc = accs[it % 2]
                    tr.ts("dve", acc[:], xin[:, 0:512], cw[:, ct, 0:1], None, ALU.mult, None, [xin, cw], [acc])
                    for tap in range(1, 5):
                        tr.stt("dve", acc[:], xin[:, tap:tap + 512], cw[:, ct, tap:tap + 1], acc[:], ALU.mult, ALU.add,
                               [xin, cw, acc], [acc])
                    tr.act(sx[:], acc[:], AF.Silu, [acc], [sx])
                    if ct < 2 * KHL:
                        kh = ct % KHL
                        tr.act(sq[:], sx[:], AF.Square, [sx], [sq])
                        pb = tr.bank()
                        tr.mm(pb[:], onesf[:], sq[:], True, True, [onesf, sq], [pb])
                        tr.act(rn[:], pb[:], AF.Sqrt, [pb], [rn], bias=L2_EPS)
                        tr.rcp(rn[:], rn[:], [rn], [rn])
                        tr.stt("dve", xnb[:], sx[:], (128 ** -0.5 if ct < KHL else 1.0), rn[:], ALU.mult, ALU.mult, [sx, rn], [xnb])
                        dst = self.qn if ct < KHL else self.kn
                        tr.dma("pool", dst[b * 4:b * 4 + 4, :, kh, :].rearrange("s p t -> p s t"),
                               xnb[:].rearrange("p (s t) -> p s t", t=128), reads=[xnb])
                        if ct < KHL:
                            continue
                    else:
                        tr.cp("act", xnb[:], sx[:], [sx], [xnb])
                    pb = tr.bank()
                    pv = pb[:].bitcast(BF16)
                    for s in range(4):
                        tr.tp(pv[:, s * 128:(s + 1) * 128], xnb[:, s * 128:(s + 1) * 128], ident[:], [xnb, ident], [pb])
                    tr.cp("act", tk[:], pv[:, 0:512].rearrange("p (s d) -> p s d", d=128), [pb], [tk])
                    if ct < 2 * KHL:
                        kh = ct - KHL
                        tr.dma("pool", self.ktok[b * 4:b * 4 + 4, :, kh * 128:(kh + 1) * 128].rearrange("s p d -> p s d"),
                               tk[:], reads=[tk])
                    else:
                        vh = ct - 2 * KHL
                        tr.dma("pool", self.vtok[b * 4:b * 4 + 4, :, vh * 128:(vh + 1) * 128].rearrange("s p d -> p s d"),
                               tk[:], reads=[tk])

    def gdn_scan(self, tr, T, j, d, og_view):
        nt = T // 128
        bwd = d == 1
        H = VHL
        with tr.phase():
            ident = self.load_const_bf16(tr, "ident", C_IDENT)
            identf = self.load_const_f32(tr, "identf", C_IDENT)
            onesf = self.load_const_f32(tr, "onesf", C_ONES)
            tri = self.load_const_f32(tr, "tri", C_TRIB if bwd else C_TRIF)
            mb4 = tr.sb("mb4", [128, 4, 128], F32)
            nmbs4 = tr.sb("nmbs4", [128, 4, 128], F32)
            for hh in range(4):
                cm = C_MBB if bwd else C_MBF
                cms = C_MBSB if bwd else C_MBSF
                tr.dma("sp", mb4[:, hh, :], self.cst[:, cm:cm + 128], writes=[mb4])
                tr.dma("sp", nmbs4[:, hh, :], self.cst[:, cms:cms + 128], writes=[nmbs4])
            tr.ts("dve", nmbs4[:], nmbs4[:], -1.0, None, ALU.mult, None, [nmbs4], [nmbs4])
            gn = tr.sb("gn", [128, 128], F32)
            tr.dma("sp", gn[:], bc_rows(self.dn_norm_g[j:j + 1, :]), writes=[gn])
            S32 = tr.sb("S32", [128, H, 128], F32)
            Sb = tr.sb("Sb", [128, H, 128], BF16)
            tr.ms("dve", S32[:], 0.0, [S32])
            tr.ms("dve", Sb[:], 0.0, [Sb])
            NB = 3
            qchs = [tr.sb("qch%d" % i, [128, KHL, 128], BF16) for i in range(NB)]
            kchs = [tr.sb("kch%d" % i, [128, KHL, 128], BF16) for i in range(NB)]
            ktks = [tr.sb("ktk%d" % i, [128, KHL, 128], BF16) for i in range(NB)]
            vtks = [tr.sb("vtk%d" % i, [128, H, 128], BF16) for i in range(NB)]
            gbts = [tr.sb("gbt%d" % i, [128, 4, H], F32) for i in range(NB)]
            if bwd:
                ofts = [tr.sb("oft%d" % i, [128, H, 128], F32) for i in range(NB)]
                szts = [tr.sb("szt%d" % i, [128, H, 128], F32) for i in range(NB)]
                ogTts = [tr.sb("sogT%d" % i, [128, H, 128], BF16) for i in range(NB)]
            else:
                osts = [tr.sb("ost%d" % i, [128, H, 128], F32) for i in range(NB)]
            gcs = [tr.sb("gc%d" % i, [128, 2 * H], F32) for i in range(NB)]
            egcs = [tr.sb("egc%d" % i, [128, 2 * H], F32) for i in range(NB)]
            dkfs = [tr.sb("dkf%d" % i, [128, H], F32) for i in range(NB)]
            bks = [tr.sb("bk%d" % i, [128, H], F32) for i in range(NB)]
            ngcs = [tr.sb("ngc%d" % i, [128, H], F32) for i in range(NB)]
            NSETS = 4
            G = []
            for g_ in range(NSETS):
                t_ = {}
                for nm, dt_ in (("gbc", F32), ("E", BF16), ("ETs", F32), ("Lb", F32), ("Ub", F32), ("Aq", BF16),
                                ("PA", F32), ("QA", F32), ("N32", F32), ("Nb", BF16), ("bv", BF16), ("bke", BF16),
                                ("kd", BF16), ("u32", F32), ("wTb", BF16), ("vnew", BF16), ("o1", F32),
                                ("ogb", BF16), ("stmp", F32)):
                    t_[nm] = tr.sb("%s_%d" % (nm, g_), [128, 4, 128], dt_)
                t_["ost"] = tr.sb("gost_%d" % g_, [128, 8], F32)
                G.append(t_)

            def b4(ap2):
                return ap2.unsqueeze(2).to_broadcast([128, 4, 128])

            def fl(buf):
                return buf[:].rearrange("p a b -> p (a b)")

            order = range(nt - 1, -1, -1) if bwd else range(nt)
            live = []
            setno = [0]

            def finish(cx):
                cx["remaining"] -= 1
                if cx["remaining"]:
                    return
                n_ = cx["n"]
                if bwd:
                    tr.dma("pool", og_view(n_ // 4)[:, :, (n_ % 4) * 128:(n_ % 4 + 1) * 128], cx["ogTt"][:], reads=[cx["ogTt"]])
                else:
                    tr.dma("pool", self.of[n_], fl(cx["ostg"]), reads=[cx["ostg"]])

            for ci, n in enumerate(order):
                qch, kch, ktk, vtk, gbt = qchs[ci % NB], kchs[ci % NB], ktks[ci % NB], vtks[ci % NB], gbts[ci % NB]
                gc, egc, dkf, bk, ngc = gcs[ci % NB], egcs[ci % NB], dkfs[ci % NB], bks[ci % NB], ngcs[ci % NB]
                cx = {"n": n, "remaining": H // 4}
                tr.dma("sp", qch[:], self.qn[n], writes=[qch])
                tr.dma("sp", kch[:], self.kn[n], writes=[kch])
                tr.dma("sp", fl(ktk), self.ktok[n], writes=[ktk])
                tr.dma("sp", fl(vtk), self.vtok[n], writes=[vtk])
                tr.dma("sp", fl(gbt), self.gb[n], writes=[gbt])
                if bwd:
                    oft, szt, ogTt = ofts[ci % NB], szts[ci % NB], ogTts[ci % NB]
                    cx["ogTt"] = ogTt
                    tr.dma("sp", fl(oft), self.of[n], writes=[oft])
                    tr.dma("sp", fl(szt), self.sz[n], writes=[szt])
                else:
                    ostg = osts[ci % NB]
                    cx["ostg"] = ostg
                graw = gbt[:, 2 * d, :]
                beta = gbt[:, 2 * d + 1, :]
                pb = tr.bank()
                tr.mm(pb[:, 0:H], tri[:], graw, True, True, [tri, gbt], [pb])
                tr.mm(pb[:, H:2 * H], onesf[:], graw, True, True, [onesf, gbt], [pb])
                tr.cp("dve", gc[:], pb[:, 0:2 * H], [pb], [gc])
                tr.act(egc[:], gc[:], AF.Exp, [gc], [egc])
                tr.ts("dve", ngc[:], gc[:, 0:H], -1.0, None, ALU.mult, None, [gc], [ngc])
                tr.tt("dve", dkf[:], gc[:, H:2 * H], gc[:, 0:H], ALU.subtract, [gc], [dkf])
                tr.act(dkf[:], dkf[:], AF.Exp, [dkf], [dkf])
                tr.tt("dve", bk[:], beta, egc[:, 0:H], ALU.mult, [gbt, egc], [bk])

                def group(gq, t_, cx=cx, qch=qch, kch=kch, ktk=ktk, vtk=vtk, gbt=gbt, gc=gc, egc=egc, dkf=dkf, bk=bk, ngc=ngc,
                          graw=graw, beta=beta, oft=(oft if bwd else None), szt=(szt if bwd else None),
                          ogTt=(ogTt if bwd else None), ostg=(None if bwd else ostg)):
                    gbc, E, ETs, Lb, Ub, Aq = t_["gbc"], t_["E"], t_["ETs"], t_["Lb"], t_["Ub"], t_["Aq"]
                    N32, Nb, bv, bke, kd = t_["N32"], t_["Nb"], t_["bv"], t_["bke"], t_["kd"]
                    u32, wTb, vnew, o1, ogb, stmp, ost = (t_["u32"], t_["wTb"], t_["vnew"], t_["o1"], t_["ogb"], t_["stmp"],
                                                          t_["ost"])
                    ot = o1
                    osq = stmp
                    h0 = 4 * gq
                    kh0 = 2 * gq
                    tr.tt("dve", gbc[:], identf[:].unsqueeze(1).to_broadcast([128, 4, 128]), b4(gc[:, h0:h0 + 4]), ALU.mult,
                          [identf, gc], [gbc])
                    pA = tr.bank()
                    pB = tr.bank()
                    tr.mm(pA[:], onesf[:], fl(gbc), True, False, [onesf, gbc], [pA])
                    tr.mm(pA[:], identf[:], fl(mb4), False, True, [identf, mb4], [pA])
                    tr.mm(pB[:], onesf[:], fl(gbc), True, False, [onesf, gbc], [pB])
                    tr.mm(pB[:], identf[:], fl(nmbs4), False, True, [identf, nmbs4], [pB])
                    for hh in range(4):
                        cs = slice(hh * 128, (hh + 1) * 128)
                        tr.act(E[:, hh, :], pA[:, cs], AF.Exp, [pA, ngc], [E], bias=ngc[:, h0 + hh:h0 + hh + 1])
                        tr.act(ETs[:, hh, :], pB[:, cs], AF.Exp, [pB, gc], [ETs], bias=gc[:, h0 + hh:h0 + hh + 1], scale=-1.0)
                    pC = tr.bank()
                    for i_ in range(2):
                        tr.mm(pC[:, i_ * 128:(i_ + 1) * 128], kch[:, kh0 + i_, :], kch[:, kh0 + i_, :], True, True, [kch], [pC])
                        tr.mm(pC[:, (2 + i_) * 128:(3 + i_) * 128], kch[:, kh0 + i_, :], qch[:, kh0 + i_, :], True, True,
                              [kch, qch], [pC])
                    for hh in range(4):
                        i_ = hh // 2
                        tr.stt("dve", Lb[:, hh, :], pC[:, i_ * 128:(i_ + 1) * 128], beta[:, h0 + hh:h0 + hh + 1], ETs[:, hh, :],
                               ALU.mult, ALU.mult, [pC, gbt, ETs], [Lb])
                    for i_ in range(2):
                        tr.tt("dve", Aq[:, 2 * i_:2 * i_ + 2, :],
                              pC[:, (2 + i_) * 128:(3 + i_) * 128].unsqueeze(1).to_broadcast([128, 2, 128]),
                              E[:, 2 * i_:2 * i_ + 2, :], ALU.mult, [pC, E], [Aq])
                    yield
                    pT = tr.bank()
                    for hh in range(4):
                        tr.tp(pT[:, hh * 128:(hh + 1) * 128], Lb[:, hh, :], identf[:], [Lb, identf], [pT])
                    tr.cp("act", fl(Ub), pT[:], [pT], [Ub])
                    tr.tt("dve", N32[:], identf[:].unsqueeze(1).to_broadcast([128, 4, 128]), Ub[:], ALU.subtract,
                          [identf, Ub], [N32])
                    yield
                    Pc, Qc = Ub, Lb
                    for k in range(1, 7):
                        if k % 2:
                            Qn, Pn = t_["QA"], t_["PA"]
                        else:
                            Qn, Pn = Lb, Ub
                        pX = tr.bank()
                        for hh in range(4):
                            tr.mm(pX[:, hh * 128:(hh + 1) * 128], Pc[:, hh, :], Qc[:, hh, :], True, True, [Pc, Qc], [pX])
                        tr.cp("act", fl(Qn), pX[:], [pX], [Qn])
                        yield
                        pZ = tr.bank()
                        for hh in range(4):
                            tr.mm(pZ[:, hh * 128:(hh + 1) * 128], Qn[:, hh, :], N32[:, hh, :], True, True, [Qn, N32], [pZ])
                        if k < 6:
                            pY = tr.bank()
                            for hh in range(4):
                                tr.tp(pY[:, hh * 128:(hh + 1) * 128], Qn[:, hh, :], identf[:], [Qn, identf], [pY])
                            tr.cp("act", fl(Pn), pY[:], [pY], [Pn])
                        tr.tt("dve", fl(N32), fl(N32), pZ[:], ALU.add, [N32, pZ], [N32])
                        Pc, Qc = Pn, Qn
                        yield
                    tr.cp("act", Nb[:], N32[:], [N32], [Nb])
                    tr.tt("dve", bv[:], vtk[:, h0:h0 + 4, :], b4(beta[:, h0:h0 + 4]), ALU.mult, [vtk, gbt], [bv])
                    for i_ in range(2):
                        kx = ktk[:, kh0 + i_, :].unsqueeze(1).to_broadcast([128, 2, 128])
                        hs = slice(h0 + 2 * i_, h0 + 2 * i_ + 2)
                        tr.tt("pool", bke[:, 2 * i_:2 * i_ + 2, :], kx, bk[:, hs].unsqueeze(2).to_broadcast([128, 2, 128]),
                              ALU.mult, [ktk, bk], [bke])
                        tr.tt("pool", kd[:, 2 * i_:2 * i_ + 2, :], kx, dkf[:, hs].unsqueeze(2).to_broadcast([128, 2, 128]),
                              ALU.mult, [ktk, dkf], [kd])
                    yield
                    pU = tr.bank()
                    pW = tr.bank()
                    for hh in range(4):
                        cs = slice(hh * 128, (hh + 1) * 128)
                        tr.mm(pU[:, cs], Nb[:, hh, :], bv[:, hh, :], True, True, [Nb, bv], [pU])
                        tr.mm(pW[:, cs], bke[:, hh, :], Nb[:, hh, :], True, True, [Nb, bke], [pW])
                    tr.cp("act", fl(u32), pU[:], [pU], [u32])
                    tr.cp("dve", fl(wTb), pW[:], [pW], [wTb])
                    yield
                    pA2 = tr.bank()
                    for hh in range(4):
                        tr.mm(pA2[:, hh * 128:(hh + 1) * 128], wTb[:, hh, :], Sb[:, h0 + hh, :], True, True, [wTb, Sb], [pA2])
                    tr.stt("dve", fl(vnew), pA2[:], -1.0, fl(u32), ALU.mult, ALU.add, [pA2, u32], [vnew])
                    yield
                    pB1 = tr.bank()
                    pB2 = tr.bank()
                    pC2 = tr.bank()
                    for hh in range(4):
                        cs = slice(hh * 128, (hh + 1) * 128)
                        tr.mm(pB1[:, cs], qch[:, kh0 + hh // 2, :], Sb[:, h0 + hh, :], True, True, [qch, Sb], [pB1])
                        tr.mm(pB2[:, cs], Aq[:, hh, :], vnew[:, hh, :], True, True, [Aq, vnew], [pB2])
                        tr.mm(pC2[:, cs], kd[:, hh, :], vnew[:, hh, :], True, True, [kd, vnew], [pC2])
                    tr.tt("dve", o1[:], pB1[:].rearrange("p (a b) -> p a b", b=128), b4(egc[:, h0:h0 + 4]), ALU.mult,
                          [pB1, egc], [o1])
                    tr.tt("pool", stmp[:], S32[:, h0:h0 + 4, :], b4(egc[:, H + h0:H + h0 + 4]), ALU.mult, [S32, egc], [stmp])
                    tr.tt("dve", S32[:, h0:h0 + 4, :], stmp[:], pC2[:].rearrange("p (a b) -> p a b", b=128), ALU.add,
                          [stmp, pC2], [S32])
                    tr.cp("act", Sb[:, h0:h0 + 4, :], S32[:, h0:h0 + 4, :], [S32], [Sb])
                    if not bwd:
                        tr.tt("dve", ostg[:, h0:h0 + 4, :], o1[:], pB2[:].rearrange("p (a b) -> p a b", b=128), ALU.add,
                              [o1, pB2], [ostg])
                        finish(cx)
                        return
                    tr.tt("dve", ot[:], o1[:], pB2[:].rearrange("p (a b) -> p a b", b=128), ALU.add, [o1, pB2], [ot])
                    yield
                    tr.tt("pool", ot[:], ot[:], oft[:, h0:h0 + 4, :], ALU.add, [ot, oft], [ot])
                    tr.act(osq[:], ot[:], AF.Square, [ot], [osq])
                    tr.red("dve", ost[:, 0:4], osq[:], [osq], [ost])
                    tr.act(ost[:, 4:8], ost[:, 0:4], AF.Sqrt, [ost], [ost], bias=RMS_EPS, scale=1.0 / 128)
                    tr.rcp(ost[:, 4:8], ost[:, 4:8], [ost], [ost])
                    tr.tt("dve", ot[:], ot[:], b4(ost[:, 4:8]), ALU.mult, [ot, ost], [ot])
                    tr.tt("pool", ot[:], ot[:], gn[:].unsqueeze(1).to_broadcast([128, 4, 128]), ALU.mult, [ot, gn], [ot])
                    tr.tt("dve", ogb[:], ot[:], szt[:, h0:h0 + 4, :], ALU.mult, [ot, szt], [ogb])
                    yield
                    pT2 = tr.bank()
                    pT2v = pT2[:].bitcast(BF16)
                    for hh in range(4):
                        tr.tp(pT2v[:, hh * 128:(hh + 1) * 128], ogb[:, hh, :], ident[:], [ogb, ident], [pT2])
                    tr.cp("act", ogTt[:, h0:h0 + 4, :].rearrange("p a b -> p (a b)"), pT2v[:, 0:512], [pT2], [ogTt])
                    finish(cx)

                for gq in range(H // 4):
                    live.append(group(gq, G[setno[0] % NSETS]))
                    setno[0] += 1
                def step_all():
                    nonlocal live
                    nxt = []
                    for g_ in live:
                        try:
                            next(g_)
                            nxt.append(g_)
                        except StopIteration:
                            pass
                    live = nxt

                if ci == 0:
                    for _ in range(9):
                        step_all()
                while len(live) > (H // 4 if ci < nt - 1 else 0):
                    step_all()

    def build(self, depth=4):
        nc = self.nc
        with contextlib.ExitStack() as es:
            tr = self.tr = TR(nc, es)
            AQ = HL * 256
            KW = KHL * 128
            VW = VHL * 128
            for si, T in enumerate(self.seqs):
                self.resnorm(tr, T, self.xin[si], None, self.xs, self.norm_g[0:1, :], False, self.hT)
                for i in range(depth):
                    j = i // 2
                    if i % 2 == 0:
                        w = self.attn_w_in[j]
                        ogv = self.og_loc_blk(si, "a", 2 * HL)
                        self.proj(tr, T, self.src_hT(), 16, w[:, 0:AQ], AQ, "feat", self.ev_feat_bf16(self.qT, 128 ** -0.5))
                        self.proj(tr, T, self.src_hT(), 16, w[:, AQ:2 * AQ], AQ, "feat", self.ev_feat_bf16(self.kT, 1.0))
                        self.proj(tr, T, self.src_hT(), 16, w[:, 2 * AQ:3 * AQ], AQ, "tok", self.ev_v())
                        self.proj(tr, T, self.src_hT(), 16, w[:, 3 * AQ:4 * AQ], AQ, "tok", self.ev_tok_f32(self.sg, 0, 1, AF.Silu))
                        self.attn_core(tr, T, j, 0.8 - 0.6 * math.exp(-0.3 * i), ogv)
                        tr.allgather(list(zip(self.og_loc[si, "a"], self.og_all[si, "a"])))
                        self.proj(tr, T, self.src_gathered(si, "a", 2 * HL), 16, self.attn_w_out[j], 2048, "tok",
                                  self.ev_tok_f32(self.y, 0, 4, None))
                    else:
                        w = self.dn_w_in[j]
                        ogv = self.og_loc_blk(si, "g", VHL)
                        self.proj(tr, T, self.src_hT(), 16, w[:, 0:KW], KW, "feat", self.ev_pc(0))
                        self.proj(tr, T, self.src_hT(), 16, w[:, KW:2 * KW], KW, "feat", self.ev_pc(KHL))
                        self.proj(tr, T, self.src_hT(), 16, w[:, 2 * KW:2 * KW + VW], VW, "feat", self.ev_pc(2 * KHL))
                        self.proj(tr, T, self.src_hT(), 16, w[:, 2 * KW + VW:2 * KW + 2 * VW], VW, "tok",
                                  self.ev_tok_f32(self.sz, 0, VW // 512, AF.Silu))
                        self.proj(tr, T, self.src_hT(), 16, w[:, 2 * KW + 2 * VW:2 * KW + 2 * VW + 4 * VHL], 4 * VHL, "tok",
                                  self.ev_gates(j))
                        import os
                        skip = os.environ.get("DBG_SKIP", "")
                        if "conv" not in skip:
                            self.gdn_conv(tr, T, j)
                        if "scanf" not in skip:
                            self.gdn_scan(tr, T, j, 0, ogv)
                        if "scanb" not in skip:
                            self.gdn_scan(tr, T, j, 1, ogv)
                        tr.allgather(list(zip(self.og_loc[si, "g"], self.og_all[si, "g"])))
                        for c2 in range(2):
                            self.proj(tr, T, self.src_gathered(si, "g", VHL), 32, self.dn_w_out[j][:, c2 * 1024:(c2 + 1) * 1024],
                                      1024, "tok", self.ev_tok_f32(self.y, c2 * 1024, 2, None))
                    last = i == depth - 1
                    if last:
                        self.resnorm(tr, T, self.xs, self.y, None, self.final_norm_g[0:1, :], True, self.yout[si])
                    else:
                        self.resnorm(tr, T, self.xs, self.y, self.xs, self.norm_g[i + 1:i + 2, :], False, self.hT)
        return nc


def core_weights(w, core):
    f = lambda a: np.ascontiguousarray(a, np.float32)
    heads = core_heads(core)
    o = {}
    awi = w["attn_w_in"]
    cols = []
    for base in (0, 2048, 4096, 6144):
        for h in heads:
            cols.append(np.arange(base + h * 256, base + (h + 1) * 256))
    o["attn_w_in"] = f(awi[:, :, np.concatenate(cols)])
    perm = np.concatenate([np.arange(h * 256, (h + 1) * 256) for r in range(NCORE) for h in core_heads(r)])
    o["attn_w_out"] = f(w["attn_w_out"][:, perm, :])
    dwi = w["dn_w_in"]
    kh = np.arange(core * KHL * 128, (core + 1) * KHL * 128)
    vh = np.arange(core * VHL * 128, (core + 1) * VHL * 128)
    ab = np.concatenate([12288 + t * 32 + np.arange(core * VHL, (core + 1) * VHL) for t in range(4)])
    o["dn_w_in"] = f(dwi[:, :, np.concatenate([kh, 2048 + kh, 4096 + vh, 8192 + vh, ab])])
    o["dn_conv_w"] = f(w["dn_conv_w"][:, :, np.concatenate([kh, 2048 + kh, 4096 + vh])])
    for k in ("dn_a_log_fwd", "dn_dt_bias_fwd", "dn_a_log_bwd", "dn_dt_bias_bwd"):
        o[k] = f(w[k][:, core * VHL:(core + 1) * VHL])
    o["dn_w_out"] = f(w["dn_w_out"])
    o["norm_g"] = f(w["norm_g"])
    o["attn_lambda"] = f(w["attn_lambda"]).reshape(2, 512)
    o["attn_subln_g"] = f(w["attn_subln_g"])
    o["dn_norm_g"] = f(w["dn_norm_g"])
    o["final_norm_g"] = f(w["final_norm_g"]).reshape(1, D)
    o["cst"] = make_consts(core)
    return o


def make_in_maps(xs, w):
    maps = []
    for c in range(NCORE):
        m = core_weights(w, c)
        for i, x in enumerate(xs):
            m["x%d" % i] = np.ascontiguousarray(x, np.float32)
        maps.append(m)
    return maps


def kernel(x_prompt, x_sample, **w):
    prog = Prog(SEQS)
    nc = prog.build()
    in_maps = make_in_maps([x_prompt[0], x_sample[0], x_sample[1]], w)
    res = run_bass_kernel_spmd(nc, in_maps, core_ids=list(range(NCORE)))
    y_prompt = np.asarray(res.results[0]["y0"], np.float32)[None]
    y_sample = np.stack([np.asarray(res.results[0]["y1"], np.float32), np.asarray(res.results[0]["y2"], np.float32)], axis=0)
    return (y_prompt, y_sample)
```
